# Optimizing a Trainium2 kernel written in Bass

```python
import math
import jax
import jax.numpy as jnp
from jax import lax
import numpy as np

D_MODEL = 1024
BATCH = 16
SEQ = 256
DEPTH = 2
DEC_BATCH = 4
DEC_SEQ = 2048
PAST_LEN = 256

GRID_W = 64
A_HEADS = 4
A_DH = 64
A_W = A_HEADS * 2 * A_DH
B_HEADS = 8
B_KV = 2
B_DH = 64
B_W = B_HEADS * B_DH
C_GROUPS = 4
C_W = 512
C_GW = C_W // C_GROUPS
POOL_WINDOWS = (2, 4, 8, 16)
A_Q = 2 * A_HEADS * A_DH
A_K = 2 * A_HEADS * A_DH
A_V = A_HEADS * 2 * A_DH
B_Q = B_HEADS * B_DH
B_K = B_KV * B_DH
B_V = B_KV * B_DH
IN_W = A_Q + A_K + A_V + B_Q + B_K + B_V + C_W
D_FF = 2816
Q_BLOCK = 128
ROPE_THETA = 10000.0
ROPE_F = A_DH // 4
EPS = 1e-6

kernel_name = 'hybrid_diffusion_prefix_ctx_step'


def _rmsnorm(x, g):
    xf = x.astype(jnp.float32)
    y = xf * lax.rsqrt(jnp.mean(xf * xf, axis=-1, keepdims=True) + EPS)
    return y.astype(x.dtype) * g


def _rope_tables(rows, dtype):
    row = jnp.repeat(jnp.arange(rows, dtype=jnp.float32), GRID_W)
    col = jnp.tile(jnp.arange(GRID_W, dtype=jnp.float32), rows)
    inv = ROPE_THETA ** (-jnp.arange(ROPE_F, dtype=jnp.float32) / ROPE_F)
    ang = jnp.stack([row[:, None] * inv, col[:, None] * inv], axis=1)
    return jnp.cos(ang).astype(dtype), jnp.sin(ang).astype(dtype)


def _apply_rope(x, cos, sin):
    b, s, d = x.shape[0], x.shape[1], x.shape[-1]
    xs = x.reshape(b, s, -1, 2, 2, d // 4)
    a, bb = xs[..., 0, :], xs[..., 1, :]
    cs, sn = cos[:, None], sin[:, None]
    out = jnp.stack([a * cs - bb * sn, bb * cs + a * sn], axis=-2)
    return out.reshape(x.shape)


def _to_blocks(q):
    b, s = q.shape[0], q.shape[1]
    return jnp.moveaxis(q.reshape(b, s // Q_BLOCK, Q_BLOCK, *q.shape[2:]), 1, 0)


def _from_blocks(o):
    o = jnp.moveaxis(o, 0, 1)
    return o.reshape(o.shape[0], o.shape[1] * o.shape[2], *o.shape[3:])


def _diff_attention(q, k, v, lam):
    scale = A_DH ** -0.5

    def block(qb):
        s = jnp.einsum('bqahd,bkahd->bahqk', qb, k).astype(jnp.float32) * scale
        p = jax.nn.softmax(s, axis=-1)
        w = p[:, 0] - lam * p[:, 1]
        return jnp.einsum('bhqk,bkhe->bqhe', w.astype(v.dtype), v)

    return _from_blocks(lax.map(block, _to_blocks(q)))


def _gqa_attention(q, k, v):
    b, s = q.shape[0], q.shape[1]
    qg = q.reshape(b, s, B_KV, B_HEADS // B_KV, B_DH)
    scale = B_DH ** -0.5

    def block(qb):
        sc = jnp.einsum('bqgrd,bkgd->bgrqk', qb, k).astype(jnp.float32) * scale
        p = jax.nn.softmax(sc, axis=-1).astype(v.dtype)
        return jnp.einsum('bgrqk,bkgd->bqgrd', p, v)

    return _from_blocks(lax.map(block, _to_blocks(qg))).reshape(b, s, B_W)


def _pool_mixer(u, w_pool, scale):
    b, s, _ = u.shape
    ug = u.reshape(b, s, C_GROUPS, C_GW).astype(jnp.float32)
    cs = jnp.pad(lax.cumsum(ug, axis=1), ((0, 0), (1, 0), (0, 0), (0, 0)))
    t = np.arange(s)
    outs = []
    for g, w in enumerate(POOL_WINDOWS):
        lo = np.clip(t - w // 2, 0, s)
        hi = np.clip(t - w // 2 + w, 0, s)
        cnt = (hi - lo).astype(np.float32)
        mean = (cs[:, hi, g] - cs[:, lo, g]) / cnt[None, :, None]
        outs.append(mean - ug[:, :, g])
    d = jnp.stack(outs, axis=2).astype(u.dtype)
    y = jnp.einsum('bsgc,gcd->bsgd', d, w_pool).reshape(b, s, C_W)
    return y * scale


def _dwconv3(h, w, bias):
    hp = jnp.pad(h, ((0, 0), (1, 1), (0, 0)))
    return hp[:, :-2] * w[0] + hp[:, 1:-1] * w[1] + hp[:, 2:] * w[2] + bias


def _layer(x, mod, p, lam_init, ctx_kv, rope):
    b, s, _ = x.shape
    sh1, sc1, g1, sh2, sc2, g2 = jnp.split(mod, 6, axis=-1)
    h = _rmsnorm(x, p['norm1']) * (1 + sc1) + sh1
    splits = [int(v) for v in np.cumsum([A_Q, A_K, A_V, B_Q, B_K, B_V])]
    qa, ka, va, qb, kb, vb, uc = jnp.split(h @ p['w_in'], splits, axis=-1)
    qa = _rmsnorm(qa.reshape(b, s, 2, A_HEADS, A_DH), p['qn_a'])
    ka = _rmsnorm(ka.reshape(b, s, 2, A_HEADS, A_DH), p['kn_a'])
    va = va.reshape(b, s, A_HEADS, 2 * A_DH)
    qb = _rmsnorm(qb.reshape(b, s, B_HEADS, B_DH), p['qn_b'])
    kb = _rmsnorm(kb.reshape(b, s, B_KV, B_DH), p['kn_b'])
    vb = vb.reshape(b, s, B_KV, B_DH)
    own_kv = (ka, va, kb, vb)
    if rope is not None:
        cos, sin = rope
        qa = _apply_rope(qa, cos, sin)
        ka = _apply_rope(ka, cos, sin)
        qb = _apply_rope(qb, cos, sin)
        kb = _apply_rope(kb, cos, sin)
    if ctx_kv is not None:
        cka, cva, ckb, cvb = ctx_kv
        ka = jnp.concatenate([cka.astype(ka.dtype), ka], axis=1)
        va = jnp.concatenate([cva.astype(va.dtype), va], axis=1)
        kb = jnp.concatenate([ckb.astype(kb.dtype), kb], axis=1)
        vb = jnp.concatenate([cvb.astype(vb.dtype), vb], axis=1)
    lam = (jnp.exp(jnp.sum(p['lam_q1'].astype(jnp.float32) * p['lam_k1'].astype(jnp.float32)))
           - jnp.exp(jnp.sum(p['lam_q2'].astype(jnp.float32) * p['lam_k2'].astype(jnp.float32)))
           + lam_init)
    oa = _diff_attention(qa, ka, va, lam)
    oa = (_rmsnorm(oa, p['subln']) * (1.0 - lam_init)).reshape(b, s, A_W)
    ob = _gqa_attention(qb, kb, vb)
    oc = _pool_mixer(uc, p['w_pool'], p['pool_scale'])
    ga, gb, gc = jnp.split(jax.nn.sigmoid(h @ p['w_gate'] + p['b_gate']), 3, axis=-1)
    merged = ga * (oa @ p['w_br_a']) + gb * (ob @ p['w_br_b']) + gc * (oc @ p['w_br_c'])
    x = x + g1 * (merged @ p['w_out'])
    h2 = _rmsnorm(x, p['norm2']) * (1 + sc2) + sh2
    a, g = jnp.split(_dwconv3(h2 @ p['w_up'], p['conv_w'], p['conv_b']), 2, axis=-1)
    x = x + g2 * ((jax.nn.silu(a) * g) @ p['w_down'])
    return x, own_kv


def setup_inputs(seed: int = 0) -> dict:
    key = jax.random.key(seed)
    ks = iter(jax.random.split(key, 48))
    D, L = D_MODEL, DEPTH

    def nrm(shape, scale):
        return scale * jax.random.normal(next(ks), shape, jnp.float32)

    return {
        'x_prompt': nrm((BATCH, SEQ, D), 1.0),
        'x_sample': nrm((DEC_BATCH, DEC_SEQ, D), 1.0),
        'cache_diff_k': nrm((DEC_BATCH, L, PAST_LEN, 2, A_HEADS, A_DH), 1.0),
        'cache_diff_v': nrm((DEC_BATCH, L, PAST_LEN, A_HEADS, 2 * A_DH), 1.0),
        'cache_gqa_k': nrm((DEC_BATCH, L, PAST_LEN, B_KV, B_DH), 1.0),
        'cache_gqa_v': nrm((DEC_BATCH, L, PAST_LEN, B_KV, B_DH), 1.0),
        'c': nrm((DEC_BATCH, D), 1.0),
        'c_ctx': nrm((D,), 1.0),
        'norm1_g': 1.0 + nrm((L, D), 0.1),
        'norm2_g': 1.0 + nrm((L, D), 0.1),
        'w_mod': nrm((L, D, 6 * D), 0.5 * D ** -0.5),
        'b_mod': nrm((L, 6 * D), 0.02),
        'w_in': nrm((L, D, IN_W), D ** -0.5),
        'qn_a': 1.0 + nrm((L, A_DH), 0.1),
        'kn_a': 1.0 + nrm((L, A_DH), 0.1),
        'qn_b': 1.0 + nrm((L, B_DH), 0.1),
        'kn_b': 1.0 + nrm((L, B_DH), 0.1),
        'lam_q1': nrm((L, A_DH), 0.1),
        'lam_k1': nrm((L, A_DH), 0.1),
        'lam_q2': nrm((L, A_DH), 0.1),
        'lam_k2': nrm((L, A_DH), 0.1),
        'subln_g': 1.0 + nrm((L, 2 * A_DH), 0.1),
        'w_pool': nrm((L, C_GROUPS, C_GW, C_GW), C_GW ** -0.5),
        'pool_scale': 1.0 + nrm((L, C_W), 0.1),
        'w_br_a': nrm((L, A_W, D), A_W ** -0.5),
        'w_br_b': nrm((L, B_W, D), B_W ** -0.5),
        'w_br_c': nrm((L, C_W, D), C_W ** -0.5),
        'w_gate': nrm((L, D, 3 * D), D ** -0.5),
        'b_gate': nrm((L, 3 * D), 0.02),
        'w_out': nrm((L, D, D), D ** -0.5),
        'w_up': nrm((L, D, 2 * D_FF), D ** -0.5),
        'conv_w': nrm((L, 3, 2 * D_FF), 3 ** -0.5),
        'conv_b': nrm((L, 2 * D_FF), 0.02),
        'w_down': nrm((L, D_FF, D), D_FF ** -0.5),
    }


def reference(x_prompt, x_sample, cache_diff_k, cache_diff_v, cache_gqa_k, cache_gqa_v, c, c_ctx,
              norm1_g, norm2_g, w_mod, b_mod, w_in, qn_a, kn_a, qn_b, kn_b,
              lam_q1, lam_k1, lam_q2, lam_k2, subln_g, w_pool, pool_scale,
              w_br_a, w_br_b, w_br_c, w_gate, b_gate, w_out, w_up, conv_w, conv_b, w_down):
    rows = x_sample.shape[1] // GRID_W
    rope = _rope_tables(rows, x_sample.dtype)
    y_p, y_s = x_prompt, x_sample
    st_dk, st_dv, st_gk, st_gv = [], [], [], []
    for l in range(DEPTH):
        p = {
            'norm1': norm1_g[l], 'norm2': norm2_g[l], 'w_in': w_in[l],
            'qn_a': qn_a[l], 'kn_a': kn_a[l], 'qn_b': qn_b[l], 'kn_b': kn_b[l],
            'lam_q1': lam_q1[l], 'lam_k1': lam_k1[l], 'lam_q2': lam_q2[l], 'lam_k2': lam_k2[l],
            'subln': subln_g[l], 'w_pool': w_pool[l], 'pool_scale': pool_scale[l],
            'w_br_a': w_br_a[l], 'w_br_b': w_br_b[l], 'w_br_c': w_br_c[l],
            'w_gate': w_gate[l], 'b_gate': b_gate[l], 'w_out': w_out[l],
            'w_up': w_up[l], 'conv_w': conv_w[l], 'conv_b': conv_b[l], 'w_down': w_down[l],
        }
        lam_init = 0.8 - 0.6 * math.exp(-0.3 * l)
        mod_ctx = jax.nn.silu(c_ctx) @ w_mod[l] + b_mod[l]
        mod_lat = (jax.nn.silu(c) @ w_mod[l] + b_mod[l])[:, None, :]
        y_p, kv = _layer(y_p, mod_ctx, p, lam_init, None, None)
        st_dk.append(kv[0])
        st_dv.append(kv[1])
        st_gk.append(kv[2])
        st_gv.append(kv[3])
        ctx = (cache_diff_k[:, l], cache_diff_v[:, l], cache_gqa_k[:, l], cache_gqa_v[:, l])
        y_s, _ = _layer(y_s, mod_lat, p, lam_init, ctx, rope)
    new_diff_k = jnp.stack(st_dk, axis=1)
    new_diff_v = jnp.stack(st_dv, axis=1)
    new_gqa_k = jnp.stack(st_gk, axis=1)
    new_gqa_v = jnp.stack(st_gv, axis=1)
    return (y_p, y_s, new_diff_k, new_diff_v, new_gqa_k, new_gqa_v)
```

```python
import math
from contextlib import ExitStack
import numpy as np
import concourse.bass as bass
import concourse.mybir as mybir
from concourse.bass_utils import run_bass_kernel_spmd

F32 = mybir.dt.float32
BF16 = mybir.dt.bfloat16
AF = mybir.ActivationFunctionType
ALU = mybir.AluOpType

D = 1024
L = 2
PAST = 256
GRID_W = 64
EPS = 1e-6
DFF = 2816
TS = 1024
TP = 512
SEM_EPOCH = 15000
DEBUG = False
DBG_OUT = {}
ENGS = ("pe", "act", "dve", "pool", "sp")

BLK = {}
_n = 0
for _name, _cnt in (("Q", 8), ("K", 6), ("UC", 4), ("V", 5), ("GATE", 24), ("BRA", 8), ("BRB", 8), ("BRC", 8),
                    ("OUT", 8), ("UP", 44), ("DOWN", 24), ("POOLW", 4), ("MOD", 48)):
    BLK[_name] = _n
    _n += _cnt
NBLK = _n

VC = {}
_n = 0
for _name, _cnt in (("norm1", 8), ("norm2", 8), ("bmod", 48), ("bgate", 24), ("convw", 132), ("convb", 44),
                    ("pscale", 4), ("qna", 1), ("kna", 1), ("qnb", 1), ("knb", 1), ("subln", 1), ("lam", 4)):
    VC[_name] = _n
    _n += _cnt
NVEC = _n


class Reg:
    __slots__ = ("name", "w", "r", "excl")

    def __init__(self, name="", excl=False):
        self.name = name
        self.excl = excl
        self.w = None
        self.r = {}

    def inherit(self, other):
        if other.w is not None:
            s_, i_ = other.w
            if self.r.get(s_, -1) < i_:
                self.r[s_] = i_
        for s_, i_ in other.r.items():
            if self.r.get(s_, -1) < i_:
                self.r[s_] = i_


class _Rec:
    def __init__(self):
        self.calls = []

    def __getattr__(self, name):
        def f(*a, **kw):
            self.calls.append((name, a, kw))
            return None
        return f


def _replay(calls):
    def fn(eng):
        ins = None
        for (name, a, kw) in calls:
            ins = getattr(eng, name)(*a, **kw)
        return ins
    return fn


class FW:
    def __init__(self, nc):
        self.nc = nc
        self.ops = {e: [] for e in ENGS}
        self.src_ops = {e: [] for e in ENGS}
        self.rr = {}

    def _deps(self, reads, writes):
        deps = set()
        for r in reads:
            if r.w is not None:
                deps.add(r.w)
            if r.excl:
                for x in r.r.items():
                    deps.add(x)
        for w in writes:
            if w.w is not None:
                deps.add(w.w)
            for x in w.r.items():
                deps.add(x)
        return deps

    def _commit(self, me, reads, writes):
        for r in reads:
            if r.r.get(me[0], -1) < me[1]:
                r.r[me[0]] = me[1]
        for w in writes:
            w.w = me
            w.r = {}

    def op(self, eng, fn, reads=(), writes=()):
        deps = self._deps(reads, writes)
        if eng == "pe":
            deps = {d for d in deps if d[0] != "pe"}
        idx = len(self.src_ops[eng])
        rc_ = _Rec()
        fn(rc_)
        fn = _replay(rc_.calls)
        rec = dict(eng=eng, fn=fn, deps=deps, src=eng, idx=idx, signal=False, dma=False)
        self.src_ops[eng].append(rec)
        self.ops[eng].append(rec)
        self._commit((eng, idx), reads, writes)

    NSUB = 8

    def dma(self, q, out, in_, reads=(), writes=(), stream="d0", sub=None):
        deps = self._deps(reads, writes)
        if sub is None:
            sub = self.rr.get(stream, 0) % self.NSUB
            self.rr[stream] = self.rr.get(stream, 0) + 1
        stream = f"{stream}_{sub}"
        if stream not in self.src_ops:
            self.src_ops[stream] = []
        if self.src_ops[stream]:
            deps.add((stream, len(self.src_ops[stream]) - 1))
        idx = len(self.src_ops[stream])
        rec = dict(eng=q, fn=lambda e: e.dma_start(out=out, in_=in_), deps=deps,
                   src=stream, idx=idx, signal=True, dma=True)
        self.src_ops[stream].append(rec)
        self.ops[q].append(rec)
        self._commit((stream, idx), reads, writes)

    def final_wait(self, eng, regs):
        deps = self._deps(regs, ())
        self.ops[eng].append(dict(eng=eng, fn=None, deps=deps, src=None, idx=None, signal=False, dma=False))

    def emit(self, es):
        nc = self.nc
        for e in ENGS:
            for rec in self.ops[e]:
                for (s, i) in rec["deps"]:
                    self.src_ops[s][i]["signal"] = True
        sems = {}
        for s, lst in self.src_ops.items():
            step = 16 if (lst and lst[0]["dma"]) else 1
            cnt = 0
            epoch = 0
            for rec in lst:
                if rec["signal"]:
                    if cnt + step > SEM_EPOCH:
                        epoch += 1
                        cnt = 0
                    cnt += step
                    rec["sig"] = (epoch, cnt)
                    if (s, epoch) not in sems:
                        sems[(s, epoch)] = es.enter_context(nc.semaphore(f"s_{s}_{epoch}"))
        block = es.enter_context(nc.Block())
        src_ops = self.src_ops

        def run(e):
            def body(eng):
                waited = {}
                for rec in self.ops[e]:
                    need = {}
                    for (s, i) in rec["deps"]:
                        ep, v = src_ops[s][i]["sig"]
                        k = (s, ep)
                        if waited.get(k, 0) >= v:
                            continue
                        if need.get(k, 0) < v:
                            need[k] = v
                    for k, v in need.items():
                        eng.wait_ge(sems[k], v)
                        waited[k] = v
                    if rec["fn"] is None:
                        continue
                    ins = rec["fn"](eng)
                    if rec["signal"]:
                        ins.then_inc(sems[(rec["src"], rec["sig"][0])], 16 if rec["dma"] else 1)
            return body

        block.tensor(run("pe"))
        block.scalar(run("act"))
        block.vector(run("dve"))
        block.gpsimd(run("pool"))
        block.sync(run("sp"))


def build_program(wseq=None, wseq_out=None):
    nc = bass.Bass("TRN2", target_bir_lowering=False)
    dt = lambda name, shape, kind: nc.dram_tensor(name, shape, F32, kind=kind).ap()
    XS = dt("xs", [8, 128, TS], "ExternalInput")
    XP = dt("xp", [8, 128, TP], "ExternalInput")
    W = dt("w", [L, NBLK, 128, 1024], "ExternalInput")
    VEC = dt("vec", [L, 128, NVEC], "ExternalInput")
    CT = dt("ct", [128, 16], "ExternalInput")
    CST = dt("cst", [128, 384], "ExternalInput")
    ROPE = dt("rope", [2, 128, TS], "ExternalInput")
    PCNT = dt("pcnt", [128, 128], "ExternalInput")
    MSK = dt("msk", [128, 16], "ExternalInput")
    xinK = [nc.dram_tensor(f"xinK{l}", [768, 1024], BF16) for l in range(L)]
    xoutK = [nc.dram_tensor(f"xoutK{l}", [1536, 1024], BF16) for l in range(L)]
    xinV = [nc.dram_tensor(f"xinV{l}", [1024, 672], BF16) for l in range(L)]
    xoutV = [nc.dram_tensor(f"xoutV{l}", [2048, 672], BF16) for l in range(L)]
    xin2 = [nc.dram_tensor(f"xin2_{l}", [128, 32], BF16) for l in range(L)]
    xout2 = [nc.dram_tensor(f"xout2_{l}", [256, 32], BF16) for l in range(L)]
    PAIRS = [[0, 1], [2, 3], [4, 5], [6, 7]]
    CK = dt("ck", [L, 6, 128, PAST], "ExternalInput")
    CV = dt("cv", [L, 2, 128, 640], "ExternalInput")
    YS = dt("ys", [8, 128, TS], "ExternalOutput")
    YP = dt("yp", [8, 128, TP], "ExternalOutput")
    KO = dt("ko", [L, 6, 128, TP], "ExternalOutput")
    VO = dt("vo", [L, TP, 640], "ExternalOutput")
    DBG = dt("dbg", [10, 8, 128, 512], "ExternalOutput") if DEBUG else None

    es = ExitStack()
    with es:
        fw = FW(nc)
        sbt = lambda n, s, d: es.enter_context(nc.sbuf_tensor(n, s, d))
        x = sbt("x", [128, 8, TS], F32)
        x_r = [[Reg() for _ in range(4)] for _ in range(8)]
        AR_KV, AR_UC, AR_BLK = 0, 13824, 13824 + 2112
        arena = sbt("arena", [128, 13824 + 2112 + 7168], F32)

        def carve(off_w, nbytes, dtype):
            a = arena[:, off_w:off_w + nbytes // 4]
            return a.bitcast(dtype) if dtype != F32 else a
        kt = carve(AR_KV, 6 * 2304 * 2, BF16).rearrange("p (c t) -> p c t", c=6)
        vt = carve(AR_KV + 6912, 18 * 768 * 2, BF16).rearrange("p (c t) -> p c t", c=18)
        ucb = carve(AR_UC, 4 * 1056 * 2, BF16).rearrange("p (c t) -> p c t", c=4)
        hT = carve(AR_BLK, 8192, BF16).rearrange("p (c t) -> p c t", c=8)
        qm = carve(AR_BLK + 2048, 8192, BF16).rearrange("p (c t) -> p c t", c=8)
        ao = carve(AR_BLK + 4096, 8192, BF16).rearrange("p (c t) -> p c t", c=8)
        ocb = carve(AR_BLK + 6144, 4096, BF16).rearrange("p (c t) -> p c t", c=4)
        hid = carve(AR_KV, 8 * 2048 * 2, BF16).rearrange("p (c t) -> p c t", c=8)
        stg = carve(AR_KV + 8192, 2 * 2056 * 4, F32).rearrange("p (c t) -> p c t", c=2)
        h2T = carve(AR_UC, 8 * 2048 * 2, BF16).rearrange("p (c t) -> p c t", c=8)
        NSCR = 8
        scr = [sbt(f"scr{i}", [128, 544], F32) for i in range(NSCR)]
        scr_r = [Reg() for _ in range(NSCR)]
        NRING = 12
        ring = [sbt(f"ring{i}", [128, 8, 128], BF16) for i in range(NRING)]
        ring_r = [Reg() for _ in range(NRING)]
        rope = sbt("rope_sb", [128, 2, TS], BF16)
        cst = sbt("cst_sb", [128, 384], BF16)
        ones_bf, onesblk_bf, perm_bf = cst[:, 0:128], cst[:, 128:256], cst[:, 256:384]
        ones_f = sbt("ones_f", [128, 128], F32)
        vec = sbt("vec_sb", [128, L, NVEC], F32)
        ct = sbt("ct_sb", [128, 16], F32)
        sc_bf = sbt("sc_bf", [128, 8, 2], BF16)
        modv = sbt("modv", [128, L, 48, 2], F32)
        dv = sbt("dv", [128, L, 2, 6, 8], F32)
        lamv = sbt("lamv", [128, L, 8], F32)
        pcnt = sbt("pcnt_sb", [128, 128], F32)
        msk = sbt("msk_sb", [128, 16], F32)
        r_msk = Reg()
        halo_u = sbt("halo_u", [128, 4, 2, 32], BF16)
        r_halo_u = Reg()
        uh = sbt("uh", [128, 4, 32], BF16)
        r_uh = Reg()
        h2s = sbt("h2s", [128, 32], BF16)
        r_h2s = Reg()
        h2g = sbt("h2g", [128, 2, 32], BF16)
        r_h2g = Reg()
        h2h = sbt("h2h", [128, 8, 2], BF16)
        r_h2h = Reg()
        stg_h = [Reg(), Reg()]
        pacc = [[sbt(f"pacc{i}{j}", [128, 512], F32) for j in range(2)] for i in range(2)]
        pacc_r = [[Reg() for j in range(2)] for i in range(2)]
        qo = sbt("qo", [128, 8, 512], BF16)
        qo_r = [Reg() for _ in range(8)]
        hid2 = sbt("hid2", [128, 8, 1024], BF16)
        hid2_r = [[Reg() for _ in range(4)] for _ in range(8)]
        dbuf = sbt("dbuf", [128, 512], BF16)
        r_dbuf = Reg()
        epsc = sbt("epsc", [128, 1], F32)

        R = lambda: Reg()
        r_rope, r_cst, r_onesf, r_vec, r_ct, r_sc, r_modv, r_dv, r_lam, r_pcnt, r_eps = (R() for _ in range(11))
        kt_r = [[Reg() for _ in range(5)] for _ in range(6)]
        vt_r = [Reg() for _ in range(18)]
        uc_r = [[Reg() for _ in range(4)] for _ in range(4)]
        hT_r, qm_r, ao_r, oc_r = ([Reg() for _ in range(8)] for _ in range(4))
        hid_r = [[Reg() for _ in range(4)] for _ in range(8)]
        h2_r = [[Reg() for _ in range(4)] for _ in range(8)]
        stg_r = [[Reg() for _ in range(4)] for _ in range(2)]
        ps = [es.enter_context(nc.psum_tensor(f"ps{i}", [128, 512], F32)) for i in range(8)]
        ps_r = [Reg(excl=True) for _ in range(8)]
        out_regs = []
        xin_regs = []
        xinK_regs = []

        state = dict(scr=0, ring=0, ps=0, issued=0)

        def tap(i, src, regs, nchunk=8, is_bf=True):
            if DBG is None or wseq_out is not None:
                return
            for c in range(nchunk):
                o = Reg()
                out_regs.append(o)
                fw.dma("pool" if is_bf else "sp", DBG[i, c, :, :], src[:, c, 0:512], reads=regs, writes=[o], stream="dbg")

        def new_scr():
            i = state["scr"] % NSCR
            state["scr"] += 1
            return scr[i], scr_r[i]

        def new_ps(pool=(0, 1, 2, 3, 4, 5, 6, 7)):
            i = pool[state["ps"] % len(pool)]
            state["ps"] += 1
            return ps[i], ps_r[i]

        PF = NRING - 3

        def _issue_w(n):
            l_, b_, kc_ = wseq[n]
            i = n % NRING
            src = W[l_, b_, :, 0:kc_ * 128].rearrange("p (k n) -> p k n", k=kc_)
            fw.dma("pool", ring[i][:, 0:kc_, :], src, writes=[ring_r[i]], stream="dw", sub=i)

        def wblk(l, b, kc=8):
            n = state["ring"]
            state["ring"] += 1
            if wseq_out is not None:
                wseq_out.append((l, b, kc))
                i = n % NRING
                return ring[i], ring_r[i]
            while state["issued"] < min(len(wseq), n + PF + 1):
                _issue_w(state["issued"])
                state["issued"] += 1
            i = n % NRING
            return ring[i], ring_r[i]

        fw.dma("pool", cst[:], CST[:, :], writes=[r_cst], stream="dc")
        fw.dma("pool", rope[:, 0, :], ROPE[0, :, :], writes=[r_rope], stream="dc")
        fw.dma("pool", rope[:, 1, :], ROPE[1, :, :], writes=[r_rope], stream="dc")
        for l in range(L):
            fw.dma("sp", vec[:, l, :], VEC[l, :, :], writes=[r_vec], stream="d0")
        fw.dma("sp", ct[:], CT[:, :], writes=[r_ct], stream="d0")
        fw.dma("sp", pcnt[:], PCNT[:, :], writes=[r_pcnt], stream="d0")
        fw.dma("sp", msk[:], MSK[:, :], writes=[r_msk], stream="d0")
        fw.op("dve", lambda e: e.memset(ones_f[:], 1.0), writes=[r_onesf])
        fw.op("dve", lambda e: e.memset(epsc[:], EPS), writes=[r_eps])
        fw.op("pool", lambda e: e.memset(qo[:], 0.0), writes=qo_r)
        fw.op("dve", lambda e: e.memset(uh[:], 0.0), writes=[r_uh])
        fw.op("dve", lambda e: e.memset(h2s[:], 0.0), writes=[r_h2s])
        fw.op("act", lambda e: e.activation(sc_bf[:, :, 0], ct[:, 0:8], AF.Silu), reads=[r_ct], writes=[r_sc])
        fw.op("act", lambda e: e.activation(sc_bf[:, :, 1], ct[:, 8:16], AF.Silu), reads=[r_ct], writes=[r_sc])

        def vcol(l, name, i=0):
            c = VC[name] + i
            return vec[:, l, c:c + 1]

        def mod_block(l, j, pool=None):
            wt, wr = wblk(l, BLK["MOD"] + j)
            pt, pr = new_ps(pool) if pool is not None else new_ps()

            def mm(e):
                for k in range(8):
                    ins = e.matmul(pt[:, 0:2], lhsT=wt[:, k, :], rhs=sc_bf[:, k, :], start=(k == 0), stop=(k == 7))
                return ins
            fw.op("pe", mm, reads=[wr, r_sc], writes=[pr])
            fw.op("dve", lambda e: e.tensor_scalar(modv[:, l, j, :], pt[:, 0:2], vcol(l, "bmod", j), None, ALU.add),
                  reads=[pr, r_vec], writes=[r_modv])

        def mod_der(l):
            for p in range(2):
                def der(e):
                    n1 = vec[:, l, VC["norm1"]:VC["norm1"] + 8]
                    n2 = vec[:, l, VC["norm2"]:VC["norm2"] + 8]
                    e.scalar_tensor_tensor(dv[:, l, p, 0, :], modv[:, l, 8:16, p], 1.0, n1, ALU.add, ALU.mult)
                    e.tensor_copy(dv[:, l, p, 1, :], modv[:, l, 0:8, p])
                    e.tensor_copy(dv[:, l, p, 2, :], modv[:, l, 16:24, p])
                    e.scalar_tensor_tensor(dv[:, l, p, 3, :], modv[:, l, 32:40, p], 1.0, n2, ALU.add, ALU.mult)
                    e.tensor_copy(dv[:, l, p, 4, :], modv[:, l, 24:32, p])
                    return e.tensor_copy(dv[:, l, p, 5, :], modv[:, l, 40:48, p])
                fw.op("dve", der, reads=[r_modv, r_vec], writes=[r_dv])

        for j in range(48):
            mod_block(0, j)
        mod_der(0)
        pending_mod = list(range(48))

        def mod_step(pool):
            if pending_mod:
                mod_block(1, pending_mod.pop(0), pool)

        def mod_flush():
            if pending_mod or not state.get("der1"):
                while pending_mod:
                    mod_block(1, pending_mod.pop(0))
                mod_der(1)
                state["der1"] = True

        for l in range(L):
            lam_init = 0.8 - 0.6 * math.exp(-0.3 * l)
            c0 = VC["lam"]
            fw.op("dve", lambda e, l=l, c0=c0: e.tensor_tensor(lamv[:, l, 2:3], vec[:, l, c0:c0 + 1], vec[:, l, c0 + 1:c0 + 2], ALU.mult),
                  reads=[r_vec], writes=[r_lam])
            fw.op("dve", lambda e, l=l, c0=c0: e.tensor_tensor(lamv[:, l, 3:4], vec[:, l, c0 + 2:c0 + 3], vec[:, l, c0 + 3:c0 + 4], ALU.mult),
                  reads=[r_vec, r_lam], writes=[r_lam])
            pt, pr = new_ps()
            fw.op("pe", lambda e, pt=pt, l=l: e.matmul(pt[:, 0:2], lhsT=ones_f[:], rhs=lamv[:, l, 2:4], start=True, stop=True),
                  reads=[r_lam, r_onesf], writes=[pr])
            fw.op("act", lambda e, pt=pt, l=l: e.activation(lamv[:, l, 4:6], pt[:, 0:2], AF.Exp), reads=[pr, r_lam], writes=[r_lam])
            fw.op("dve", lambda e, l=l, li=lam_init: e.scalar_tensor_tensor(lamv[:, l, 0:1], lamv[:, l, 5:6], -li, lamv[:, l, 4:5], ALU.add, ALU.subtract),
                  reads=[r_lam], writes=[r_lam])
            fw.op("dve", lambda e, l=l, li=lam_init: e.tensor_scalar(lamv[:, l, 1:2], vcol(l, "subln"), 1.0 - li, None, ALU.mult),
                  reads=[r_vec, r_lam], writes=[r_lam])

        def rms_norm_block(l, p, tb, which, dst, dst_r, xcol0):
            cs = slice(xcol0, xcol0 + 512)
            pt, pr = new_ps()
            sqs = []
            for c in range(8):
                st, sr = new_scr()
                sq = st[:, 0:256].bitcast(BF16)
                fw.op("act", lambda e, sq=sq, c=c: e.activation(sq, x[:, c, cs], AF.Square), reads=[x_r[c][tb]], writes=[sr])
                fw.op("pe", lambda e, sq=sq, c=c, pt=pt: e.matmul(pt[:], lhsT=ones_bf, rhs=sq, start=(c == 0), stop=(c == 7)),
                      reads=[sr, r_cst], writes=[pr])
            rt, rr = new_scr()
            fw.op("act", lambda e, rt=rt, pt=pt: e.activation(rt[:, 0:512], pt[:], AF.Ln, bias=epsc[:, 0:1], scale=1.0 / D),
                  reads=[pr, r_eps], writes=[rr])
            fw.op("act", lambda e, rt=rt: e.activation(rt[:, 0:512], rt[:, 0:512], AF.Exp, scale=-0.5), reads=[rr], writes=[rr])
            for c in range(8):
                tt, tr = new_scr()
                sc = dv[:, l, p, 3 * which + 0, c:c + 1]
                sh = dv[:, l, p, 3 * which + 1, c:c + 1]
                fw.op("dve", lambda e, tt=tt, c=c, sc=sc, rt=rt: e.scalar_tensor_tensor(tt[:, 0:512], x[:, c, cs], sc, rt[:, 0:512], ALU.mult, ALU.mult),
                      reads=[x_r[c][tb], rr, r_dv], writes=[tr])
                fw.op("act", lambda e, tt=tt, c=c, sh=sh: e.activation(dst(c), tt[:, 0:512], AF.Identity, bias=sh),
                      reads=[tr, r_dv], writes=[dst_r(c)])

        def proj_fm(wt, wr, rhs_fn, rhs_regs, kc=8, n=512):
            pt, pr = new_ps()

            def mm(e):
                for k in range(kc):
                    ins = e.matmul(pt[:, 0:n], lhsT=wt[:, k, :], rhs=rhs_fn(k), start=(k == 0), stop=(k == kc - 1))
                return ins
            fw.op("pe", mm, reads=[wr] + list(rhs_regs), writes=[pr])
            return pt, pr

        def qk_post(l, pt, pr, gname, dst_ap, dst_regs, rope_cols=None, kout=None, dst_hi=None, dst_hi_regs=None):
            st, sr = new_scr()
            sq = st[:, 0:256].bitcast(BF16)
            fw.op("act", lambda e: e.activation(sq, pt[:], AF.Square), reads=[pr], writes=[sr])
            p2, p2r = new_ps()
            fw.op("pe", lambda e: e.matmul(p2[:], lhsT=onesblk_bf, rhs=sq, start=True, stop=True), reads=[sr, r_cst], writes=[p2r])
            rt, rr = new_scr()
            fw.op("act", lambda e: e.activation(rt[:, 0:512], p2[:], AF.Ln, bias=epsc[:, 0:1], scale=1.0 / 64), reads=[p2r, r_eps], writes=[rr])
            fw.op("act", lambda e: e.activation(rt[:, 0:512], rt[:, 0:512], AF.Exp, scale=-0.5), reads=[rr], writes=[rr])
            g = vcol(l, gname)
            if kout is not None:
                kst, kstr = new_scr()
                fw.op("dve", lambda e: e.scalar_tensor_tensor(kst[:, 0:512], pt[:], g, rt[:, 0:512], ALU.mult, ALU.mult),
                      reads=[pr, rr, r_vec], writes=[kstr])
                o = Reg()
                out_regs.append(o)
                fw.dma("sp", kout, kst[:, 0:512], reads=[kstr], writes=[o], stream="do")
            if rope_cols is None:
                if dst_hi is None:
                    fw.op("dve", lambda e: e.scalar_tensor_tensor(dst_ap, pt[:], g, rt[:, 0:512], ALU.mult, ALU.mult),
                          reads=[pr, rr, r_vec], writes=dst_regs)
                else:
                    fw.op("dve", lambda e: e.scalar_tensor_tensor(dst_ap[0:64], pt[0:64, :], g[0:64], rt[0:64, 0:512], ALU.mult, ALU.mult),
                          reads=[pr, rr, r_vec], writes=dst_regs)
                    fw.op("dve", lambda e: e.scalar_tensor_tensor(dst_hi[64:128], pt[64:128, :], g[64:128], rt[64:128, 0:512], ALU.mult, ALU.mult),
                          reads=[pr, rr, r_vec], writes=dst_hi_regs)
                return
            qt, qr = new_scr()
            qn = qt[:, 0:256].bitcast(BF16)
            fw.op("dve", lambda e: e.scalar_tensor_tensor(qn, pt[:], g, rt[:, 0:512], ALU.mult, ALU.mult),
                  reads=[pr, rr, r_vec], writes=[qr])
            p3, p3r = new_ps()
            fw.op("pe", lambda e: e.matmul(p3[:], lhsT=perm_bf, rhs=qn, start=True, stop=True), reads=[qr, r_cst], writes=[p3r])
            t1, t1r = new_scr()
            fw.op("pool", lambda e: e.tensor_tensor(t1[:, 0:512], qn, rope[:, 0, rope_cols], ALU.mult), reads=[qr, r_rope], writes=[t1r])
            t2, t2r = new_scr()
            fw.op("dve", lambda e: e.tensor_tensor(t2[:, 0:512], p3[:], rope[:, 1, rope_cols], ALU.mult), reads=[p3r, r_rope], writes=[t2r])
            if dst_hi is None:
                fw.op("pool", lambda e: e.tensor_tensor(dst_ap, t1[:, 0:512], t2[:, 0:512], ALU.add), reads=[t1r, t2r], writes=dst_regs)
            else:
                fw.op("dve", lambda e: e.tensor_tensor(dst_ap[0:64], t1[0:64, 0:512], t2[0:64, 0:512], ALU.add), reads=[t1r, t2r], writes=dst_regs)
                fw.op("dve", lambda e: e.tensor_tensor(dst_hi[64:128], t1[64:128, 0:512], t2[64:128, 0:512], ALU.add), reads=[t1r, t2r], writes=dst_hi_regs)

        def run_pass(p):
            T = TS if p == 0 else TP
            NB = T // 512
            sample = (p == 0)
            XIN, YOUT = (XS, YS) if sample else (XP, YP)
            if sample:
                seqs = [dict(t0=0, S=TS, past=PAST, kcol0=0, vch0=0, ucol0=0, nk=PAST + 2048)]
            else:
                seqs = [dict(t0=256 * i, S=256, past=0, kcol0=256 * i, vch0=2 * i, ucol0=272 * i, nk=256) for i in range(2)]
            for c in range(8):
                for tb in range(NB):
                    fw.dma("sp", x[:, c, tb * 512:(tb + 1) * 512], XIN[c, :, tb * 512:(tb + 1) * 512], writes=[x_r[c][tb]], stream="dx")

            def kreg(c, col):
                if sample:
                    return kt_r[c][0] if col < PAST else kt_r[c][1 + (col - PAST) // 1024]
                return kt_r[c][0]

            for l in range(L):
                if l == 1:
                    mod_flush()
                if sample:
                    for c in range(6):
                        fw.dma("pool", kt[:, c, 0:PAST], CK[l, c, :, :], writes=[kt_r[c][0]], stream="dc")
                    for j in range(2):
                        fw.dma("pool", vt[:, j, 0:576], CV[l, j, :, 0:576], writes=[vt_r[j]], stream="dc")
                        fw.dma("pool", vt[:, j, 640:704], CV[l, j, :, 576:640], writes=[vt_r[j]], stream="dc")
                fw.op("pool", lambda e: e.memset(vt[:, :, 576:640], 1.0), writes=vt_r)
                fw.op("pool", lambda e: e.memset(vt[:, :, 704:768], 1.0), writes=vt_r)
                fw.op("pool", lambda e: e.memset(ucb[:, :, :], 0.0), writes=[rr for row in uc_r for rr in row])

                for tb in range(NB):
                    rms_norm_block(l, p, tb, 0, lambda c: hT[:, c, :], lambda c: hT_r[c], tb * 512)
                    def k_post(c, pt, pr):
                        if sample:
                            kx_, kxr_ = new_scr()
                            kxb = kx_[:, 0:256].bitcast(BF16)
                            qk_post(l, pt, pr, "kna" if c < 4 else "knb", kxb, [kxr_],
                                    rope_cols=slice(tb * 512, tb * 512 + 512))
                            o = Reg()
                            xinK_regs.append(o)
                            fw.dma("sp", xinK[l].ap()[c * 128:(c + 1) * 128, tb * 512:(tb + 1) * 512], kxb, reads=[kxr_], writes=[o], stream="dxo")
                        else:
                            qk_post(l, pt, pr, "kna" if c < 4 else "knb", kt[:, c, 0:512], [kt_r[c][0]], kout=KO[l, c, :, :])
                    prev_ = None
                    for c in range(6):
                        wt, wr = wblk(l, BLK["K"] + c)
                        pt, pr = proj_fm(wt, wr, lambda k: hT[:, k, :], hT_r)
                        if prev_ is not None:
                            k_post(*prev_)
                        prev_ = (c, pt, pr)
                    k_post(*prev_)
                    for g in range(4):
                        wt, wr = wblk(l, BLK["UC"] + g)
                        pt, pr = proj_fm(wt, wr, lambda k: hT[:, k, :], hT_r)
                        if sample:
                            fw.op("act", lambda e, pt=pt, g=g, tb=tb: e.copy(ucb[:, g, 8 + tb * 512:8 + tb * 512 + 512], pt[:]),
                                  reads=[pr], writes=[uc_r[g][tb]])
                        else:
                            for i in range(2):
                                fw.op("act", lambda e, pt=pt, g=g, i=i: e.copy(ucb[:, g, 272 * i + 8:272 * i + 264], pt[:, 256 * i:256 * i + 256]),
                                      reads=[pr], writes=[uc_r[g][0]])
                    for j in range(5):
                        wt, wr = wblk(l, BLK["V"] + j)
                        for tt in range(4):
                            tok0 = tb * 512 + tt * 128
                            vch = tok0 // 128
                            pt, pr = new_ps()

                            def mmv(e, wt=wt, pt=pt, tt=tt):
                                for k in range(8):
                                    ins = e.matmul(pt[:, 0:128], lhsT=hT[:, k, tt * 128:(tt + 1) * 128], rhs=wt[:, k, :], start=(k == 0), stop=(k == 7))
                                return ins
                            fw.op("pe", mmv, reads=[wr] + hT_r, writes=[pr])
                            if j < 4:
                                dsts = [vt[:, vch, j * 128:(j + 1) * 128]]
                                srcs = [pt[:, 0:128]]
                            else:
                                dsts = [vt[:, vch, 512:576], vt[:, vch, 640:704]]
                                srcs = [pt[:, 0:64], pt[:, 64:128]]
                            if sample:
                                vx_, vxr_ = new_scr()
                                vxb = vx_[:, 0:64].bitcast(BF16)
                                fw.op("act", lambda e: e.copy(vxb, pt[:, 0:128]), reads=[pr], writes=[vxr_])
                                o = Reg()
                                xin_regs.append(o)
                                fw.dma("sp", xinV[l].ap()[tok0:tok0 + 128, j * 128:(j + 1) * 128], vxb, reads=[vxr_], writes=[o], stream="dxo")
                            else:
                                for d_, s_ in zip(dsts, srcs):
                                    fw.op("act", lambda e, d_=d_, s_=s_: e.copy(d_, s_), reads=[pr], writes=[vt_r[vch]])
                            if not sample:
                                vs_, vsr_ = new_scr()
                                fw.op("dve", lambda e, pt=pt, vs_=vs_: e.tensor_copy(vs_[:, 0:128], pt[:, 0:128]),
                                      reads=[pr], writes=[vsr_])
                                o = Reg()
                                out_regs.append(o)
                                fw.dma("sp", VO[l, tok0:tok0 + 128, j * 128:(j + 1) * 128], vs_[:, 0:128], reads=[vsr_], writes=[o], stream="do")

                if sample:
                    for g in range(4):
                        fw.op("dve", lambda e, g=g: e.tensor_copy(uh[:, g, 0:8], ucb[:, g, 8:16]), reads=uc_r[g], writes=[r_uh])
                        fw.op("dve", lambda e, g=g: e.tensor_copy(uh[:, g, 8:16], ucb[:, g, TS:TS + 8]), reads=uc_r[g], writes=[r_uh])
                    for g in range(4):
                        o = Reg()
                        xin_regs.append(o)
                        fw.dma("sp", xinV[l].ap()[g * 128:(g + 1) * 128, 640:672], uh[:, g, :], reads=[r_uh], writes=[o], stream="dxo")
                    r_xoutK, r_xout = Reg(), Reg()
                    fw.op("pool", lambda e: e.collective_compute("AllGather", ALU.bypass, replica_groups=PAIRS,
                                                                 ins=[xinK[l].ap()], outs=[xoutK[l].ap()]),
                          reads=list(xinK_regs), writes=[r_xoutK])
                    fw.op("pool", lambda e: e.collective_compute("AllGather", ALU.bypass, replica_groups=PAIRS,
                                                                 ins=[xinV[l].ap()], outs=[xoutV[l].ap()]),
                          reads=list(xin_regs), writes=[r_xout])
                    del xin_regs[:]
                    del xinK_regs[:]
                    for r_ in range(2):
                        for c in range(6):
                            fw.dma("sp", kt[:, c, PAST + r_ * 1024:PAST + (r_ + 1) * 1024], xoutK[l].ap()[r_ * 768 + c * 128:r_ * 768 + (c + 1) * 128, :],
                                   reads=[r_xoutK], writes=[kt_r[c][1 + r_]], stream="dxi")
                        base = r_ * 1024
                        for j in range(8):
                            ch = 2 + r_ * 8 + j
                            fw.dma("sp", vt[:, ch, 0:576], xoutV[l].ap()[base + j * 128:base + (j + 1) * 128, 0:576],
                                   reads=[r_xout], writes=[vt_r[ch]], stream="dxi")
                            fw.dma("sp", vt[:, ch, 640:704], xoutV[l].ap()[base + j * 128:base + (j + 1) * 128, 576:640],
                                   reads=[r_xout], writes=[vt_r[ch]], stream="dxi")
                    for g in range(4):
                        for r_ in range(2):
                            fw.dma("sp", halo_u[:, g, r_, :], xoutV[l].ap()[r_ * 1024 + g * 128:r_ * 1024 + (g + 1) * 128, 640:672],
                                   reads=[r_xout], writes=[r_halo_u], stream="dxi")
                    for g in range(4):
                        fw.op("pool", lambda e, g=g: e.tensor_scalar(ucb[:, g, 0:8], halo_u[:, g, 0, 8:16], msk[:, 0:1], None, ALU.mult),
                              reads=[r_halo_u, r_msk], writes=uc_r[g])
                        fw.op("pool", lambda e, g=g: e.tensor_scalar(ucb[:, g, 8 + TS:16 + TS], halo_u[:, g, 1, 0:8], msk[:, 1:2], None, ALU.mult),
                              reads=[r_halo_u, r_msk], writes=uc_r[g])

                for tb in range(NB):
                    c0 = tb * 512
                    rms_norm_block(l, p, tb, 0, lambda c: hT[:, c, :], lambda c: hT_r[c], c0)
                    fw.op("dve", lambda e: e.memset(qm[64:128, :, :], 0.0), writes=qm_r)
                    prev_ = None
                    for c in range(8):
                        wt, wr = wblk(l, BLK["Q"] + c)
                        pt, pr = proj_fm(wt, wr, lambda k: hT[:, k, :], hT_r)
                        if prev_ is not None:
                            qk_post(l, prev_[1], prev_[2], "qna" if prev_[0] < 4 else "qnb", qm[:, prev_[0], :], [qm_r[prev_[0]]],
                                    rope_cols=slice(c0, c0 + 512) if sample else None, dst_hi=qo[:, prev_[0], :], dst_hi_regs=[qo_r[prev_[0]]])
                        prev_ = (c, pt, pr)
                    qk_post(l, prev_[1], prev_[2], "qna" if prev_[0] < 4 else "qnb", qm[:, prev_[0], :], [qm_r[prev_[0]]],
                            rope_cols=slice(c0, c0 + 512) if sample else None, dst_hi=qo[:, prev_[0], :], dst_hi_regs=[qo_r[prev_[0]]])
                    if not sample and l == 0:
                        tap(0, hT, hT_r)
                        tap(1, qm, qm_r)
                    if sample:
                        units = [(seqs[0], 0, 512)]
                    else:
                        units = [(seqs[0], 0, 256), (seqs[1], 256, 256)]
                    LAG = 2
                    order = [("d", 0), ("g", 0), ("g", 1), ("d", 1), ("g", 2), ("g", 3), ("d", 2), ("g", 4), ("g", 5), ("d", 3), ("g", 6), ("g", 7)]
                    for (sq_, qc0, NQ) in units:
                        nkc = sq_["nk"] // 128
                        qs = slice(qc0, qc0 + NQ)
                        steps = []
                        for (kind, h) in order:
                            for kc in range(nkc):
                                for a_ in range(2 if kind == "d" else 1):
                                    steps.append((kind, h, kc, a_))
                        pend = []

                        def rec_qk(kind, h, kc, a_):
                            ksl = slice(sq_["kcol0"] + kc * 128, sq_["kcol0"] + (kc + 1) * 128)
                            if kind == "d":
                                hp = (h % 2) * 64
                                kch = a_ * 2 + h // 2
                                qch = kch
                            else:
                                hp = (h % 2) * 64
                                kch = 4 + h // 4
                                qch = 4 + h // 2
                            sc_t, sc_r = new_ps((0, 1, 2))
                            qsrc, qsrc_r = (qm, qm_r) if hp == 0 else (qo, qo_r)
                            fw.op("pe", lambda e: e.matmul(sc_t[:, 0:NQ], lhsT=kt[:, kch, ksl], rhs=qsrc[:, qch, qs], start=True, stop=True),
                                  reads=[kreg(kch, kc * 128), qsrc_r[qch]], writes=[sc_r])
                            pt_, ptr_ = new_scr()
                            pT = pt_[:, 0:256].bitcast(BF16)
                            fw.op("act", lambda e: e.activation(pT[:, 0:NQ], sc_t[:, 0:NQ], AF.Exp, scale=0.125), reads=[sc_r], writes=[ptr_])
                            return pT, ptr_

                        def rec_pv(kind, h, kc, a_, pT, ptr_):
                            vch = sq_["vch0"] + kc
                            first, last = (kc == 0), (kc == nkc - 1)
                            if kind == "d":
                                bset = h % 2
                                o_t, o_r = ps[4 + 2 * bset + a_], ps_r[4 + 2 * bset + a_]
                                fw.op("pe", lambda e: e.matmul(o_t[:, 0:NQ], lhsT=vt[:, vch, h * 128:(h + 1) * 128], rhs=pT[:, 0:NQ], start=first, stop=last),
                                      reads=[ptr_, vt_r[vch]], writes=[o_r])
                                pa, par = pacc[bset][a_], pacc_r[bset][a_]
                                eng_ = "dve"
                                if first:
                                    fw.op(eng_, lambda e: e.tensor_copy(pa[:, 0:NQ], pT[:, 0:NQ]), reads=[ptr_], writes=[par])
                                else:
                                    fw.op(eng_, lambda e: e.tensor_tensor(pa[:, 0:NQ], pa[:, 0:NQ], pT[:, 0:NQ], ALU.add), reads=[ptr_, par], writes=[par])
                                if last and a_ == 1:
                                    diff_epilogue(h)
                            else:
                                g = h // 4
                                o_t, o_r = ps[3], ps_r[3]
                                fw.op("pe", lambda e: e.matmul(o_t[:, 0:NQ], lhsT=vt[:, vch, 512 + g * 128:640 + g * 128], rhs=pT[:, 0:NQ], start=first, stop=last),
                                      reads=[ptr_, vt_r[vch]], writes=[o_r])
                                if last:
                                    jp = (h % 2) * 64
                                    qc = 4 + h // 2
                                    rc, rcr = new_scr()
                                    fw.op("act", lambda e: e.activation(rc[64:128, 0:NQ], o_t[64:128, 0:NQ], AF.Ln), reads=[o_r], writes=[rcr])
                                    fw.op("act", lambda e: e.activation(rc[64:128, 0:NQ], rc[64:128, 0:NQ], AF.Exp, scale=-1.0), reads=[rcr], writes=[rcr])
                                    fw.op("dve", lambda e: e.tensor_tensor(ao[jp:jp + 64, qc, qs], o_t[0:64, 0:NQ], rc[64:128, 0:NQ], ALU.mult),
                                          reads=[o_r, rcr], writes=[ao_r[qc]])

                        def diff_epilogue(h):
                            bset = h % 2
                            (o1, o1r), (o2, o2r) = [(ps[i], ps_r[i]) for i in (4 + 2 * bset, 5 + 2 * bset)]
                            r1, r1r = new_scr()
                            r2, r2r = new_scr()
                            for a_, (rx, rxr) in enumerate(((r1, r1r), (r2, r2r))):
                                d_t, d_r = new_ps((0, 1, 2))
                                fw.op("pe", lambda e: e.matmul(d_t[:, 0:NQ], lhsT=ones_f[:], rhs=pacc[bset][a_][:, 0:NQ], start=True, stop=True),
                                      reads=[pacc_r[bset][a_], r_onesf], writes=[d_r])
                                fw.op("act", lambda e: e.activation(rx[:, 0:NQ], d_t[:, 0:NQ], AF.Ln), reads=[d_r], writes=[rxr])
                                fw.op("act", lambda e: e.activation(rx[:, 0:NQ], rx[:, 0:NQ], AF.Exp, scale=-1.0), reads=[rxr], writes=[rxr])
                            fw.op("dve", lambda e: e.tensor_tensor(r1[:, 0:NQ], o1[:, 0:NQ], r1[:, 0:NQ], ALU.mult), reads=[o1r, r1r], writes=[r1r])
                            fw.op("dve", lambda e: e.tensor_tensor(r2[:, 0:NQ], o2[:, 0:NQ], r2[:, 0:NQ], ALU.mult), reads=[o2r, r2r], writes=[r2r])
                            fw.op("dve", lambda e: e.scalar_tensor_tensor(r1[:, 0:NQ], r2[:, 0:NQ], lamv[:, l, 0:1], r1[:, 0:NQ], ALU.mult, ALU.add),
                                  reads=[r2r, r_lam], writes=[r1r])
                            s_t, s_r = new_scr()
                            sqb = s_t[:, 0:256].bitcast(BF16)
                            fw.op("act", lambda e: e.activation(sqb[:, 0:NQ], r1[:, 0:NQ], AF.Square), reads=[r1r], writes=[s_r])
                            pss, pssr = new_ps((0, 1, 2))
                            fw.op("pe", lambda e: e.matmul(pss[:, 0:NQ], lhsT=ones_bf, rhs=sqb[:, 0:NQ], start=True, stop=True),
                                  reads=[s_r, r_cst], writes=[pssr])
                            fw.op("act", lambda e: e.activation(r2[:, 0:NQ], pss[:, 0:NQ], AF.Ln, bias=epsc[:, 0:1], scale=1.0 / 128),
                                  reads=[pssr, r_eps], writes=[r2r])
                            fw.op("act", lambda e: e.activation(r2[:, 0:NQ], r2[:, 0:NQ], AF.Exp, scale=-0.5), reads=[r2r], writes=[r2r])
                            fw.op("dve", lambda e: e.scalar_tensor_tensor(ao[:, h, qs], r1[:, 0:NQ], lamv[:, l, 1:2], r2[:, 0:NQ], ALU.mult, ALU.mult),
                                  reads=[r1r, r2r, r_lam], writes=[ao_r[h]])

                        for i in range(len(steps) + LAG):
                            if i < len(steps):
                                pend.append(rec_qk(*steps[i]))
                            if i >= LAG:
                                pT, ptr_ = pend.pop(0)
                                rec_pv(*steps[i - LAG], pT, ptr_)
                            if (not sample) and l == 0:
                                mod_step((0, 1, 2))
                    PO = 64 if sample else 0
                    for g in range(4):
                        w = 2 << g
                        if sample:
                            segs = [(seqs[0]["ucol0"] + c0, 512, 0, c0 == 0, c0 + 512 == TS)]
                        else:
                            segs = [(272 * i, 256, 256 * i, True, True) for i in range(2)]
                        dbf, dr_ = dbuf[:, :], r_dbuf
                        for (uc0, nt, oc0, is_s, is_e) in segs:
                            cur, cur_r = new_scr()
                            fw.op("pool", lambda e, cur=cur, uc0=uc0, nt=nt, g=g: e.tensor_copy(cur[:, 0:nt + 16], ucb[:, g, uc0:uc0 + nt + 16]),
                                  reads=uc_r[g], writes=[cur_r])
                            u_t, u_r = cur, cur_r
                            lo, hi = 0, nt + 16
                            step = 1
                            first = True
                            ww = 2
                            while ww <= w:
                                nxt, nxt_r = new_scr()
                                if first:
                                    fw.op("pool", lambda e, nxt=nxt, cur=cur, hi=hi: e.tensor_tensor(nxt[:, 1:hi], cur[:, 0:hi - 1], cur[:, 1:hi], ALU.add),
                                          reads=[cur_r], writes=[nxt_r])
                                    lo, hi = 1, hi
                                    first = False
                                else:
                                    sft = ww // 4
                                    fw.op("pool", lambda e, nxt=nxt, cur=cur, lo=lo, hi=hi, sft=sft: e.tensor_tensor(
                                        nxt[:, lo + sft:hi - sft], cur[:, lo:hi - 2 * sft], cur[:, lo + 2 * sft:hi], ALU.add),
                                        reads=[cur_r], writes=[nxt_r])
                                    lo, hi = lo + sft, hi - sft
                                cur, cur_r = nxt, nxt_r
                                ww *= 2
                            fw.op("dve", lambda e, cur=cur, u_t=u_t, nt=nt, oc0=oc0, w=w: e.scalar_tensor_tensor(
                                dbf[:, oc0:oc0 + nt], cur[:, 8:8 + nt], 1.0 / w, u_t[:, 8:8 + nt], ALU.mult, ALU.subtract),
                                reads=[cur_r, u_r], writes=[dr_])
                            hw_ = w // 2
                            if is_s:
                                fw.op("pool", lambda e, cur=cur, hw_=hw_, g=g: e.tensor_tensor(cur[:, 8:8 + hw_], cur[:, 8:8 + hw_], pcnt[:, PO + g * 16:PO + g * 16 + hw_], ALU.mult),
                                      reads=[cur_r, r_pcnt], writes=[cur_r])
                                fw.op("pool", lambda e, cur=cur, u_t=u_t, hw_=hw_, oc0=oc0: e.tensor_tensor(dbf[:, oc0:oc0 + hw_], cur[:, 8:8 + hw_], u_t[:, 8:8 + hw_], ALU.subtract),
                                      reads=[cur_r, u_r], writes=[dr_])
                            if is_e and hw_ > 1:
                                ne = hw_ - 1
                                a0 = 8 + nt - ne
                                fw.op("pool", lambda e, cur=cur, ne=ne, a0=a0, g=g: e.tensor_tensor(cur[:, a0:a0 + ne], cur[:, a0:a0 + ne], pcnt[:, PO + g * 16 + 8:PO + g * 16 + 8 + ne], ALU.mult),
                                      reads=[cur_r, r_pcnt], writes=[cur_r])
                                fw.op("pool", lambda e, cur=cur, u_t=u_t, ne=ne, a0=a0, oc0=oc0, nt=nt: e.tensor_tensor(
                                    dbf[:, oc0 + nt - ne:oc0 + nt], cur[:, a0:a0 + ne], u_t[:, a0:a0 + ne], ALU.subtract),
                                    reads=[cur_r, u_r], writes=[dr_])
                        wt, wr = wblk(l, BLK["POOLW"] + g, kc=1)
                        pt, pr = proj_fm(wt, wr, lambda k, dbf=dbf: dbf[:, 0:512], [dr_], kc=1)
                        fw.op("act", lambda e, pt=pt, g=g: e.activation(ocb[:, g, :], pt[:], AF.Copy, scale=vcol(l, "pscale", g)),
                              reads=[pr, r_vec], writes=[oc_r[g]])
                    if not sample and l == 0:
                        tap(2, ao, ao_r)
                        tap(3, ocb, oc_r, nchunk=4)
                    for m in range(8):
                        gts = []
                        for br in range(3):
                            wt, wr = wblk(l, BLK["GATE"] + br * 8 + m)
                            pt, pr = proj_fm(wt, wr, lambda k: hT[:, k, :], hT_r)
                            gt, gr = new_scr()
                            fw.op("act", lambda e, gt=gt, pt=pt, br=br, m=m: e.activation(gt[:, 0:512], pt[:], AF.Sigmoid, bias=vcol(l, "bgate", br * 8 + m)),
                                  reads=[pr, r_vec], writes=[gr])
                            srcT, srcR = [(ao, ao_r), (ao, ao_r), (ocb, oc_r)][br]
                            off = [0, 4, 0][br]
                            wt2, wr2 = wblk(l, BLK[("BRA", "BRB", "BRC")[br]] + m, kc=4)
                            pt2, pr2 = proj_fm(wt2, wr2, lambda k, srcT=srcT, off=off: srcT[:, off + k, :], srcR[off:off + 4], kc=4)
                            fw.op("dve", lambda e, gt=gt, pt2=pt2: e.tensor_tensor(gt[:, 0:512], gt[:, 0:512], pt2[:], ALU.mult), reads=[pr2, gr], writes=[gr])
                            gts.append((gt, gr))
                        (g0, g0r), (g1, g1r), (g2, g2r) = gts
                        fw.op("pool", lambda e, g0=g0, g1=g1: e.tensor_tensor(g0[:, 0:512], g0[:, 0:512], g1[:, 0:512], ALU.add), reads=[g1r, g0r], writes=[g0r])
                        fw.op("pool", lambda e, g0=g0, g2=g2, m=m: e.tensor_tensor(qm[:, m, :], g0[:, 0:512], g2[:, 0:512], ALU.add), reads=[g0r, g2r], writes=[qm_r[m]])
                    if not sample and l == 0:
                        tap(4, qm, qm_r)
                    for m in range(8):
                        wt, wr = wblk(l, BLK["OUT"] + m)
                        pt, pr = proj_fm(wt, wr, lambda k: qm[:, k, :], qm_r)
                        fw.op("dve", lambda e, pt=pt, m=m: e.scalar_tensor_tensor(x[:, m, c0:c0 + 512], pt[:], dv[:, l, p, 2, m:m + 1], x[:, m, c0:c0 + 512], ALU.mult, ALU.add),
                              reads=[pr, r_dv, x_r[m][tb]], writes=[x_r[m][tb]])

                if not sample and l == 0:
                    tap(5, x, [x_r[c][0] for c in range(8)], is_bf=False)
                alias_src = [rr for row in kt_r for rr in row] + [rr for row in uc_r for rr in row] + hT_r + qm_r + ao_r + oc_r
                for rr in [q for row in hid_r for q in row] + [q for row in h2_r for q in row] + [q for row in stg_r for q in row]:
                    for s_ in alias_src:
                        rr.inherit(s_)
                for tb in range(NB):
                    rms_norm_block(l, p, tb, 1, lambda c, tb=tb: h2T[:, c, tb * 512:(tb + 1) * 512], lambda c, tb=tb: h2_r[c][tb], tb * 512)
                fw.op("pool", lambda e: e.memset(stg[:, :, :], 0.0), writes=[q for row in stg_r for q in row] + stg_h)
                if sample:
                    fw.op("dve", lambda e: e.tensor_copy(h2s[:, 0:8], h2T[:, :, 0]), reads=[h2_r[c][0] for c in range(8)], writes=[r_h2s])
                    fw.op("dve", lambda e: e.tensor_copy(h2s[:, 8:16], h2T[:, :, TS - 1]), reads=[h2_r[c][NB - 1] for c in range(8)], writes=[r_h2s])
                    r_x2i, r_x2o = Reg(), Reg()
                    fw.dma("sp", xin2[l].ap()[:, :], h2s[:], reads=[r_h2s], writes=[r_x2i], stream="dxo")
                    fw.op("pool", lambda e: e.collective_compute("AllGather", ALU.bypass, replica_groups=PAIRS,
                                                                 ins=[xin2[l].ap()], outs=[xout2[l].ap()]),
                          reads=[r_x2i], writes=[r_x2o])
                    for r_ in range(2):
                        fw.dma("sp", h2g[:, r_, :], xout2[l].ap()[r_ * 128:(r_ + 1) * 128, :], reads=[r_x2o], writes=[r_h2g], stream="dxi")
                    fw.op("pool", lambda e: e.tensor_scalar(h2h[:, :, 0], h2g[:, 0, 8:16], msk[:, 0:1], None, ALU.mult), reads=[r_h2g, r_msk], writes=[r_h2h])
                    fw.op("pool", lambda e: e.tensor_scalar(h2h[:, :, 1], h2g[:, 1, 0:8], msk[:, 1:2], None, ALU.mult), reads=[r_h2g, r_msk], writes=[r_h2h])
                hbufs, hregs = [hid, hid2], [hid_r, hid2_r]

                def up_round(hh):
                    n_h = 8 if hh < 2 else 6
                    hb, hr = hbufs[hh % 2], hregs[hh % 2]
                    for jj in range(n_h):
                        j = hh * 8 + jj
                        accs = {}
                        for half in range(2):
                            ch = half * 22 + j
                            wt, wr = wblk(l, BLK["UP"] + ch)
                            if sample:
                                pth, prh = proj_fm(wt, wr, lambda k: h2h[:, k, :], [r_h2h], n=2)
                                fw.op("act", lambda e, pth=pth, half=half: e.copy(stg[:, half, 0:1], pth[:, 0:1]), reads=[prh], writes=[stg_h[half]])
                                fw.op("act", lambda e, pth=pth, half=half: e.copy(stg[:, half, TS + 1:TS + 2], pth[:, 1:2]), reads=[prh], writes=[stg_h[half]])
                            for tb in range(NB):
                                pt, pr = proj_fm(wt, wr, lambda k, tb=tb: h2T[:, k, tb * 512:(tb + 1) * 512], [h2_r[k][tb] for k in range(8)])
                                if sample:
                                    fw.op("act", lambda e, pt=pt, half=half, tb=tb: e.copy(stg[:, half, 1 + tb * 512:1 + tb * 512 + 512], pt[:]),
                                          reads=[pr], writes=[stg_r[half][tb]])
                                else:
                                    for i in range(2):
                                        fw.op("act", lambda e, pt=pt, half=half, i=i: e.copy(stg[:, half, 258 * i + 1:258 * i + 257], pt[:, 256 * i:256 * i + 256]),
                                              reads=[pr], writes=[stg_r[half][0]])
                        for tb in range(NB):
                            for half in range(2):
                                ch = half * 22 + j
                                acc, accr = new_scr()
                                eng = "dve"
                                w0, w1, w2 = (vcol(l, "convw", t_ * 44 + ch) for t_ in range(3))
                                bcol = vcol(l, "convb", ch)
                                if sample:
                                    pieces = [(1 + tb * 512, 512, 0)]
                                    rd = [stg_r[half][t_] for t_ in range(max(0, tb - 1), min(NB, tb + 2))] + [stg_h[half]]
                                else:
                                    pieces = [(258 * i + 1, 256, 256 * i) for i in range(2)]
                                    rd = [stg_r[half][0]]
                                for (s0, n_, o0) in pieces:
                                    fw.op("act", lambda e, acc=acc, s0=s0, n_=n_, o0=o0, half=half, w1=w1, bcol=bcol: e.activation(
                                        acc[:, o0:o0 + n_], stg[:, half, s0:s0 + n_], AF.Identity, bias=bcol, scale=w1), reads=rd + [r_vec], writes=[accr])
                                    fw.op(eng, lambda e, acc=acc, s0=s0, n_=n_, o0=o0, half=half, w0=w0: e.scalar_tensor_tensor(
                                        acc[:, o0:o0 + n_], stg[:, half, s0 - 1:s0 - 1 + n_], w0, acc[:, o0:o0 + n_], ALU.mult, ALU.add), reads=rd + [r_vec, accr], writes=[accr])
                                    fw.op(eng, lambda e, acc=acc, s0=s0, n_=n_, o0=o0, half=half, w2=w2: e.scalar_tensor_tensor(
                                        acc[:, o0:o0 + n_], stg[:, half, s0 + 1:s0 + 1 + n_], w2, acc[:, o0:o0 + n_], ALU.mult, ALU.add), reads=rd + [r_vec, accr], writes=[accr])
                                accs[half] = (acc, accr)
                            (aa, aar), (ag, agr) = accs[0], accs[1]
                            fw.op("act", lambda e, aa=aa: e.activation(aa[:, 0:512], aa[:, 0:512], AF.Silu), reads=[aar], writes=[aar])
                            fw.op("dve", lambda e, aa=aa, ag=ag, jj=jj, tb=tb: e.tensor_tensor(hb[:, jj, tb * 512:(tb + 1) * 512], aa[:, 0:512], ag[:, 0:512], ALU.mult),
                                  reads=[aar, agr], writes=[hr[jj][tb]])

                def down_round(hh):
                    n_h = 8 if hh < 2 else 6
                    hb, hr = hbufs[hh % 2], hregs[hh % 2]
                    for m in range(8):
                        wt, wr = wblk(l, BLK["DOWN"] + m * 3 + hh, kc=n_h)
                        for tb in range(NB):
                            pt, pr = new_ps()

                            def mmd(e, pt=pt, tb=tb, wt=wt, n_h=n_h):
                                for jj in range(n_h):
                                    ins = e.matmul(pt[:], lhsT=wt[:, jj, :], rhs=hb[:, jj, tb * 512:(tb + 1) * 512], start=(jj == 0), stop=(jj == n_h - 1))
                                return ins
                            fw.op("pe", mmd, reads=[wr] + [hr[jj][tb] for jj in range(n_h)], writes=[pr])
                            fw.op("dve", lambda e, pt=pt, m=m, tb=tb: e.scalar_tensor_tensor(
                                x[:, m, tb * 512:(tb + 1) * 512], pt[:], dv[:, l, p, 5, m:m + 1], x[:, m, tb * 512:(tb + 1) * 512], ALU.mult, ALU.add),
                                reads=[pr, r_dv, x_r[m][tb]], writes=[x_r[m][tb]])
                if not sample and l == 0:
                    tap(6, x, [x_r[c][0] for c in range(8)], is_bf=False)

                up_round(0)
                up_round(1)
                down_round(0)
                up_round(2)
                down_round(1)
                down_round(2)
                alias_src = [q for row in hid_r for q in row] + [q for row in h2_r for q in row] + [q for row in stg_r for q in row]
                for rr in [q for row in kt_r for q in row] + [q for row in uc_r for q in row] + hT_r + qm_r + ao_r + oc_r:
                    for s_ in alias_src:
                        rr.inherit(s_)
            for c in range(8):
                for tb in range(NB):
                    o = Reg()
                    out_regs.append(o)
                    fw.dma("sp", YOUT[c, :, tb * 512:(tb + 1) * 512], x[:, c, tb * 512:(tb + 1) * 512], reads=[x_r[c][tb]], writes=[o], stream="do")

        run_pass(1)
        run_pass(0)
        if wseq_out is not None:
            return None
        fw.final_wait("sp", out_regs)
        fw.emit(es)
    return nc


def _blk(wcols):
    K = wcols.shape[0]
    kc = K // 128
    out = np.zeros((128, 1024), np.float32)
    out[:, :kc * 128] = wcols.reshape(kc, 128, 128).transpose(1, 0, 2).reshape(128, kc * 128)
    return out


def _fm(v):
    return np.ascontiguousarray(v.reshape(-1, 128).T)


def _consts():
    cst = np.zeros((128, 384), np.float32)
    cst[:, 0:128] = 1.0
    cst[0:64, 128:192] = 1.0
    cst[64:128, 192:256] = 1.0
    for i in range(128):
        d = i % 64
        part = (d % 32) // 16
        partner = i + 16 if part == 0 else i - 16
        cst[partner, 256 + i] = 1.0
    t = np.arange(2048)
    row = (t // GRID_W).astype(np.float32)
    col = (t % GRID_W).astype(np.float32)
    inv = (10000.0 ** (-np.arange(16, dtype=np.float32) / 16)).astype(np.float32)
    rope = np.zeros((2, 128, 2048), np.float32)
    for i in range(128):
        d = i % 64
        axis, part, f = d // 32, (d % 32) // 16, d % 16
        ang = (row if axis == 0 else col) * inv[f]
        rope[0, i] = np.cos(ang)
        rope[1, i] = np.sin(ang) * (-1.0 if part == 0 else 1.0)
    pc = np.zeros((128, 64), np.float32)
    for g in range(4):
        w = 2 << g
        hw = w // 2
        for tt in range(hw):
            pc[:, g * 16 + tt] = 1.0 / (tt + hw)
        ne = hw - 1
        for i in range(ne):
            pc[:, g * 16 + 8 + i] = 1.0 / (ne - i + hw)
    return cst, rope, pc


def _pc_sample(rank):
    pc = np.zeros((128, 64), np.float32)
    for g in range(4):
        w = 2 << g
        hw = w // 2
        for tt in range(hw):
            pc[:, g * 16 + tt] = (1.0 / (tt + hw)) if rank == 0 else (1.0 / w)
        ne = hw - 1
        for i in range(ne):
            pc[:, g * 16 + 8 + i] = (1.0 / (ne - i + hw)) if rank == 1 else (1.0 / w)
    return pc


_NC_CACHE = {}


def kernel(x_prompt, x_sample, cache_diff_k, cache_diff_v, cache_gqa_k, cache_gqa_v, c, c_ctx,
           norm1_g, norm2_g, w_mod, b_mod, w_in, qn_a, kn_a, qn_b, kn_b,
           lam_q1, lam_k1, lam_q2, lam_k2, subln_g, w_pool, pool_scale,
           w_br_a, w_br_b, w_br_c, w_gate, b_gate, w_out, w_up, conv_w, conv_b, w_down):
    f = lambda a: np.asarray(a, dtype=np.float32)
    x_prompt, x_sample = f(x_prompt), f(x_sample)
    Wb = np.zeros((L, NBLK, 128, 1024), np.float32)
    vec = np.zeros((L, 128, NVEC), np.float32)
    for l in range(L):
        wi = f(w_in[l])
        for i in range(4):
            Wb[l, BLK["Q"] + i] = _blk(wi[:, i * 128:(i + 1) * 128])
            Wb[l, BLK["Q"] + 4 + i] = _blk(wi[:, 1536 + i * 128:1536 + (i + 1) * 128])
            Wb[l, BLK["K"] + i] = _blk(wi[:, 512 + i * 128:512 + (i + 1) * 128])
            Wb[l, BLK["V"] + i] = _blk(wi[:, 1024 + i * 128:1024 + (i + 1) * 128])
            Wb[l, BLK["UC"] + i] = _blk(wi[:, 2304 + i * 128:2304 + (i + 1) * 128])
        for g in range(2):
            kb = wi[:, 2048 + g * 64:2048 + (g + 1) * 64]
            Wb[l, BLK["K"] + 4 + g] = _blk(np.concatenate([kb, kb], axis=1))
        Wb[l, BLK["V"] + 4] = _blk(wi[:, 2176:2304])
        wg = f(w_gate[l])
        for j in range(24):
            Wb[l, BLK["GATE"] + j] = _blk(wg[:, j * 128:(j + 1) * 128])
        for nm, wsrc in (("BRA", w_br_a), ("BRB", w_br_b), ("BRC", w_br_c)):
            ws = f(wsrc[l])
            for m in range(8):
                Wb[l, BLK[nm] + m] = _blk(ws[:, m * 128:(m + 1) * 128])
        wo = f(w_out[l])
        for m in range(8):
            Wb[l, BLK["OUT"] + m] = _blk(wo[:, m * 128:(m + 1) * 128])
        wu = f(w_up[l])
        for j in range(44):
            Wb[l, BLK["UP"] + j] = _blk(wu[:, j * 128:(j + 1) * 128])
        wd = f(w_down[l])
        for m in range(8):
            for kg in range(3):
                rows = wd[kg * 1024:min((kg + 1) * 1024, DFF), m * 128:(m + 1) * 128]
                Wb[l, BLK["DOWN"] + m * 3 + kg] = _blk(rows)
        wp = f(w_pool[l])
        for g in range(4):
            Wb[l, BLK["POOLW"] + g] = _blk(wp[g])
        wm = f(w_mod[l])
        for j in range(48):
            Wb[l, BLK["MOD"] + j] = _blk(wm[:, j * 128:(j + 1) * 128])
        vec[l, :, VC["norm1"]:VC["norm1"] + 8] = _fm(f(norm1_g[l]))
        vec[l, :, VC["norm2"]:VC["norm2"] + 8] = _fm(f(norm2_g[l]))
        vec[l, :, VC["bmod"]:VC["bmod"] + 48] = _fm(f(b_mod[l]))
        vec[l, :, VC["bgate"]:VC["bgate"] + 24] = _fm(f(b_gate[l]))
        cw = f(conv_w[l])
        for t_ in range(3):
            vec[l, :, VC["convw"] + t_ * 44:VC["convw"] + (t_ + 1) * 44] = _fm(cw[t_])
        vec[l, :, VC["convb"]:VC["convb"] + 44] = _fm(f(conv_b[l]))
        vec[l, :, VC["pscale"]:VC["pscale"] + 4] = _fm(f(pool_scale[l]))
        for nm, src in (("qna", qn_a), ("kna", kn_a), ("qnb", qn_b), ("knb", kn_b)):
            vec[l, :, VC[nm]] = np.tile(f(src[l]), 2)
        vec[l, :, VC["subln"]] = f(subln_g[l])
        for i, src in enumerate((lam_q1, lam_k1, lam_q2, lam_k2)):
            vec[l, 0:64, VC["lam"] + i] = f(src[l])
    cst, rope_full, pc = _consts()
    cdk, cdv, cgk, cgv = f(cache_diff_k), f(cache_diff_v), f(cache_gqa_k), f(cache_gqa_v)
    in_maps = []
    for core in range(8):
        b = core // 2
        rank = core % 2
        xs = np.ascontiguousarray(x_sample[b, rank * TS:(rank + 1) * TS].T).reshape(8, 128, TS)
        rope = np.ascontiguousarray(rope_full[:, :, rank * TS:(rank + 1) * TS])
        mskv = np.zeros((128, 16), np.float32)
        mskv[:, 0] = float(rank)
        mskv[:, 1] = float(1 - rank)
        pcs = np.concatenate([pc, _pc_sample(rank)], axis=1)
        xp = np.ascontiguousarray(x_prompt[2 * core:2 * core + 2].reshape(TP, D).T).reshape(8, 128, TP)
        ct = np.concatenate([_fm(f(c)[b]), _fm(f(c_ctx))], axis=1)
        ck = np.zeros((L, 6, 128, PAST), np.float32)
        cv = np.zeros((L, 2, 128, 640), np.float32)
        for l in range(L):
            ka = cdk[b, l].reshape(PAST, 512).T
            ck[l, 0:4] = ka.reshape(4, 128, PAST)
            kb = cgk[b, l].reshape(PAST, 128).T
            for g in range(2):
                ck[l, 4 + g] = np.concatenate([kb[g * 64:(g + 1) * 64], kb[g * 64:(g + 1) * 64]], axis=0)
            va = cdv[b, l].reshape(PAST, 512)
            vb = cgv[b, l].reshape(PAST, 128)
            cv[l] = np.concatenate([va, vb], axis=1).reshape(2, 128, 640)
        in_maps.append(dict(xs=xs, xp=xp, w=Wb, vec=vec, ct=np.ascontiguousarray(ct), cst=cst, rope=rope, pcnt=pcs, msk=mskv, ck=ck, cv=cv))
    if "nc" not in _NC_CACHE:
        seq = []
        build_program(wseq=None, wseq_out=seq)
        _NC_CACHE["nc"] = build_program(wseq=seq)
    nc = _NC_CACHE["nc"]
    res = run_bass_kernel_spmd(nc, in_maps, core_ids=list(range(8)))
    R_ = res.results
    if DEBUG:
        DBG_OUT["dbg"] = [r["dbg"] for r in R_]
    y_p = np.zeros((16, 256, D), np.float32)
    y_s = np.zeros((4, 2048, D), np.float32)
    ndk = np.zeros((16, L, 256, 2, 4, 64), np.float32)
    ndv = np.zeros((16, L, 256, 4, 128), np.float32)
    ngk = np.zeros((16, L, 256, 2, 64), np.float32)
    ngv = np.zeros((16, L, 256, 2, 64), np.float32)
    for core in range(8):
        r = R_[core]
        yp = r["yp"].reshape(D, TP).T.reshape(2, 256, D)
        y_p[2 * core:2 * core + 2] = yp
        ys = r["ys"].reshape(D, TS).T
        b, hf = core // 2, core % 2
        y_s[b, hf * TS:(hf + 1) * TS] = ys
        ko, vo = r["ko"], r["vo"]
        for l in range(L):
            ka = ko[l, 0:4].reshape(512, TP).T.reshape(2, 256, 2, 4, 64)
            ndk[2 * core:2 * core + 2, l] = ka
            kb = np.stack([ko[l, 4, 0:64, :], ko[l, 5, 0:64, :]], axis=0)
            ngk[2 * core:2 * core + 2, l] = kb.transpose(2, 0, 1).reshape(2, 256, 2, 64)
            v = vo[l].reshape(2, 256, 640)
            ndv[2 * core:2 * core + 2, l] = v[:, :, 0:512].reshape(2, 256, 4, 128)
            ngv[2 * core:2 * core + 2, l] = v[:, :, 512:640].reshape(2, 256, 2, 64)
    return (y_p, y_s, ndk, ndv, ngk, ngv)
```

```python
import math
from contextlib import ExitStack
import numpy as np
import concourse.bass as bass
import concourse.mybir as mybir
from concourse.bass_utils import run_bass_kernel_spmd

F32 = mybir.dt.float32
BF16 = mybir.dt.bfloat16
AF = mybir.ActivationFunctionType
ALU = mybir.AluOpType

D = 1024
L = 2
PAST = 256
GRID_W = 64
EPS = 1e-6
DFF = 2816
TS = 1024
TP = 512
SEM_EPOCH = 15000
DEBUG = False
DBG_OUT = {}
ENGS = ("pe", "act", "dve", "pool", "sp")

BLK = {}
_n = 0
for _name, _cnt in (("Q", 8), ("K", 6), ("UC", 4), ("V", 5), ("GATE", 24), ("BRA", 8), ("BRB", 8), ("BRC", 8),
                    ("OUT", 8), ("UP", 44), ("DOWN", 24), ("POOLW", 4), ("MOD", 48)):
    BLK[_name] = _n
    _n += _cnt
NBLK = _n

VC = {}
_n = 0
for _name, _cnt in (("norm1", 8), ("norm2", 8), ("bmod", 48), ("bgate", 24), ("convw", 132), ("convb", 44),
                    ("pscale", 4), ("qna", 1), ("kna", 1), ("qnb", 1), ("knb", 1), ("subln", 1), ("lam", 4)):
    VC[_name] = _n
    _n += _cnt
NVEC = _n


class Reg:
    __slots__ = ("name", "w", "r", "excl")

    def __init__(self, name="", excl=False):
        self.name = name
        self.excl = excl
        self.w = None
        self.r = {}

    def inherit(self, other):
        if other.w is not None:
            s_, i_ = other.w
            if self.r.get(s_, -1) < i_:
                self.r[s_] = i_
        for s_, i_ in other.r.items():
            if self.r.get(s_, -1) < i_:
                self.r[s_] = i_


class _Rec:
    def __init__(self):
        self.calls = []

    def __getattr__(self, name):
        def f(*a, **kw):
            self.calls.append((name, a, kw))
            return None
        return f


def _replay(calls):
    def fn(eng):
        ins = None
        for (name, a, kw) in calls:
            ins = getattr(eng, name)(*a, **kw)
        return ins
    return fn


class FW:
    def __init__(self, nc):
        self.nc = nc
        self.ops = {e: [] for e in ENGS}
        self.src_ops = {e: [] for e in ENGS}
        self.rr = {}

    def _deps(self, reads, writes):
        deps = set()
        for r in reads:
            if r.w is not None:
                deps.add(r.w)
            if r.excl:
                for x in r.r.items():
                    deps.add(x)
        for w in writes:
            if w.w is not None:
                deps.add(w.w)
            for x in w.r.items():
                deps.add(x)
        return deps

    def _commit(self, me, reads, writes):
        for r in reads:
            if r.r.get(me[0], -1) < me[1]:
                r.r[me[0]] = me[1]
        for w in writes:
            w.w = me
            w.r = {}

    def op(self, eng, fn, reads=(), writes=()):
        deps = self._deps(reads, writes)
        if eng == "pe":
            deps = {d for d in deps if d[0] != "pe"}
        idx = len(self.src_ops[eng])
        rc_ = _Rec()
        fn(rc_)
        fn = _replay(rc_.calls)
        rec = dict(eng=eng, fn=fn, deps=deps, src=eng, idx=idx, signal=False, dma=False)
        self.src_ops[eng].append(rec)
        self.ops[eng].append(rec)
        self._commit((eng, idx), reads, writes)

    NSUB = 8

    def dma(self, q, out, in_, reads=(), writes=(), stream="d0", sub=None):
        deps = self._deps(reads, writes)
        if sub is None:
            sub = self.rr.get(stream, 0) % self.NSUB
            self.rr[stream] = self.rr.get(stream, 0) + 1
        stream = f"{stream}_{sub}"
        if stream not in self.src_ops:
            self.src_ops[stream] = []
        if self.src_ops[stream]:
            deps.add((stream, len(self.src_ops[stream]) - 1))
        idx = len(self.src_ops[stream])
        rec = dict(eng=q, fn=lambda e: e.dma_start(out=out, in_=in_), deps=deps,
                   src=stream, idx=idx, signal=True, dma=True)
        self.src_ops[stream].append(rec)
        self.ops[q].append(rec)
        self._commit((stream, idx), reads, writes)

    def final_wait(self, eng, regs):
        deps = self._deps(regs, ())
        self.ops[eng].append(dict(eng=eng, fn=None, deps=deps, src=None, idx=None, signal=False, dma=False))

    def emit(self, es):
        nc = self.nc
        for e in ENGS:
            for rec in self.ops[e]:
                for (s, i) in rec["deps"]:
                    self.src_ops[s][i]["signal"] = True
        sems = {}
        for s, lst in self.src_ops.items():
            step = 16 if (lst and lst[0]["dma"]) else 1
            cnt = 0
            epoch = 0
            for rec in lst:
                if rec["signal"]:
                    if cnt + step > SEM_EPOCH:
                        epoch += 1
                        cnt = 0
                    cnt += step
                    rec["sig"] = (epoch, cnt)
                    if (s, epoch) not in sems:
                        sems[(s, epoch)] = es.enter_context(nc.semaphore(f"s_{s}_{epoch}"))
        block = es.enter_context(nc.Block())
        src_ops = self.src_ops

        def run(e):
            def body(eng):
                waited = {}
                for rec in self.ops[e]:
                    need = {}
                    for (s, i) in rec["deps"]:
                        ep, v = src_ops[s][i]["sig"]
                        k = (s, ep)
                        if waited.get(k, 0) >= v:
                            continue
                        if need.get(k, 0) < v:
                            need[k] = v
                    for k, v in need.items():
                        eng.wait_ge(sems[k], v)
                        waited[k] = v
                    if rec["fn"] is None:
                        continue
                    ins = rec["fn"](eng)
                    if rec["signal"]:
                        ins.then_inc(sems[(rec["src"], rec["sig"][0])], 16 if rec["dma"] else 1)
            return body

        block.tensor(run("pe"))
        block.scalar(run("act"))
        block.vector(run("dve"))
        block.gpsimd(run("pool"))
        block.sync(run("sp"))


def build_program(wseq=None, wseq_out=None):
    nc = bass.Bass("TRN2", target_bir_lowering=False)
    dt = lambda name, shape, kind: nc.dram_tensor(name, shape, F32, kind=kind).ap()
    XS = dt("xs", [8, 128, TS], "ExternalInput")
    XP = dt("xp", [8, 128, TP], "ExternalInput")
    W = dt("w", [L, NBLK, 128, 1024], "ExternalInput")
    VEC = dt("vec", [L, 128, NVEC], "ExternalInput")
    CT = dt("ct", [128, 16], "ExternalInput")
    CST = dt("cst", [128, 384], "ExternalInput")
    ROPE = dt("rope", [2, 128, TS], "ExternalInput")
    PCNT = dt("pcnt", [128, 128], "ExternalInput")
    MSK = dt("msk", [128, 16], "ExternalInput")
    xinK = [nc.dram_tensor(f"xinK{l}", [768, 1024], BF16) for l in range(L)]
    xoutK = [nc.dram_tensor(f"xoutK{l}", [1536, 1024], BF16) for l in range(L)]
    xinV = [nc.dram_tensor(f"xinV{l}", [1024, 672], BF16) for l in range(L)]
    xoutV = [nc.dram_tensor(f"xoutV{l}", [2048, 672], BF16) for l in range(L)]
    xin2 = [nc.dram_tensor(f"xin2_{l}", [128, 32], BF16) for l in range(L)]
    xout2 = [nc.dram_tensor(f"xout2_{l}", [256, 32], BF16) for l in range(L)]
    PAIRS = [[0, 1], [2, 3], [4, 5], [6, 7]]
    CK = dt("ck", [L, 6, 128, PAST], "ExternalInput")
    CV = dt("cv", [L, 2, 128, 640], "ExternalInput")
    YS = dt("ys", [8, 128, TS], "ExternalOutput")
    YP = dt("yp", [8, 128, TP], "ExternalOutput")
    KO = dt("ko", [L, 6, 128, TP], "ExternalOutput")
    VO = dt("vo", [L, TP, 640], "ExternalOutput")
    DBG = dt("dbg", [10, 8, 128, 512], "ExternalOutput") if DEBUG else None

    es = ExitStack()
    with es:
        fw = FW(nc)
        sbt = lambda n, s, d: es.enter_context(nc.sbuf_tensor(n, s, d))
        x = sbt("x", [128, 8, TS], F32)
        x_r = [[Reg() for _ in range(4)] for _ in range(8)]
        AR_KV, AR_UC, AR_BLK = 0, 13824, 13824 + 2112
        arena = sbt("arena", [128, 13824 + 2112 + 7168], F32)

        def carve(off_w, nbytes, dtype):
            a = arena[:, off_w:off_w + nbytes // 4]
            return a.bitcast(dtype) if dtype != F32 else a
        kt = carve(AR_KV, 6 * 2304 * 2, BF16).rearrange("p (c t) -> p c t", c=6)
        vt = carve(AR_KV + 6912, 18 * 768 * 2, BF16).rearrange("p (c t) -> p c t", c=18)
        ucb = carve(AR_UC, 4 * 1056 * 2, BF16).rearrange("p (c t) -> p c t", c=4)
        hT = carve(AR_BLK, 8192, BF16).rearrange("p (c t) -> p c t", c=8)
        qm = carve(AR_BLK + 2048, 8192, BF16).rearrange("p (c t) -> p c t", c=8)
        ao = carve(AR_BLK + 4096, 8192, BF16).rearrange("p (c t) -> p c t", c=8)
        ocb = carve(AR_BLK + 6144, 4096, BF16).rearrange("p (c t) -> p c t", c=4)
        hid = carve(AR_KV, 8 * 2048 * 2, BF16).rearrange("p (c t) -> p c t", c=8)
        stg = carve(AR_KV + 8192, 2 * 2056 * 4, F32).rearrange("p (c t) -> p c t", c=2)
        h2T = carve(AR_UC, 8 * 2048 * 2, BF16).rearrange("p (c t) -> p c t", c=8)
        NSCR = 8
        scr = [sbt(f"scr{i}", [128, 544], F32) for i in range(NSCR)]
        scr_r = [Reg() for _ in range(NSCR)]
        NRING = 12
        ring = [sbt(f"ring{i}", [128, 8, 128], BF16) for i in range(NRING)]
        ring_r = [Reg() for _ in range(NRING)]
        rope = sbt("rope_sb", [128, 2, TS], BF16)
        cst = sbt("cst_sb", [128, 384], BF16)
        ones_bf, onesblk_bf, perm_bf = cst[:, 0:128], cst[:, 128:256], cst[:, 256:384]
        ones_f = sbt("ones_f", [128, 128], F32)
        vec = sbt("vec_sb", [128, L, NVEC], F32)
        ct = sbt("ct_sb", [128, 16], F32)
        sc_bf = sbt("sc_bf", [128, 8, 2], BF16)
        modv = sbt("modv", [128, L, 48, 2], F32)
        dv = sbt("dv", [128, L, 2, 6, 8], F32)
        lamv = sbt("lamv", [128, L, 8], F32)
        pcnt = sbt("pcnt_sb", [128, 128], F32)
        msk = sbt("msk_sb", [128, 16], F32)
        r_msk = Reg()
        halo_u = sbt("halo_u", [128, 4, 2, 32], BF16)
        r_halo_u = Reg()
        uh = sbt("uh", [128, 4, 32], BF16)
        r_uh = Reg()
        h2s = sbt("h2s", [128, 32], BF16)
        r_h2s = Reg()
        h2g = sbt("h2g", [128, 2, 32], BF16)
        r_h2g = Reg()
        h2h = sbt("h2h", [128, 8, 2], BF16)
        r_h2h = Reg()
        stg_h = [Reg(), Reg()]
        pacc = [[sbt(f"pacc{i}{j}", [128, 512], F32) for j in range(2)] for i in range(2)]
        pacc_r = [[Reg() for j in range(2)] for i in range(2)]
        qo = sbt("qo", [128, 8, 512], BF16)
        qo_r = [Reg() for _ in range(8)]
        hid2 = sbt("hid2", [128, 8, 1024], BF16)
        hid2_r = [[Reg() for _ in range(4)] for _ in range(8)]
        dbuf = sbt("dbuf", [128, 512], BF16)
        r_dbuf = Reg()
        epsc = sbt("epsc", [128, 1], F32)

        R = lambda: Reg()
        r_rope, r_cst, r_onesf, r_vec, r_ct, r_sc, r_modv, r_dv, r_lam, r_pcnt, r_eps = (R() for _ in range(11))
        kt_r = [[Reg() for _ in range(5)] for _ in range(6)]
        vt_r = [Reg() for _ in range(18)]
        uc_r = [[Reg() for _ in range(4)] for _ in range(4)]
        hT_r, qm_r, ao_r, oc_r = ([Reg() for _ in range(8)] for _ in range(4))
        hid_r = [[Reg() for _ in range(4)] for _ in range(8)]
        h2_r = [[Reg() for _ in range(4)] for _ in range(8)]
        stg_r = [[Reg() for _ in range(4)] for _ in range(2)]
        ps = [es.enter_context(nc.psum_tensor(f"ps{i}", [128, 512], F32)) for i in range(8)]
        ps_r = [Reg(excl=True) for _ in range(8)]
        out_regs = []
        xin_regs = []
        xinK_regs = []

        state = dict(scr=0, ring=0, ps=0, issued=0)

        def tap(i, src, regs, nchunk=8, is_bf=True):
            if DBG is None or wseq_out is not None:
                return
            for c in range(nchunk):
                o = Reg()
                out_regs.append(o)
                fw.dma("pool" if is_bf else "sp", DBG[i, c, :, :], src[:, c, 0:512], reads=regs, writes=[o], stream="dbg")

        def new_scr():
            i = state["scr"] % NSCR
            state["scr"] += 1
            return scr[i], scr_r[i]

        def new_ps(pool=(0, 1, 2, 3, 4, 5, 6, 7)):
            i = pool[state["ps"] % len(pool)]
            state["ps"] += 1
            return ps[i], ps_r[i]

        PF = NRING - 3

        def _issue_w(n):
            l_, b_, kc_ = wseq[n]
            i = n % NRING
            src = W[l_, b_, :, 0:kc_ * 128].rearrange("p (k n) -> p k n", k=kc_)
            fw.dma("pool", ring[i][:, 0:kc_, :], src, writes=[ring_r[i]], stream="dw", sub=i)

        def wblk(l, b, kc=8):
            n = state["ring"]
            state["ring"] += 1
            if wseq_out is not None:
                wseq_out.append((l, b, kc))
                i = n % NRING
                return ring[i], ring_r[i]
            while state["issued"] < min(len(wseq), n + PF + 1):
                _issue_w(state["issued"])
                state["issued"] += 1
            i = n % NRING
            return ring[i], ring_r[i]

        fw.dma("pool", cst[:], CST[:, :], writes=[r_cst], stream="dc")
        fw.dma("pool", rope[:, 0, :], ROPE[0, :, :], writes=[r_rope], stream="dc")
        fw.dma("pool", rope[:, 1, :], ROPE[1, :, :], writes=[r_rope], stream="dc")
        for l in range(L):
            fw.dma("sp", vec[:, l, :], VEC[l, :, :], writes=[r_vec], stream="d0")
        fw.dma("sp", ct[:], CT[:, :], writes=[r_ct], stream="d0")
        fw.dma("sp", pcnt[:], PCNT[:, :], writes=[r_pcnt], stream="d0")
        fw.dma("sp", msk[:], MSK[:, :], writes=[r_msk], stream="d0")
        fw.op("dve", lambda e: e.memset(ones_f[:], 1.0), writes=[r_onesf])
        fw.op("dve", lambda e: e.memset(epsc[:], EPS), writes=[r_eps])
        fw.op("pool", lambda e: e.memset(qo[:], 0.0), writes=qo_r)
        fw.op("dve", lambda e: e.memset(uh[:], 0.0), writes=[r_uh])
        fw.op("dve", lambda e: e.memset(h2s[:], 0.0), writes=[r_h2s])
        fw.op("act", lambda e: e.activation(sc_bf[:, :, 0], ct[:, 0:8], AF.Silu), reads=[r_ct], writes=[r_sc])
        fw.op("act", lambda e: e.activation(sc_bf[:, :, 1], ct[:, 8:16], AF.Silu), reads=[r_ct], writes=[r_sc])

        def vcol(l, name, i=0):
            c = VC[name] + i
            return vec[:, l, c:c + 1]

        def mod_block(l, j, pool=None):
            wt, wr = wblk(l, BLK["MOD"] + j)
            pt, pr = new_ps(pool) if pool is not None else new_ps()

            def mm(e):
                for k in range(8):
                    ins = e.matmul(pt[:, 0:2], lhsT=wt[:, k, :], rhs=sc_bf[:, k, :], start=(k == 0), stop=(k == 7))
                return ins
            fw.op("pe", mm, reads=[wr, r_sc], writes=[pr])
            fw.op("dve", lambda e: e.tensor_scalar(modv[:, l, j, :], pt[:, 0:2], vcol(l, "bmod", j), None, ALU.add),
                  reads=[pr, r_vec], writes=[r_modv])

        def mod_der(l, parts=(0, 1, 2, 3, 4, 5)):
            for p in range(2):
                def der(e):
                    n1 = vec[:, l, VC["norm1"]:VC["norm1"] + 8]
                    n2 = vec[:, l, VC["norm2"]:VC["norm2"] + 8]
                    ins = None
                    if 0 in parts:
                        ins = e.scalar_tensor_tensor(dv[:, l, p, 0, :], modv[:, l, 8:16, p], 1.0, n1, ALU.add, ALU.mult)
                    if 1 in parts:
                        ins = e.tensor_copy(dv[:, l, p, 1, :], modv[:, l, 0:8, p])
                    if 2 in parts:
                        ins = e.tensor_copy(dv[:, l, p, 2, :], modv[:, l, 16:24, p])
                    if 3 in parts:
                        ins = e.scalar_tensor_tensor(dv[:, l, p, 3, :], modv[:, l, 32:40, p], 1.0, n2, ALU.add, ALU.mult)
                    if 4 in parts:
                        ins = e.tensor_copy(dv[:, l, p, 4, :], modv[:, l, 24:32, p])
                    if 5 in parts:
                        ins = e.tensor_copy(dv[:, l, p, 5, :], modv[:, l, 40:48, p])
                    return ins
                fw.op("dve", der, reads=[r_modv, r_vec], writes=[r_dv])

        for j in range(16):
            mod_block(0, j)
        mod_der(0, (0, 1))
        pending_mod = [(0, j) for j in range(16, 48)] + [(1, j) for j in range(48)]

        def mod_pop(pool):
            l_, j_ = pending_mod.pop(0)
            mod_block(l_, j_, pool)
            if (l_, j_) == (0, 47):
                mod_der(0, (2, 3, 4, 5))

        def mod_step(pool):
            n_ = 2 if (pending_mod and pending_mod[0][0] == 0) else 1
            for _ in range(n_):
                if pending_mod:
                    mod_pop(pool)

        def mod_flush():
            if pending_mod or not state.get("der1"):
                while pending_mod:
                    mod_pop(None)
                mod_der(1)
                state["der1"] = True

        for l in range(L):
            lam_init = 0.8 - 0.6 * math.exp(-0.3 * l)
            c0 = VC["lam"]
            fw.op("dve", lambda e, l=l, c0=c0: e.tensor_tensor(lamv[:, l, 2:3], vec[:, l, c0:c0 + 1], vec[:, l, c0 + 1:c0 + 2], ALU.mult),
                  reads=[r_vec], writes=[r_lam])
            fw.op("dve", lambda e, l=l, c0=c0: e.tensor_tensor(lamv[:, l, 3:4], vec[:, l, c0 + 2:c0 + 3], vec[:, l, c0 + 3:c0 + 4], ALU.mult),
                  reads=[r_vec, r_lam], writes=[r_lam])
            pt, pr = new_ps()
            fw.op("pe", lambda e, pt=pt, l=l: e.matmul(pt[:, 0:2], lhsT=ones_f[:], rhs=lamv[:, l, 2:4], start=True, stop=True),
                  reads=[r_lam, r_onesf], writes=[pr])
            fw.op("act", lambda e, pt=pt, l=l: e.activation(lamv[:, l, 4:6], pt[:, 0:2], AF.Exp), reads=[pr, r_lam], writes=[r_lam])
            fw.op("dve", lambda e, l=l, li=lam_init: e.scalar_tensor_tensor(lamv[:, l, 0:1], lamv[:, l, 5:6], -li, lamv[:, l, 4:5], ALU.add, ALU.subtract),
                  reads=[r_lam], writes=[r_lam])
            fw.op("dve", lambda e, l=l, li=lam_init: e.tensor_scalar(lamv[:, l, 1:2], vcol(l, "subln"), 1.0 - li, None, ALU.mult),
                  reads=[r_vec, r_lam], writes=[r_lam])

        def rms_norm_block(l, p, tb, which, dst, dst_r, xcol0):
            cs = slice(xcol0, xcol0 + 512)
            pt, pr = new_ps()
            sqs = []
            for c in range(8):
                st, sr = new_scr()
                sq = st[:, 0:256].bitcast(BF16)
                fw.op("act", lambda e, sq=sq, c=c: e.activation(sq, x[:, c, cs], AF.Square), reads=[x_r[c][tb]], writes=[sr])
                fw.op("pe", lambda e, sq=sq, c=c, pt=pt: e.matmul(pt[:], lhsT=ones_bf, rhs=sq, start=(c == 0), stop=(c == 7)),
                      reads=[sr, r_cst], writes=[pr])
            rt, rr = new_scr()
            fw.op("act", lambda e, rt=rt, pt=pt: e.activation(rt[:, 0:512], pt[:], AF.Ln, bias=epsc[:, 0:1], scale=1.0 / D),
                  reads=[pr, r_eps], writes=[rr])
            fw.op("act", lambda e, rt=rt: e.activation(rt[:, 0:512], rt[:, 0:512], AF.Exp, scale=-0.5), reads=[rr], writes=[rr])
            for c in range(8):
                tt, tr = new_scr()
                sc = dv[:, l, p, 3 * which + 0, c:c + 1]
                sh = dv[:, l, p, 3 * which + 1, c:c + 1]
                fw.op("dve", lambda e, tt=tt, c=c, sc=sc, rt=rt: e.scalar_tensor_tensor(tt[:, 0:512], x[:, c, cs], sc, rt[:, 0:512], ALU.mult, ALU.mult),
                      reads=[x_r[c][tb], rr, r_dv], writes=[tr])
                fw.op("act", lambda e, tt=tt, c=c, sh=sh: e.activation(dst(c), tt[:, 0:512], AF.Identity, bias=sh),
                      reads=[tr, r_dv], writes=[dst_r(c)])

        def proj_fm(wt, wr, rhs_fn, rhs_regs, kc=8, n=512):
            pt, pr = new_ps()

            def mm(e):
                for k in range(kc):
                    ins = e.matmul(pt[:, 0:n], lhsT=wt[:, k, :], rhs=rhs_fn(k), start=(k == 0), stop=(k == kc - 1))
                return ins
            fw.op("pe", mm, reads=[wr] + list(rhs_regs), writes=[pr])
            return pt, pr

        def qk_post(l, pt, pr, gname, dst_ap, dst_regs, rope_cols=None, kout=None, dst_hi=None, dst_hi_regs=None):
            st, sr = new_scr()
            sq = st[:, 0:256].bitcast(BF16)
            fw.op("act", lambda e: e.activation(sq, pt[:], AF.Square), reads=[pr], writes=[sr])
            p2, p2r = new_ps()
            fw.op("pe", lambda e: e.matmul(p2[:], lhsT=onesblk_bf, rhs=sq, start=True, stop=True), reads=[sr, r_cst], writes=[p2r])
            rt, rr = new_scr()
            fw.op("act", lambda e: e.activation(rt[:, 0:512], p2[:], AF.Ln, bias=epsc[:, 0:1], scale=1.0 / 64), reads=[p2r, r_eps], writes=[rr])
            fw.op("act", lambda e: e.activation(rt[:, 0:512], rt[:, 0:512], AF.Exp, scale=-0.5), reads=[rr], writes=[rr])
            g = vcol(l, gname)
            if kout is not None:
                kst, kstr = new_scr()
                fw.op("dve", lambda e: e.scalar_tensor_tensor(kst[:, 0:512], pt[:], g, rt[:, 0:512], ALU.mult, ALU.mult),
                      reads=[pr, rr, r_vec], writes=[kstr])
                o = Reg()
                out_regs.append(o)
                fw.dma("sp", kout, kst[:, 0:512], reads=[kstr], writes=[o], stream="do")
            if rope_cols is None:
                if dst_hi is None:
                    fw.op("dve", lambda e: e.scalar_tensor_tensor(dst_ap, pt[:], g, rt[:, 0:512], ALU.mult, ALU.mult),
                          reads=[pr, rr, r_vec], writes=dst_regs)
                else:
                    fw.op("dve", lambda e: e.scalar_tensor_tensor(dst_ap[0:64], pt[0:64, :], g[0:64], rt[0:64, 0:512], ALU.mult, ALU.mult),
                          reads=[pr, rr, r_vec], writes=dst_regs)
                    fw.op("dve", lambda e: e.scalar_tensor_tensor(dst_hi[64:128], pt[64:128, :], g[64:128], rt[64:128, 0:512], ALU.mult, ALU.mult),
                          reads=[pr, rr, r_vec], writes=dst_hi_regs)
                return
            qt, qr = new_scr()
            qn = qt[:, 0:256].bitcast(BF16)
            fw.op("dve", lambda e: e.scalar_tensor_tensor(qn, pt[:], g, rt[:, 0:512], ALU.mult, ALU.mult),
                  reads=[pr, rr, r_vec], writes=[qr])
            p3, p3r = new_ps()
            fw.op("pe", lambda e: e.matmul(p3[:], lhsT=perm_bf, rhs=qn, start=True, stop=True), reads=[qr, r_cst], writes=[p3r])
            t1, t1r = new_scr()
            fw.op("pool", lambda e: e.tensor_tensor(t1[:, 0:512], qn, rope[:, 0, rope_cols], ALU.mult), reads=[qr, r_rope], writes=[t1r])
            t2, t2r = new_scr()
            fw.op("dve", lambda e: e.tensor_tensor(t2[:, 0:512], p3[:], rope[:, 1, rope_cols], ALU.mult), reads=[p3r, r_rope], writes=[t2r])
            if dst_hi is None:
                fw.op("pool", lambda e: e.tensor_tensor(dst_ap, t1[:, 0:512], t2[:, 0:512], ALU.add), reads=[t1r, t2r], writes=dst_regs)
            else:
                fw.op("dve", lambda e: e.tensor_tensor(dst_ap[0:64], t1[0:64, 0:512], t2[0:64, 0:512], ALU.add), reads=[t1r, t2r], writes=dst_regs)
                fw.op("dve", lambda e: e.tensor_tensor(dst_hi[64:128], t1[64:128, 0:512], t2[64:128, 0:512], ALU.add), reads=[t1r, t2r], writes=dst_hi_regs)

        def run_pass(p):
            T = TS if p == 0 else TP
            NB = T // 512
            sample = (p == 0)
            XIN, YOUT = (XS, YS) if sample else (XP, YP)
            if sample:
                seqs = [dict(t0=0, S=TS, past=PAST, kcol0=0, vch0=0, ucol0=0, nk=PAST + 2048)]
            else:
                seqs = [dict(t0=256 * i, S=256, past=0, kcol0=256 * i, vch0=2 * i, ucol0=272 * i, nk=256) for i in range(2)]
            for c in range(8):
                for tb in range(NB):
                    fw.dma("sp", x[:, c, tb * 512:(tb + 1) * 512], XIN[c, :, tb * 512:(tb + 1) * 512], writes=[x_r[c][tb]], stream="dx")

            def kreg(c, col):
                if sample:
                    return kt_r[c][0] if col < PAST else kt_r[c][1 + (col - PAST) // 1024]
                return kt_r[c][0]

            for l in range(L):
                if l == 1:
                    mod_flush()
                if sample:
                    for c in range(6):
                        fw.dma("pool", kt[:, c, 0:PAST], CK[l, c, :, :], writes=[kt_r[c][0]], stream="dc")
                    for j in range(2):
                        fw.dma("pool", vt[:, j, 0:576], CV[l, j, :, 0:576], writes=[vt_r[j]], stream="dc")
                        fw.dma("pool", vt[:, j, 640:704], CV[l, j, :, 576:640], writes=[vt_r[j]], stream="dc")
                fw.op("pool", lambda e: e.memset(vt[:, :, 576:640], 1.0), writes=vt_r)
                fw.op("pool", lambda e: e.memset(vt[:, :, 704:768], 1.0), writes=vt_r)
                fw.op("pool", lambda e: e.memset(ucb[:, :, :], 0.0), writes=[rr for row in uc_r for rr in row])

                for tb in range(NB):
                    rms_norm_block(l, p, tb, 0, lambda c: hT[:, c, :], lambda c: hT_r[c], tb * 512)
                    def k_post(c, pt, pr):
                        if sample:
                            kx_, kxr_ = new_scr()
                            kxb = kx_[:, 0:256].bitcast(BF16)
                            qk_post(l, pt, pr, "kna" if c < 4 else "knb", kxb, [kxr_],
                                    rope_cols=slice(tb * 512, tb * 512 + 512))
                            o = Reg()
                            xinK_regs.append(o)
                            fw.dma("sp", xinK[l].ap()[c * 128:(c + 1) * 128, tb * 512:(tb + 1) * 512], kxb, reads=[kxr_], writes=[o], stream="dxo")
                        else:
                            qk_post(l, pt, pr, "kna" if c < 4 else "knb", kt[:, c, 0:512], [kt_r[c][0]], kout=KO[l, c, :, :])
                    prev_ = None
                    for c in range(6):
                        wt, wr = wblk(l, BLK["K"] + c)
                        pt, pr = proj_fm(wt, wr, lambda k: hT[:, k, :], hT_r)
                        if prev_ is not None:
                            k_post(*prev_)
                        prev_ = (c, pt, pr)
                    k_post(*prev_)
                    for g in range(4):
                        wt, wr = wblk(l, BLK["UC"] + g)
                        pt, pr = proj_fm(wt, wr, lambda k: hT[:, k, :], hT_r)
                        if sample:
                            fw.op("act", lambda e, pt=pt, g=g, tb=tb: e.copy(ucb[:, g, 8 + tb * 512:8 + tb * 512 + 512], pt[:]),
                                  reads=[pr], writes=[uc_r[g][tb]])
                        else:
                            for i in range(2):
                                fw.op("act", lambda e, pt=pt, g=g, i=i: e.copy(ucb[:, g, 272 * i + 8:272 * i + 264], pt[:, 256 * i:256 * i + 256]),
                                      reads=[pr], writes=[uc_r[g][0]])
                    for j in range(5):
                        wt, wr = wblk(l, BLK["V"] + j)
                        for tt in range(4):
                            tok0 = tb * 512 + tt * 128
                            vch = tok0 // 128
                            pt, pr = new_ps()

                            def mmv(e, wt=wt, pt=pt, tt=tt):
                                for k in range(8):
                                    ins = e.matmul(pt[:, 0:128], lhsT=hT[:, k, tt * 128:(tt + 1) * 128], rhs=wt[:, k, :], start=(k == 0), stop=(k == 7))
                                return ins
                            fw.op("pe", mmv, reads=[wr] + hT_r, writes=[pr])
                            if j < 4:
                                dsts = [vt[:, vch, j * 128:(j + 1) * 128]]
                                srcs = [pt[:, 0:128]]
                            else:
                                dsts = [vt[:, vch, 512:576], vt[:, vch, 640:704]]
                                srcs = [pt[:, 0:64], pt[:, 64:128]]
                            if sample:
                                vx_, vxr_ = new_scr()
                                vxb = vx_[:, 0:64].bitcast(BF16)
                                fw.op("act", lambda e: e.copy(vxb, pt[:, 0:128]), reads=[pr], writes=[vxr_])
                                o = Reg()
                                xin_regs.append(o)
                                fw.dma("sp", xinV[l].ap()[tok0:tok0 + 128, j * 128:(j + 1) * 128], vxb, reads=[vxr_], writes=[o], stream="dxo")
                            else:
                                for d_, s_ in zip(dsts, srcs):
                                    fw.op("act", lambda e, d_=d_, s_=s_: e.copy(d_, s_), reads=[pr], writes=[vt_r[vch]])
                            if not sample:
                                vs_, vsr_ = new_scr()
                                fw.op("dve", lambda e, pt=pt, vs_=vs_: e.tensor_copy(vs_[:, 0:128], pt[:, 0:128]),
                                      reads=[pr], writes=[vsr_])
                                o = Reg()
                                out_regs.append(o)
                                fw.dma("sp", VO[l, tok0:tok0 + 128, j * 128:(j + 1) * 128], vs_[:, 0:128], reads=[vsr_], writes=[o], stream="do")

                if sample:
                    for g in range(4):
                        fw.op("dve", lambda e, g=g: e.tensor_copy(uh[:, g, 0:8], ucb[:, g, 8:16]), reads=uc_r[g], writes=[r_uh])
                        fw.op("dve", lambda e, g=g: e.tensor_copy(uh[:, g, 8:16], ucb[:, g, TS:TS + 8]), reads=uc_r[g], writes=[r_uh])
                    for g in range(4):
                        o = Reg()
                        xin_regs.append(o)
                        fw.dma("sp", xinV[l].ap()[g * 128:(g + 1) * 128, 640:672], uh[:, g, :], reads=[r_uh], writes=[o], stream="dxo")
                    r_xoutK, r_xout = Reg(), Reg()
                    fw.op("pool", lambda e: e.collective_compute("AllGather", ALU.bypass, replica_groups=PAIRS,
                                                                 ins=[xinK[l].ap()], outs=[xoutK[l].ap()]),
                          reads=list(xinK_regs), writes=[r_xoutK])
                    fw.op("pool", lambda e: e.collective_compute("AllGather", ALU.bypass, replica_groups=PAIRS,
                                                                 ins=[xinV[l].ap()], outs=[xoutV[l].ap()]),
                          reads=list(xin_regs), writes=[r_xout])
                    del xin_regs[:]
                    del xinK_regs[:]
                    for r_ in range(2):
                        for c in range(6):
                            fw.dma("sp", kt[:, c, PAST + r_ * 1024:PAST + (r_ + 1) * 1024], xoutK[l].ap()[r_ * 768 + c * 128:r_ * 768 + (c + 1) * 128, :],
                                   reads=[r_xoutK], writes=[kt_r[c][1 + r_]], stream="dxi")
                        base = r_ * 1024
                        for j in range(8):
                            ch = 2 + r_ * 8 + j
                            fw.dma("sp", vt[:, ch, 0:576], xoutV[l].ap()[base + j * 128:base + (j + 1) * 128, 0:576],
                                   reads=[r_xout], writes=[vt_r[ch]], stream="dxi")
                            fw.dma("sp", vt[:, ch, 640:704], xoutV[l].ap()[base + j * 128:base + (j + 1) * 128, 576:640],
                                   reads=[r_xout], writes=[vt_r[ch]], stream="dxi")
                    for g in range(4):
                        for r_ in range(2):
                            fw.dma("sp", halo_u[:, g, r_, :], xoutV[l].ap()[r_ * 1024 + g * 128:r_ * 1024 + (g + 1) * 128, 640:672],
                                   reads=[r_xout], writes=[r_halo_u], stream="dxi")
                    for g in range(4):
                        fw.op("pool", lambda e, g=g: e.tensor_scalar(ucb[:, g, 0:8], halo_u[:, g, 0, 8:16], msk[:, 0:1], None, ALU.mult),
                              reads=[r_halo_u, r_msk], writes=uc_r[g])
                        fw.op("pool", lambda e, g=g: e.tensor_scalar(ucb[:, g, 8 + TS:16 + TS], halo_u[:, g, 1, 0:8], msk[:, 1:2], None, ALU.mult),
                              reads=[r_halo_u, r_msk], writes=uc_r[g])

                for tb in range(NB):
                    c0 = tb * 512
                    rms_norm_block(l, p, tb, 0, lambda c: hT[:, c, :], lambda c: hT_r[c], c0)
                    fw.op("dve", lambda e: e.memset(qm[64:128, :, :], 0.0), writes=qm_r)
                    prev_ = None
                    for c in range(8):
                        wt, wr = wblk(l, BLK["Q"] + c)
                        pt, pr = proj_fm(wt, wr, lambda k: hT[:, k, :], hT_r)
                        if prev_ is not None:
                            qk_post(l, prev_[1], prev_[2], "qna" if prev_[0] < 4 else "qnb", qm[:, prev_[0], :], [qm_r[prev_[0]]],
                                    rope_cols=slice(c0, c0 + 512) if sample else None, dst_hi=qo[:, prev_[0], :], dst_hi_regs=[qo_r[prev_[0]]])
                        prev_ = (c, pt, pr)
                    qk_post(l, prev_[1], prev_[2], "qna" if prev_[0] < 4 else "qnb", qm[:, prev_[0], :], [qm_r[prev_[0]]],
                            rope_cols=slice(c0, c0 + 512) if sample else None, dst_hi=qo[:, prev_[0], :], dst_hi_regs=[qo_r[prev_[0]]])
                    if not sample and l == 0:
                        tap(0, hT, hT_r)
                        tap(1, qm, qm_r)
                    if sample:
                        units = [(seqs[0], 0, 512)]
                    else:
                        units = [(seqs[0], 0, 256), (seqs[1], 256, 256)]
                    LAG = 2
                    order = [("d", 0), ("g", 0), ("g", 1), ("d", 1), ("g", 2), ("g", 3), ("d", 2), ("g", 4), ("g", 5), ("d", 3), ("g", 6), ("g", 7)]
                    for (sq_, qc0, NQ) in units:
                        nkc = sq_["nk"] // 128
                        qs = slice(qc0, qc0 + NQ)
                        steps = []
                        for (kind, h) in order:
                            for kc in range(nkc):
                                for a_ in range(2 if kind == "d" else 1):
                                    steps.append((kind, h, kc, a_))
                        pend = []

                        def rec_qk(kind, h, kc, a_):
                            ksl = slice(sq_["kcol0"] + kc * 128, sq_["kcol0"] + (kc + 1) * 128)
                            if kind == "d":
                                hp = (h % 2) * 64
                                kch = a_ * 2 + h // 2
                                qch = kch
                            else:
                                hp = (h % 2) * 64
                                kch = 4 + h // 4
                                qch = 4 + h // 2
                            sc_t, sc_r = new_ps((0, 1, 2))
                            qsrc, qsrc_r = (qm, qm_r) if hp == 0 else (qo, qo_r)
                            fw.op("pe", lambda e: e.matmul(sc_t[:, 0:NQ], lhsT=kt[:, kch, ksl], rhs=qsrc[:, qch, qs], start=True, stop=True),
                                  reads=[kreg(kch, kc * 128), qsrc_r[qch]], writes=[sc_r])
                            pt_, ptr_ = new_scr()
                            pT = pt_[:, 0:256].bitcast(BF16)
                            fw.op("act", lambda e: e.activation(pT[:, 0:NQ], sc_t[:, 0:NQ], AF.Exp, scale=0.125), reads=[sc_r], writes=[ptr_])
                            return pT, ptr_

                        def rec_pv(kind, h, kc, a_, pT, ptr_):
                            vch = sq_["vch0"] + kc
                            first, last = (kc == 0), (kc == nkc - 1)
                            if kind == "d":
                                bset = h % 2
                                o_t, o_r = ps[4 + 2 * bset + a_], ps_r[4 + 2 * bset + a_]
                                fw.op("pe", lambda e: e.matmul(o_t[:, 0:NQ], lhsT=vt[:, vch, h * 128:(h + 1) * 128], rhs=pT[:, 0:NQ], start=first, stop=last),
                                      reads=[ptr_, vt_r[vch]], writes=[o_r])
                                pa, par = pacc[bset][a_], pacc_r[bset][a_]
                                eng_ = "dve"
                                if first:
                                    fw.op(eng_, lambda e: e.tensor_copy(pa[:, 0:NQ], pT[:, 0:NQ]), reads=[ptr_], writes=[par])
                                else:
                                    fw.op(eng_, lambda e: e.tensor_tensor(pa[:, 0:NQ], pa[:, 0:NQ], pT[:, 0:NQ], ALU.add), reads=[ptr_, par], writes=[par])
                                if last and a_ == 1:
                                    diff_epilogue(h)
                            else:
                                g = h // 4
                                o_t, o_r = ps[3], ps_r[3]
                                fw.op("pe", lambda e: e.matmul(o_t[:, 0:NQ], lhsT=vt[:, vch, 512 + g * 128:640 + g * 128], rhs=pT[:, 0:NQ], start=first, stop=last),
                                      reads=[ptr_, vt_r[vch]], writes=[o_r])
                                if last:
                                    jp = (h % 2) * 64
                                    qc = 4 + h // 2
                                    rc, rcr = new_scr()
                                    fw.op("act", lambda e: e.activation(rc[64:128, 0:NQ], o_t[64:128, 0:NQ], AF.Ln), reads=[o_r], writes=[rcr])
                                    fw.op("act", lambda e: e.activation(rc[64:128, 0:NQ], rc[64:128, 0:NQ], AF.Exp, scale=-1.0), reads=[rcr], writes=[rcr])
                                    fw.op("dve", lambda e: e.tensor_tensor(ao[jp:jp + 64, qc, qs], o_t[0:64, 0:NQ], rc[64:128, 0:NQ], ALU.mult),
                                          reads=[o_r, rcr], writes=[ao_r[qc]])

                        def diff_epilogue(h):
                            bset = h % 2
                            (o1, o1r), (o2, o2r) = [(ps[i], ps_r[i]) for i in (4 + 2 * bset, 5 + 2 * bset)]
                            r1, r1r = new_scr()
                            r2, r2r = new_scr()
                            for a_, (rx, rxr) in enumerate(((r1, r1r), (r2, r2r))):
                                d_t, d_r = new_ps((0, 1, 2))
                                fw.op("pe", lambda e: e.matmul(d_t[:, 0:NQ], lhsT=ones_f[:], rhs=pacc[bset][a_][:, 0:NQ], start=True, stop=True),
                                      reads=[pacc_r[bset][a_], r_onesf], writes=[d_r])
                                fw.op("act", lambda e: e.activation(rx[:, 0:NQ], d_t[:, 0:NQ], AF.Ln), reads=[d_r], writes=[rxr])
                                fw.op("act", lambda e: e.activation(rx[:, 0:NQ], rx[:, 0:NQ], AF.Exp, scale=-1.0), reads=[rxr], writes=[rxr])
                            fw.op("dve", lambda e: e.tensor_tensor(r1[:, 0:NQ], o1[:, 0:NQ], r1[:, 0:NQ], ALU.mult), reads=[o1r, r1r], writes=[r1r])
                            fw.op("dve", lambda e: e.tensor_tensor(r2[:, 0:NQ], o2[:, 0:NQ], r2[:, 0:NQ], ALU.mult), reads=[o2r, r2r], writes=[r2r])
                            fw.op("dve", lambda e: e.scalar_tensor_tensor(r1[:, 0:NQ], r2[:, 0:NQ], lamv[:, l, 0:1], r1[:, 0:NQ], ALU.mult, ALU.add),
                                  reads=[r2r, r_lam], writes=[r1r])
                            s_t, s_r = new_scr()
                            sqb = s_t[:, 0:256].bitcast(BF16)
                            fw.op("act", lambda e: e.activation(sqb[:, 0:NQ], r1[:, 0:NQ], AF.Square), reads=[r1r], writes=[s_r])
                            pss, pssr = new_ps((0, 1, 2))
                            fw.op("pe", lambda e: e.matmul(pss[:, 0:NQ], lhsT=ones_bf, rhs=sqb[:, 0:NQ], start=True, stop=True),
                                  reads=[s_r, r_cst], writes=[pssr])
                            fw.op("act", lambda e: e.activation(r2[:, 0:NQ], pss[:, 0:NQ], AF.Ln, bias=epsc[:, 0:1], scale=1.0 / 128),
                                  reads=[pssr, r_eps], writes=[r2r])
                            fw.op("act", lambda e: e.activation(r2[:, 0:NQ], r2[:, 0:NQ], AF.Exp, scale=-0.5), reads=[r2r], writes=[r2r])
                            fw.op("dve", lambda e: e.scalar_tensor_tensor(ao[:, h, qs], r1[:, 0:NQ], lamv[:, l, 1:2], r2[:, 0:NQ], ALU.mult, ALU.mult),
                                  reads=[r1r, r2r, r_lam], writes=[ao_r[h]])

                        for i in range(len(steps) + LAG):
                            if i < len(steps):
                                pend.append(rec_qk(*steps[i]))
                            if i >= LAG:
                                pT, ptr_ = pend.pop(0)
                                rec_pv(*steps[i - LAG], pT, ptr_)
                            if (not sample) and l == 0:
                                mod_step((0, 1, 2))
                    PO = 64 if sample else 0
                    for g in range(4):
                        w = 2 << g
                        if sample:
                            segs = [(seqs[0]["ucol0"] + c0, 512, 0, c0 == 0, c0 + 512 == TS)]
                        else:
                            segs = [(272 * i, 256, 256 * i, True, True) for i in range(2)]
                        dbf, dr_ = dbuf[:, :], r_dbuf
                        for (uc0, nt, oc0, is_s, is_e) in segs:
                            cur, cur_r = new_scr()
                            fw.op("pool", lambda e, cur=cur, uc0=uc0, nt=nt, g=g: e.tensor_copy(cur[:, 0:nt + 16], ucb[:, g, uc0:uc0 + nt + 16]),
                                  reads=uc_r[g], writes=[cur_r])
                            u_t, u_r = cur, cur_r
                            lo, hi = 0, nt + 16
                            step = 1
                            first = True
                            ww = 2
                            while ww <= w:
                                nxt, nxt_r = new_scr()
                                if first:
                                    fw.op("pool", lambda e, nxt=nxt, cur=cur, hi=hi: e.tensor_tensor(nxt[:, 1:hi], cur[:, 0:hi - 1], cur[:, 1:hi], ALU.add),
                                          reads=[cur_r], writes=[nxt_r])
                                    lo, hi = 1, hi
                                    first = False
                                else:
                                    sft = ww // 4
                                    fw.op("pool", lambda e, nxt=nxt, cur=cur, lo=lo, hi=hi, sft=sft: e.tensor_tensor(
                                        nxt[:, lo + sft:hi - sft], cur[:, lo:hi - 2 * sft], cur[:, lo + 2 * sft:hi], ALU.add),
                                        reads=[cur_r], writes=[nxt_r])
                                    lo, hi = lo + sft, hi - sft
                                cur, cur_r = nxt, nxt_r
                                ww *= 2
                            fw.op("dve", lambda e, cur=cur, u_t=u_t, nt=nt, oc0=oc0, w=w: e.scalar_tensor_tensor(
                                dbf[:, oc0:oc0 + nt], cur[:, 8:8 + nt], 1.0 / w, u_t[:, 8:8 + nt], ALU.mult, ALU.subtract),
                                reads=[cur_r, u_r], writes=[dr_])
                            hw_ = w // 2
                            if is_s:
                                fw.op("pool", lambda e, cur=cur, hw_=hw_, g=g: e.tensor_tensor(cur[:, 8:8 + hw_], cur[:, 8:8 + hw_], pcnt[:, PO + g * 16:PO + g * 16 + hw_], ALU.mult),
                                      reads=[cur_r, r_pcnt], writes=[cur_r])
                                fw.op("pool", lambda e, cur=cur, u_t=u_t, hw_=hw_, oc0=oc0: e.tensor_tensor(dbf[:, oc0:oc0 + hw_], cur[:, 8:8 + hw_], u_t[:, 8:8 + hw_], ALU.subtract),
                                      reads=[cur_r, u_r], writes=[dr_])
                            if is_e and hw_ > 1:
                                ne = hw_ - 1
                                a0 = 8 + nt - ne
                                fw.op("pool", lambda e, cur=cur, ne=ne, a0=a0, g=g: e.tensor_tensor(cur[:, a0:a0 + ne], cur[:, a0:a0 + ne], pcnt[:, PO + g * 16 + 8:PO + g * 16 + 8 + ne], ALU.mult),
                                      reads=[cur_r, r_pcnt], writes=[cur_r])
                                fw.op("pool", lambda e, cur=cur, u_t=u_t, ne=ne, a0=a0, oc0=oc0, nt=nt: e.tensor_tensor(
                                    dbf[:, oc0 + nt - ne:oc0 + nt], cur[:, a0:a0 + ne], u_t[:, a0:a0 + ne], ALU.subtract),
                                    reads=[cur_r, u_r], writes=[dr_])
                        wt, wr = wblk(l, BLK["POOLW"] + g, kc=1)
                        pt, pr = proj_fm(wt, wr, lambda k, dbf=dbf: dbf[:, 0:512], [dr_], kc=1)
                        fw.op("act", lambda e, pt=pt, g=g: e.activation(ocb[:, g, :], pt[:], AF.Copy, scale=vcol(l, "pscale", g)),
                              reads=[pr, r_vec], writes=[oc_r[g]])
                    if not sample and l == 0:
                        tap(2, ao, ao_r)
                        tap(3, ocb, oc_r, nchunk=4)
                    for m in range(8):
                        gts = []
                        for br in range(3):
                            wt, wr = wblk(l, BLK["GATE"] + br * 8 + m)
                            pt, pr = proj_fm(wt, wr, lambda k: hT[:, k, :], hT_r)
                            gt, gr = new_scr()
                            fw.op("act", lambda e, gt=gt, pt=pt, br=br, m=m: e.activation(gt[:, 0:512], pt[:], AF.Sigmoid, bias=vcol(l, "bgate", br * 8 + m)),
                                  reads=[pr, r_vec], writes=[gr])
                            srcT, srcR = [(ao, ao_r), (ao, ao_r), (ocb, oc_r)][br]
                            off = [0, 4, 0][br]
                            wt2, wr2 = wblk(l, BLK[("BRA", "BRB", "BRC")[br]] + m, kc=4)
                            pt2, pr2 = proj_fm(wt2, wr2, lambda k, srcT=srcT, off=off: srcT[:, off + k, :], srcR[off:off + 4], kc=4)
                            fw.op("dve", lambda e, gt=gt, pt2=pt2: e.tensor_tensor(gt[:, 0:512], gt[:, 0:512], pt2[:], ALU.mult), reads=[pr2, gr], writes=[gr])
                            gts.append((gt, gr))
                        (g0, g0r), (g1, g1r), (g2, g2r) = gts
                        fw.op("pool", lambda e, g0=g0, g1=g1: e.tensor_tensor(g0[:, 0:512], g0[:, 0:512], g1[:, 0:512], ALU.add), reads=[g1r, g0r], writes=[g0r])
                        fw.op("pool", lambda e, g0=g0, g2=g2, m=m: e.tensor_tensor(qm[:, m, :], g0[:, 0:512], g2[:, 0:512], ALU.add), reads=[g0r, g2r], writes=[qm_r[m]])
                    if not sample and l == 0:
                        tap(4, qm, qm_r)
                    for m in range(8):
                        wt, wr = wblk(l, BLK["OUT"] + m)
                        pt, pr = proj_fm(wt, wr, lambda k: qm[:, k, :], qm_r)
                        fw.op("dve", lambda e, pt=pt, m=m: e.scalar_tensor_tensor(x[:, m, c0:c0 + 512], pt[:], dv[:, l, p, 2, m:m + 1], x[:, m, c0:c0 + 512], ALU.mult, ALU.add),
                              reads=[pr, r_dv, x_r[m][tb]], writes=[x_r[m][tb]])

                if not sample and l == 0:
                    tap(5, x, [x_r[c][0] for c in range(8)], is_bf=False)
                alias_src = [rr for row in kt_r for rr in row] + [rr for row in uc_r for rr in row] + hT_r + qm_r + ao_r + oc_r
                for rr in [q for row in hid_r for q in row] + [q for row in h2_r for q in row] + [q for row in stg_r for q in row]:
                    for s_ in alias_src:
                        rr.inherit(s_)
                for tb in range(NB):
                    rms_norm_block(l, p, tb, 1, lambda c, tb=tb: h2T[:, c, tb * 512:(tb + 1) * 512], lambda c, tb=tb: h2_r[c][tb], tb * 512)
                fw.op("pool", lambda e: e.memset(stg[:, :, :], 0.0), writes=[q for row in stg_r for q in row] + stg_h)
                if sample:
                    fw.op("dve", lambda e: e.tensor_copy(h2s[:, 0:8], h2T[:, :, 0]), reads=[h2_r[c][0] for c in range(8)], writes=[r_h2s])
                    fw.op("dve", lambda e: e.tensor_copy(h2s[:, 8:16], h2T[:, :, TS - 1]), reads=[h2_r[c][NB - 1] for c in range(8)], writes=[r_h2s])
                    r_x2i, r_x2o = Reg(), Reg()
                    fw.dma("sp", xin2[l].ap()[:, :], h2s[:], reads=[r_h2s], writes=[r_x2i], stream="dxo")
                    fw.op("pool", lambda e: e.collective_compute("AllGather", ALU.bypass, replica_groups=PAIRS,
                                                                 ins=[xin2[l].ap()], outs=[xout2[l].ap()]),
                          reads=[r_x2i], writes=[r_x2o])
                    for r_ in range(2):
                        fw.dma("sp", h2g[:, r_, :], xout2[l].ap()[r_ * 128:(r_ + 1) * 128, :], reads=[r_x2o], writes=[r_h2g], stream="dxi")
                    fw.op("pool", lambda e: e.tensor_scalar(h2h[:, :, 0], h2g[:, 0, 8:16], msk[:, 0:1], None, ALU.mult), reads=[r_h2g, r_msk], writes=[r_h2h])
                    fw.op("pool", lambda e: e.tensor_scalar(h2h[:, :, 1], h2g[:, 1, 0:8], msk[:, 1:2], None, ALU.mult), reads=[r_h2g, r_msk], writes=[r_h2h])
                hbufs, hregs = [hid, hid2], [hid_r, hid2_r]

                def up_round(hh):
                    n_h = 8 if hh < 2 else 6
                    hb, hr = hbufs[hh % 2], hregs[hh % 2]
                    for jj in range(n_h):
                        j = hh * 8 + jj
                        accs = {}
                        for half in range(2):
                            ch = half * 22 + j
                            wt, wr = wblk(l, BLK["UP"] + ch)
                            if sample:
                                pth, prh = proj_fm(wt, wr, lambda k: h2h[:, k, :], [r_h2h], n=2)
                                fw.op("act", lambda e, pth=pth, half=half: e.copy(stg[:, half, 0:1], pth[:, 0:1]), reads=[prh], writes=[stg_h[half]])
                                fw.op("act", lambda e, pth=pth, half=half: e.copy(stg[:, half, TS + 1:TS + 2], pth[:, 1:2]), reads=[prh], writes=[stg_h[half]])
                            for tb in range(NB):
                                pt, pr = proj_fm(wt, wr, lambda k, tb=tb: h2T[:, k, tb * 512:(tb + 1) * 512], [h2_r[k][tb] for k in range(8)])
                                if sample:
                                    fw.op("act", lambda e, pt=pt, half=half, tb=tb: e.copy(stg[:, half, 1 + tb * 512:1 + tb * 512 + 512], pt[:]),
                                          reads=[pr], writes=[stg_r[half][tb]])
                                else:
                                    for i in range(2):
                                        fw.op("act", lambda e, pt=pt, half=half, i=i: e.copy(stg[:, half, 258 * i + 1:258 * i + 257], pt[:, 256 * i:256 * i + 256]),
                                              reads=[pr], writes=[stg_r[half][0]])
                        for tb in range(NB):
                            for half in range(2):
                                ch = half * 22 + j
                                acc, accr = new_scr()
                                eng = "dve"
                                w0, w1, w2 = (vcol(l, "convw", t_ * 44 + ch) for t_ in range(3))
                                bcol = vcol(l, "convb", ch)
                                if sample:
                                    pieces = [(1 + tb * 512, 512, 0)]
                                    rd = [stg_r[half][t_] for t_ in range(max(0, tb - 1), min(NB, tb + 2))] + [stg_h[half]]
                                else:
                                    pieces = [(258 * i + 1, 256, 256 * i) for i in range(2)]
                                    rd = [stg_r[half][0]]
                                for (s0, n_, o0) in pieces:
                                    fw.op("act", lambda e, acc=acc, s0=s0, n_=n_, o0=o0, half=half, w1=w1, bcol=bcol: e.activation(
                                        acc[:, o0:o0 + n_], stg[:, half, s0:s0 + n_], AF.Identity, bias=bcol, scale=w1), reads=rd + [r_vec], writes=[accr])
                                    fw.op(eng, lambda e, acc=acc, s0=s0, n_=n_, o0=o0, half=half, w0=w0: e.scalar_tensor_tensor(
                                        acc[:, o0:o0 + n_], stg[:, half, s0 - 1:s0 - 1 + n_], w0, acc[:, o0:o0 + n_], ALU.mult, ALU.add), reads=rd + [r_vec, accr], writes=[accr])
                                    fw.op(eng, lambda e, acc=acc, s0=s0, n_=n_, o0=o0, half=half, w2=w2: e.scalar_tensor_tensor(
                                        acc[:, o0:o0 + n_], stg[:, half, s0 + 1:s0 + 1 + n_], w2, acc[:, o0:o0 + n_], ALU.mult, ALU.add), reads=rd + [r_vec, accr], writes=[accr])
                                accs[half] = (acc, accr)
                            (aa, aar), (ag, agr) = accs[0], accs[1]
                            fw.op("act", lambda e, aa=aa: e.activation(aa[:, 0:512], aa[:, 0:512], AF.Silu), reads=[aar], writes=[aar])
                            fw.op("dve", lambda e, aa=aa, ag=ag, jj=jj, tb=tb: e.tensor_tensor(hb[:, jj, tb * 512:(tb + 1) * 512], aa[:, 0:512], ag[:, 0:512], ALU.mult),
                                  reads=[aar, agr], writes=[hr[jj][tb]])

                def down_round(hh):
                    n_h = 8 if hh < 2 else 6
                    hb, hr = hbufs[hh % 2], hregs[hh % 2]
                    for m in range(8):
                        wt, wr = wblk(l, BLK["DOWN"] + m * 3 + hh, kc=n_h)
                        for tb in range(NB):
                            pt, pr = new_ps()

                            def mmd(e, pt=pt, tb=tb, wt=wt, n_h=n_h):
                                for jj in range(n_h):
                                    ins = e.matmul(pt[:], lhsT=wt[:, jj, :], rhs=hb[:, jj, tb * 512:(tb + 1) * 512], start=(jj == 0), stop=(jj == n_h - 1))
                                return ins
                            fw.op("pe", mmd, reads=[wr] + [hr[jj][tb] for jj in range(n_h)], writes=[pr])
                            fw.op("dve", lambda e, pt=pt, m=m, tb=tb: e.scalar_tensor_tensor(
                                x[:, m, tb * 512:(tb + 1) * 512], pt[:], dv[:, l, p, 5, m:m + 1], x[:, m, tb * 512:(tb + 1) * 512], ALU.mult, ALU.add),
                                reads=[pr, r_dv, x_r[m][tb]], writes=[x_r[m][tb]])
                if not sample and l == 0:
                    tap(6, x, [x_r[c][0] for c in range(8)], is_bf=False)

                up_round(0)
                up_round(1)
                down_round(0)
                up_round(2)
                down_round(1)
                down_round(2)
                alias_src = [q for row in hid_r for q in row] + [q for row in h2_r for q in row] + [q for row in stg_r for q in row]
                for rr in [q for row in kt_r for q in row] + [q for row in uc_r for q in row] + hT_r + qm_r + ao_r + oc_r:
                    for s_ in alias_src:
                        rr.inherit(s_)
            for c in range(8):
                for tb in range(NB):
                    o = Reg()
                    out_regs.append(o)
                    fw.dma("sp", YOUT[c, :, tb * 512:(tb + 1) * 512], x[:, c, tb * 512:(tb + 1) * 512], reads=[x_r[c][tb]], writes=[o], stream="do")

        run_pass(1)
        run_pass(0)
        if wseq_out is not None:
            return None
        fw.final_wait("sp", out_regs)
        fw.emit(es)
    return nc


def _blk(wcols):
    K = wcols.shape[0]
    kc = K // 128
    out = np.zeros((128, 1024), np.float32)
    out[:, :kc * 128] = wcols.reshape(kc, 128, 128).transpose(1, 0, 2).reshape(128, kc * 128)
    return out


def _fm(v):
    return np.ascontiguousarray(v.reshape(-1, 128).T)


def _consts():
    cst = np.zeros((128, 384), np.float32)
    cst[:, 0:128] = 1.0
    cst[0:64, 128:192] = 1.0
    cst[64:128, 192:256] = 1.0
    for i in range(128):
        d = i % 64
        part = (d % 32) // 16
        partner = i + 16 if part == 0 else i - 16
        cst[partner, 256 + i] = 1.0
    t = np.arange(2048)
    row = (t // GRID_W).astype(np.float32)
    col = (t % GRID_W).astype(np.float32)
    inv = (10000.0 ** (-np.arange(16, dtype=np.float32) / 16)).astype(np.float32)
    rope = np.zeros((2, 128, 2048), np.float32)
    for i in range(128):
        d = i % 64
        axis, part, f = d // 32, (d % 32) // 16, d % 16
        ang = (row if axis == 0 else col) * inv[f]
        rope[0, i] = np.cos(ang)
        rope[1, i] = np.sin(ang) * (-1.0 if part == 0 else 1.0)
    pc = np.zeros((128, 64), np.float32)
    for g in range(4):
        w = 2 << g
        hw = w // 2
        for tt in range(hw):
            pc[:, g * 16 + tt] = 1.0 / (tt + hw)
        ne = hw - 1
        for i in range(ne):
            pc[:, g * 16 + 8 + i] = 1.0 / (ne - i + hw)
    return cst, rope, pc


def _pc_sample(rank):
    pc = np.zeros((128, 64), np.float32)
    for g in range(4):
        w = 2 << g
        hw = w // 2
        for tt in range(hw):
            pc[:, g * 16 + tt] = (1.0 / (tt + hw)) if rank == 0 else (1.0 / w)
        ne = hw - 1
        for i in range(ne):
            pc[:, g * 16 + 8 + i] = (1.0 / (ne - i + hw)) if rank == 1 else (1.0 / w)
    return pc


_NC_CACHE = {}


def kernel(x_prompt, x_sample, cache_diff_k, cache_diff_v, cache_gqa_k, cache_gqa_v, c, c_ctx,
           norm1_g, norm2_g, w_mod, b_mod, w_in, qn_a, kn_a, qn_b, kn_b,
           lam_q1, lam_k1, lam_q2, lam_k2, subln_g, w_pool, pool_scale,
           w_br_a, w_br_b, w_br_c, w_gate, b_gate, w_out, w_up, conv_w, conv_b, w_down):
    f = lambda a: np.asarray(a, dtype=np.float32)
    x_prompt, x_sample = f(x_prompt), f(x_sample)
    Wb = np.zeros((L, NBLK, 128, 1024), np.float32)
    vec = np.zeros((L, 128, NVEC), np.float32)
    for l in range(L):
        wi = f(w_in[l])
        for i in range(4):
            Wb[l, BLK["Q"] + i] = _blk(wi[:, i * 128:(i + 1) * 128])
            Wb[l, BLK["Q"] + 4 + i] = _blk(wi[:, 1536 + i * 128:1536 + (i + 1) * 128])
            Wb[l, BLK["K"] + i] = _blk(wi[:, 512 + i * 128:512 + (i + 1) * 128])
            Wb[l, BLK["V"] + i] = _blk(wi[:, 1024 + i * 128:1024 + (i + 1) * 128])
            Wb[l, BLK["UC"] + i] = _blk(wi[:, 2304 + i * 128:2304 + (i + 1) * 128])
        for g in range(2):
            kb = wi[:, 2048 + g * 64:2048 + (g + 1) * 64]
            Wb[l, BLK["K"] + 4 + g] = _blk(np.concatenate([kb, kb], axis=1))
        Wb[l, BLK["V"] + 4] = _blk(wi[:, 2176:2304])
        wg = f(w_gate[l])
        for j in range(24):
            Wb[l, BLK["GATE"] + j] = _blk(wg[:, j * 128:(j + 1) * 128])
        for nm, wsrc in (("BRA", w_br_a), ("BRB", w_br_b), ("BRC", w_br_c)):
            ws = f(wsrc[l])
            for m in range(8):
                Wb[l, BLK[nm] + m] = _blk(ws[:, m * 128:(m + 1) * 128])
        wo = f(w_out[l])
        for m in range(8):
            Wb[l, BLK["OUT"] + m] = _blk(wo[:, m * 128:(m + 1) * 128])
        wu = f(w_up[l])
        for j in range(44):
            Wb[l, BLK["UP"] + j] = _blk(wu[:, j * 128:(j + 1) * 128])
        wd = f(w_down[l])
        for m in range(8):
            for kg in range(3):
                rows = wd[kg * 1024:min((kg + 1) * 1024, DFF), m * 128:(m + 1) * 128]
                Wb[l, BLK["DOWN"] + m * 3 + kg] = _blk(rows)
        wp = f(w_pool[l])
        for g in range(4):
            Wb[l, BLK["POOLW"] + g] = _blk(wp[g])
        wm = f(w_mod[l])
        for j in range(48):
            Wb[l, BLK["MOD"] + j] = _blk(wm[:, j * 128:(j + 1) * 128])
        vec[l, :, VC["norm1"]:VC["norm1"] + 8] = _fm(f(norm1_g[l]))
        vec[l, :, VC["norm2"]:VC["norm2"] + 8] = _fm(f(norm2_g[l]))
        vec[l, :, VC["bmod"]:VC["bmod"] + 48] = _fm(f(b_mod[l]))
        vec[l, :, VC["bgate"]:VC["bgate"] + 24] = _fm(f(b_gate[l]))
        cw = f(conv_w[l])
        for t_ in range(3):
            vec[l, :, VC["convw"] + t_ * 44:VC["convw"] + (t_ + 1) * 44] = _fm(cw[t_])
        vec[l, :, VC["convb"]:VC["convb"] + 44] = _fm(f(conv_b[l]))
        vec[l, :, VC["pscale"]:VC["pscale"] + 4] = _fm(f(pool_scale[l]))
        for nm, src in (("qna", qn_a), ("kna", kn_a), ("qnb", qn_b), ("knb", kn_b)):
            vec[l, :, VC[nm]] = np.tile(f(src[l]), 2)
        vec[l, :, VC["subln"]] = f(subln_g[l])
        for i, src in enumerate((lam_q1, lam_k1, lam_q2, lam_k2)):
            vec[l, 0:64, VC["lam"] + i] = f(src[l])
    cst, rope_full, pc = _consts()
    cdk, cdv, cgk, cgv = f(cache_diff_k), f(cache_diff_v), f(cache_gqa_k), f(cache_gqa_v)
    in_maps = []
    for core in range(8):
        b = core // 2
        rank = core % 2
        xs = np.ascontiguousarray(x_sample[b, rank * TS:(rank + 1) * TS].T).reshape(8, 128, TS)
        rope = np.ascontiguousarray(rope_full[:, :, rank * TS:(rank + 1) * TS])
        mskv = np.zeros((128, 16), np.float32)
        mskv[:, 0] = float(rank)
        mskv[:, 1] = float(1 - rank)
        pcs = np.concatenate([pc, _pc_sample(rank)], axis=1)
        xp = np.ascontiguousarray(x_prompt[2 * core:2 * core + 2].reshape(TP, D).T).reshape(8, 128, TP)
        ct = np.concatenate([_fm(f(c)[b]), _fm(f(c_ctx))], axis=1)
        ck = np.zeros((L, 6, 128, PAST), np.float32)
        cv = np.zeros((L, 2, 128, 640), np.float32)
        for l in range(L):
            ka = cdk[b, l].reshape(PAST, 512).T
            ck[l, 0:4] = ka.reshape(4, 128, PAST)
            kb = cgk[b, l].reshape(PAST, 128).T
            for g in range(2):
                ck[l, 4 + g] = np.concatenate([kb[g * 64:(g + 1) * 64], kb[g * 64:(g + 1) * 64]], axis=0)
            va = cdv[b, l].reshape(PAST, 512)
            vb = cgv[b, l].reshape(PAST, 128)
            cv[l] = np.concatenate([va, vb], axis=1).reshape(2, 128, 640)
        in_maps.append(dict(xs=xs, xp=xp, w=Wb, vec=vec, ct=np.ascontiguousarray(ct), cst=cst, rope=rope, pcnt=pcs, msk=mskv, ck=ck, cv=cv))
    if "nc" not in _NC_CACHE:
        seq = []
        build_program(wseq=None, wseq_out=seq)
        _NC_CACHE["nc"] = build_program(wseq=seq)
    nc = _NC_CACHE["nc"]
    res = run_bass_kernel_spmd(nc, in_maps, core_ids=list(range(8)))
    R_ = res.results
    if DEBUG:
        DBG_OUT["dbg"] = [r["dbg"] for r in R_]
    y_p = np.zeros((16, 256, D), np.float32)
    y_s = np.zeros((4, 2048, D), np.float32)
    ndk = np.zeros((16, L, 256, 2, 4, 64), np.float32)
    ndv = np.zeros((16, L, 256, 4, 128), np.float32)
    ngk = np.zeros((16, L, 256, 2, 64), np.float32)
    ngv = np.zeros((16, L, 256, 2, 64), np.float32)
    for core in range(8):
        r = R_[core]
        yp = r["yp"].reshape(D, TP).T.reshape(2, 256, D)
        y_p[2 * core:2 * core + 2] = yp
        ys = r["ys"].reshape(D, TS).T
        b, hf = core // 2, core % 2
        y_s[b, hf * TS:(hf + 1) * TS] = ys
        ko, vo = r["ko"], r["vo"]
        for l in range(L):
            ka = ko[l, 0:4].reshape(512, TP).T.reshape(2, 256, 2, 4, 64)
            ndk[2 * core:2 * core + 2, l] = ka
            kb = np.stack([ko[l, 4, 0:64, :], ko[l, 5, 0:64, :]], axis=0)
            ngk[2 * core:2 * core + 2, l] = kb.transpose(2, 0, 1).reshape(2, 256, 2, 64)
            v = vo[l].reshape(2, 256, 640)
            ndv[2 * core:2 * core + 2, l] = v[:, :, 0:512].reshape(2, 256, 4, 128)
            ngv[2 * core:2 * core + 2, l] = v[:, :, 512:640].reshape(2, 256, 2, 64)
    return (y_p, y_s, ndk, ndv, ngk, ngv)
```

```python
import math
from contextlib import ExitStack
import numpy as np
import concourse.bass as bass
import concourse.mybir as mybir
from concourse.bass_utils import run_bass_kernel_spmd

F32 = mybir.dt.float32
BF16 = mybir.dt.bfloat16
AF = mybir.ActivationFunctionType
ALU = mybir.AluOpType

D = 1024
L = 2
PAST = 256
GRID_W = 64
EPS = 1e-6
DFF = 2816
TS = 1024
TP = 512
SEM_EPOCH = 15000
DEBUG = False
DBG_OUT = {}
ENGS = ("pe", "act", "dve", "pool", "sp")

BLK = {}
_n = 0
for _name, _cnt in (("Q", 8), ("K", 6), ("UC", 4), ("V", 5), ("GATE", 24), ("BRA", 8), ("BRB", 8), ("BRC", 8),
                    ("OUT", 8), ("UP", 44), ("DOWN", 24), ("POOLW", 4), ("MOD", 48)):
    BLK[_name] = _n
    _n += _cnt
NBLK = _n

VC = {}
_n = 0
for _name, _cnt in (("norm1", 8), ("norm2", 8), ("bmod", 48), ("bgate", 24), ("convw", 132), ("convb", 44),
                    ("pscale", 4), ("qna", 1), ("kna", 1), ("qnb", 1), ("knb", 1), ("subln", 1), ("lam", 4)):
    VC[_name] = _n
    _n += _cnt
NVEC = _n


class Reg:
    __slots__ = ("name", "w", "r", "excl")

    def __init__(self, name="", excl=False):
        self.name = name
        self.excl = excl
        self.w = None
        self.r = {}

    def inherit(self, other):
        if other.w is not None:
            s_, i_ = other.w
            if self.r.get(s_, -1) < i_:
                self.r[s_] = i_
        for s_, i_ in other.r.items():
            if self.r.get(s_, -1) < i_:
                self.r[s_] = i_


class _Rec:
    def __init__(self):
        self.calls = []

    def __getattr__(self, name):
        def f(*a, **kw):
            self.calls.append((name, a, kw))
            return None
        return f


def _replay(calls):
    def fn(eng):
        ins = None
        for (name, a, kw) in calls:
            ins = getattr(eng, name)(*a, **kw)
        return ins
    return fn


class FW:
    def __init__(self, nc):
        self.nc = nc
        self.ops = {e: [] for e in ENGS}
        self.src_ops = {e: [] for e in ENGS}
        self.rr = {}

    def _deps(self, reads, writes):
        deps = set()
        for r in reads:
            if r.w is not None:
                deps.add(r.w)
            if r.excl:
                for x in r.r.items():
                    deps.add(x)
        for w in writes:
            if w.w is not None:
                deps.add(w.w)
            for x in w.r.items():
                deps.add(x)
        return deps

    def _commit(self, me, reads, writes):
        for r in reads:
            if r.r.get(me[0], -1) < me[1]:
                r.r[me[0]] = me[1]
        for w in writes:
            w.w = me
            w.r = {}

    def op(self, eng, fn, reads=(), writes=()):
        deps = self._deps(reads, writes)
        if eng == "pe":
            deps = {d for d in deps if d[0] != "pe"}
        idx = len(self.src_ops[eng])
        rc_ = _Rec()
        fn(rc_)
        fn = _replay(rc_.calls)
        rec = dict(eng=eng, fn=fn, deps=deps, src=eng, idx=idx, signal=False, dma=False)
        self.src_ops[eng].append(rec)
        self.ops[eng].append(rec)
        self._commit((eng, idx), reads, writes)

    NSUB = 8

    def dma(self, q, out, in_, reads=(), writes=(), stream="d0", sub=None):
        deps = self._deps(reads, writes)
        if sub is None:
            sub = self.rr.get(stream, 0) % self.NSUB
            self.rr[stream] = self.rr.get(stream, 0) + 1
        stream = f"{stream}_{sub}"
        if stream not in self.src_ops:
            self.src_ops[stream] = []
        if self.src_ops[stream]:
            deps.add((stream, len(self.src_ops[stream]) - 1))
        idx = len(self.src_ops[stream])
        rec = dict(eng=q, fn=lambda e: e.dma_start(out=out, in_=in_), deps=deps,
                   src=stream, idx=idx, signal=True, dma=True)
        self.src_ops[stream].append(rec)
        self.ops[q].append(rec)
        self._commit((stream, idx), reads, writes)

    def final_wait(self, eng, regs):
        deps = self._deps(regs, ())
        self.ops[eng].append(dict(eng=eng, fn=None, deps=deps, src=None, idx=None, signal=False, dma=False))

    def emit(self, es):
        nc = self.nc
        for e in ENGS:
            for rec in self.ops[e]:
                for (s, i) in rec["deps"]:
                    self.src_ops[s][i]["signal"] = True
        sems = {}
        for s, lst in self.src_ops.items():
            step = 16 if (lst and lst[0]["dma"]) else 1
            cnt = 0
            epoch = 0
            for rec in lst:
                if rec["signal"]:
                    if cnt + step > SEM_EPOCH:
                        epoch += 1
                        cnt = 0
                    cnt += step
                    rec["sig"] = (epoch, cnt)
                    if (s, epoch) not in sems:
                        sems[(s, epoch)] = es.enter_context(nc.semaphore(f"s_{s}_{epoch}"))
        block = es.enter_context(nc.Block())
        src_ops = self.src_ops

        def run(e):
            def body(eng):
                waited = {}
                for rec in self.ops[e]:
                    need = {}
                    for (s, i) in rec["deps"]:
                        ep, v = src_ops[s][i]["sig"]
                        k = (s, ep)
                        if waited.get(k, 0) >= v:
                            continue
                        if need.get(k, 0) < v:
                            need[k] = v
                    for k, v in need.items():
                        eng.wait_ge(sems[k], v)
                        waited[k] = v
                    if rec["fn"] is None:
                        continue
                    ins = rec["fn"](eng)
                    if rec["signal"]:
                        ins.then_inc(sems[(rec["src"], rec["sig"][0])], 16 if rec["dma"] else 1)
            return body

        block.tensor(run("pe"))
        block.scalar(run("act"))
        block.vector(run("dve"))
        block.gpsimd(run("pool"))
        block.sync(run("sp"))


def build_program(wseq=None, wseq_out=None):
    nc = bass.Bass("TRN2", target_bir_lowering=False)
    dt = lambda name, shape, kind: nc.dram_tensor(name, shape, F32, kind=kind).ap()
    XS = dt("xs", [8, 128, TS], "ExternalInput")
    XP = dt("xp", [8, 128, TP], "ExternalInput")
    W = dt("w", [L, NBLK, 128, 1024], "ExternalInput")
    VEC = dt("vec", [L, 128, NVEC], "ExternalInput")
    CT = dt("ct", [128, 16], "ExternalInput")
    CST = dt("cst", [128, 384], "ExternalInput")
    ROPE = dt("rope", [2, 128, TS], "ExternalInput")
    PCNT = dt("pcnt", [128, 128], "ExternalInput")
    MSK = dt("msk", [128, 16], "ExternalInput")
    xinK = [nc.dram_tensor(f"xinK{l}", [768, 1024], BF16) for l in range(L)]
    xoutK = [nc.dram_tensor(f"xoutK{l}", [1536, 1024], BF16) for l in range(L)]
    xinV = [nc.dram_tensor(f"xinV{l}", [1024, 672], BF16) for l in range(L)]
    xoutV = [nc.dram_tensor(f"xoutV{l}", [2048, 672], BF16) for l in range(L)]
    xin2 = [nc.dram_tensor(f"xin2_{l}", [128, 32], BF16) for l in range(L)]
    xout2 = [nc.dram_tensor(f"xout2_{l}", [256, 32], BF16) for l in range(L)]
    PAIRS = [[0, 1], [2, 3], [4, 5], [6, 7]]
    CK = dt("ck", [L, 6, 128, PAST], "ExternalInput")
    CV = dt("cv", [L, 2, 128, 640], "ExternalInput")
    YS = dt("ys", [8, 128, TS], "ExternalOutput")
    YP = dt("yp", [8, 128, TP], "ExternalOutput")
    KO = dt("ko", [L, 6, 128, TP], "ExternalOutput")
    VO = dt("vo", [L, TP, 640], "ExternalOutput")
    DBG = dt("dbg", [10, 8, 128, 512], "ExternalOutput") if DEBUG else None

    es = ExitStack()
    with es:
        fw = FW(nc)
        sbt = lambda n, s, d: es.enter_context(nc.sbuf_tensor(n, s, d))
        x = sbt("x", [128, 8, TS], F32)
        x_r = [[Reg() for _ in range(4)] for _ in range(8)]
        AR_KV, AR_UC, AR_BLK = 0, 13824, 13824 + 2112
        arena = sbt("arena", [128, 13824 + 2112 + 7168], F32)

        def carve(off_w, nbytes, dtype):
            a = arena[:, off_w:off_w + nbytes // 4]
            return a.bitcast(dtype) if dtype != F32 else a
        kt = carve(AR_KV, 6 * 2304 * 2, BF16).rearrange("p (c t) -> p c t", c=6)
        vt = carve(AR_KV + 6912, 18 * 768 * 2, BF16).rearrange("p (c t) -> p c t", c=18)
        ucb = carve(AR_UC, 4 * 1056 * 2, BF16).rearrange("p (c t) -> p c t", c=4)
        hT = carve(AR_BLK, 8192, BF16).rearrange("p (c t) -> p c t", c=8)
        qm = carve(AR_BLK + 2048, 8192, BF16).rearrange("p (c t) -> p c t", c=8)
        ao = carve(AR_BLK + 4096, 8192, BF16).rearrange("p (c t) -> p c t", c=8)
        ocb = carve(AR_BLK + 6144, 4096, BF16).rearrange("p (c t) -> p c t", c=4)
        hid = carve(AR_KV, 8 * 2048 * 2, BF16).rearrange("p (c t) -> p c t", c=8)
        stg = carve(AR_KV + 8192, 2 * 2056 * 4, F32).rearrange("p (c t) -> p c t", c=2)
        h2T = carve(AR_UC, 8 * 2048 * 2, BF16).rearrange("p (c t) -> p c t", c=8)
        NSCR = 8
        scr = [sbt(f"scr{i}", [128, 544], F32) for i in range(NSCR)]
        scr_r = [Reg() for _ in range(NSCR)]
        NRING = 12
        ring = [sbt(f"ring{i}", [128, 8, 128], BF16) for i in range(NRING)]
        ring_r = [Reg() for _ in range(NRING)]
        rope = sbt("rope_sb", [128, 2, TS], BF16)
        cst = sbt("cst_sb", [128, 384], BF16)
        ones_bf, onesblk_bf, perm_bf = cst[:, 0:128], cst[:, 128:256], cst[:, 256:384]
        ones_f = sbt("ones_f", [128, 128], F32)
        vec = sbt("vec_sb", [128, L, NVEC], F32)
        ct = sbt("ct_sb", [128, 16], F32)
        sc_bf = sbt("sc_bf", [128, 8, 2], BF16)
        modv = sbt("modv", [128, L, 48, 2], F32)
        dv = sbt("dv", [128, L, 2, 6, 8], F32)
        lamv = sbt("lamv", [128, L, 8], F32)
        pcnt = sbt("pcnt_sb", [128, 128], F32)
        msk = sbt("msk_sb", [128, 16], F32)
        r_msk = Reg()
        halo_u = sbt("halo_u", [128, 4, 2, 32], BF16)
        r_halo_u = Reg()
        uh = sbt("uh", [128, 4, 32], BF16)
        r_uh = Reg()
        h2s = sbt("h2s", [128, 32], BF16)
        r_h2s = Reg()
        h2g = sbt("h2g", [128, 2, 32], BF16)
        r_h2g = Reg()
        h2h = sbt("h2h", [128, 8, 2], BF16)
        r_h2h = Reg()
        stg_h = [Reg(), Reg()]
        pacc = [[sbt(f"pacc{i}{j}", [128, 512], F32) for j in range(2)] for i in range(2)]
        pacc_r = [[Reg() for j in range(2)] for i in range(2)]
        qo = sbt("qo", [128, 8, 512], BF16)
        qo_r = [Reg() for _ in range(8)]
        hid2 = sbt("hid2", [128, 8, 1024], BF16)
        hid2_r = [[Reg() for _ in range(4)] for _ in range(8)]
        dbuf = sbt("dbuf", [128, 512], BF16)
        r_dbuf = Reg()
        epsc = sbt("epsc", [128, 1], F32)

        R = lambda: Reg()
        r_rope, r_cst, r_onesf, r_vec, r_ct, r_sc, r_modv, r_dv, r_lam, r_pcnt, r_eps = (R() for _ in range(11))
        kt_r = [[Reg() for _ in range(5)] for _ in range(6)]
        vt_r = [Reg() for _ in range(18)]
        uc_r = [[Reg() for _ in range(4)] for _ in range(4)]
        hT_r, qm_r, ao_r, oc_r = ([Reg() for _ in range(8)] for _ in range(4))
        hid_r = [[Reg() for _ in range(4)] for _ in range(8)]
        h2_r = [[Reg() for _ in range(4)] for _ in range(8)]
        stg_r = [[Reg() for _ in range(4)] for _ in range(2)]
        ps = [es.enter_context(nc.psum_tensor(f"ps{i}", [128, 512], F32)) for i in range(8)]
        ps_r = [Reg(excl=True) for _ in range(8)]
        out_regs = []
        xin_regs = []
        xinK_regs = []

        state = dict(scr=0, ring=0, ps=0, issued=0)

        def tap(i, src, regs, nchunk=8, is_bf=True):
            if DBG is None or wseq_out is not None:
                return
            for c in range(nchunk):
                o = Reg()
                out_regs.append(o)
                fw.dma("pool" if is_bf else "sp", DBG[i, c, :, :], src[:, c, 0:512], reads=regs, writes=[o], stream="dbg")

        def new_scr():
            i = state["scr"] % NSCR
            state["scr"] += 1
            return scr[i], scr_r[i]

        def new_ps(pool=(0, 1, 2, 3, 4, 5, 6, 7)):
            i = pool[state["ps"] % len(pool)]
            state["ps"] += 1
            return ps[i], ps_r[i]

        PF = NRING - 3

        def _issue_w(n):
            l_, b_, kc_ = wseq[n]
            i = n % NRING
            src = W[l_, b_, :, 0:kc_ * 128].rearrange("p (k n) -> p k n", k=kc_)
            fw.dma("pool", ring[i][:, 0:kc_, :], src, writes=[ring_r[i]], stream="dw", sub=i)

        def wblk(l, b, kc=8):
            n = state["ring"]
            state["ring"] += 1
            if wseq_out is not None:
                wseq_out.append((l, b, kc))
                i = n % NRING
                return ring[i], ring_r[i]
            while state["issued"] < min(len(wseq), n + PF + 1):
                _issue_w(state["issued"])
                state["issued"] += 1
            i = n % NRING
            return ring[i], ring_r[i]

        fw.dma("pool", cst[:], CST[:, :], writes=[r_cst], stream="dc")
        fw.dma("pool", rope[:, 0, :], ROPE[0, :, :], writes=[r_rope], stream="dc")
        fw.dma("pool", rope[:, 1, :], ROPE[1, :, :], writes=[r_rope], stream="dc")
        for l in range(L):
            fw.dma("sp", vec[:, l, :], VEC[l, :, :], writes=[r_vec], stream="d0")
        fw.dma("sp", ct[:], CT[:, :], writes=[r_ct], stream="d0")
        fw.dma("sp", pcnt[:], PCNT[:, :], writes=[r_pcnt], stream="d0")
        fw.dma("sp", msk[:], MSK[:, :], writes=[r_msk], stream="d0")
        fw.op("dve", lambda e: e.memset(ones_f[:], 1.0), writes=[r_onesf])
        fw.op("dve", lambda e: e.memset(epsc[:], EPS), writes=[r_eps])
        fw.op("pool", lambda e: e.memset(qo[:], 0.0), writes=qo_r)
        fw.op("dve", lambda e: e.memset(uh[:], 0.0), writes=[r_uh])
        fw.op("dve", lambda e: e.memset(h2s[:], 0.0), writes=[r_h2s])
        fw.op("act", lambda e: e.activation(sc_bf[:, :, 0], ct[:, 0:8], AF.Silu), reads=[r_ct], writes=[r_sc])
        fw.op("act", lambda e: e.activation(sc_bf[:, :, 1], ct[:, 8:16], AF.Silu), reads=[r_ct], writes=[r_sc])

        def vcol(l, name, i=0):
            c = VC[name] + i
            return vec[:, l, c:c + 1]

        def mod_block(l, j, pool=None):
            wt, wr = wblk(l, BLK["MOD"] + j)
            pt, pr = new_ps(pool) if pool is not None else new_ps()

            def mm(e):
                for k in range(8):
                    ins = e.matmul(pt[:, 0:2], lhsT=wt[:, k, :], rhs=sc_bf[:, k, :], start=(k == 0), stop=(k == 7))
                return ins
            fw.op("pe", mm, reads=[wr, r_sc], writes=[pr])
            fw.op("dve", lambda e: e.tensor_scalar(modv[:, l, j, :], pt[:, 0:2], vcol(l, "bmod", j), None, ALU.add),
                  reads=[pr, r_vec], writes=[r_modv])

        def mod_der(l, parts=(0, 1, 2, 3, 4, 5)):
            for p in range(2):
                def der(e):
                    n1 = vec[:, l, VC["norm1"]:VC["norm1"] + 8]
                    n2 = vec[:, l, VC["norm2"]:VC["norm2"] + 8]
                    ins = None
                    if 0 in parts:
                        ins = e.scalar_tensor_tensor(dv[:, l, p, 0, :], modv[:, l, 8:16, p], 1.0, n1, ALU.add, ALU.mult)
                    if 1 in parts:
                        ins = e.tensor_copy(dv[:, l, p, 1, :], modv[:, l, 0:8, p])
                    if 2 in parts:
                        ins = e.tensor_copy(dv[:, l, p, 2, :], modv[:, l, 16:24, p])
                    if 3 in parts:
                        ins = e.scalar_tensor_tensor(dv[:, l, p, 3, :], modv[:, l, 32:40, p], 1.0, n2, ALU.add, ALU.mult)
                    if 4 in parts:
                        ins = e.tensor_copy(dv[:, l, p, 4, :], modv[:, l, 24:32, p])
                    if 5 in parts:
                        ins = e.tensor_copy(dv[:, l, p, 5, :], modv[:, l, 40:48, p])
                    return ins
                fw.op("dve", der, reads=[r_modv, r_vec], writes=[r_dv])

        for j in range(16):
            mod_block(0, j)
        mod_der(0, (0, 1))
        pending_mod = [(0, j) for j in range(16, 48)] + [(1, j) for j in range(48)]

        def mod_pop(pool):
            l_, j_ = pending_mod.pop(0)
            mod_block(l_, j_, pool)
            if (l_, j_) == (0, 47):
                mod_der(0, (2, 3, 4, 5))

        def mod_step(pool):
            n_ = 2 if (pending_mod and pending_mod[0][0] == 0) else 1
            for _ in range(n_):
                if pending_mod:
                    mod_pop(pool)

        def mod_flush():
            if pending_mod or not state.get("der1"):
                while pending_mod:
                    mod_pop(None)
                mod_der(1)
                state["der1"] = True

        for l in range(L):
            lam_init = 0.8 - 0.6 * math.exp(-0.3 * l)
            c0 = VC["lam"]
            fw.op("dve", lambda e, l=l, c0=c0: e.tensor_tensor(lamv[:, l, 2:3], vec[:, l, c0:c0 + 1], vec[:, l, c0 + 1:c0 + 2], ALU.mult),
                  reads=[r_vec], writes=[r_lam])
            fw.op("dve", lambda e, l=l, c0=c0: e.tensor_tensor(lamv[:, l, 3:4], vec[:, l, c0 + 2:c0 + 3], vec[:, l, c0 + 3:c0 + 4], ALU.mult),
                  reads=[r_vec, r_lam], writes=[r_lam])
            pt, pr = new_ps()
            fw.op("pe", lambda e, pt=pt, l=l: e.matmul(pt[:, 0:2], lhsT=ones_f[:], rhs=lamv[:, l, 2:4], start=True, stop=True),
                  reads=[r_lam, r_onesf], writes=[pr])
            fw.op("act", lambda e, pt=pt, l=l: e.activation(lamv[:, l, 4:6], pt[:, 0:2], AF.Exp), reads=[pr, r_lam], writes=[r_lam])
            fw.op("dve", lambda e, l=l, li=lam_init: e.scalar_tensor_tensor(lamv[:, l, 0:1], lamv[:, l, 5:6], -li, lamv[:, l, 4:5], ALU.add, ALU.subtract),
                  reads=[r_lam], writes=[r_lam])
            fw.op("dve", lambda e, l=l, li=lam_init: e.tensor_scalar(lamv[:, l, 1:2], vcol(l, "subln"), 1.0 - li, None, ALU.mult),
                  reads=[r_vec, r_lam], writes=[r_lam])

        def rms_norm_block(l, p, tb, which, dst, dst_r, xcol0):
            cs = slice(xcol0, xcol0 + 512)
            pt, pr = new_ps()
            sqs = []
            for c in range(8):
                st, sr = new_scr()
                sq = st[:, 0:256].bitcast(BF16)
                fw.op("act", lambda e, sq=sq, c=c: e.activation(sq, x[:, c, cs], AF.Square), reads=[x_r[c][tb]], writes=[sr])
                fw.op("pe", lambda e, sq=sq, c=c, pt=pt: e.matmul(pt[:], lhsT=ones_bf, rhs=sq, start=(c == 0), stop=(c == 7)),
                      reads=[sr, r_cst], writes=[pr])
            rt, rr = new_scr()
            fw.op("act", lambda e, rt=rt, pt=pt: e.activation(rt[:, 0:512], pt[:], AF.Ln, bias=epsc[:, 0:1], scale=1.0 / D),
                  reads=[pr, r_eps], writes=[rr])
            fw.op("act", lambda e, rt=rt: e.activation(rt[:, 0:512], rt[:, 0:512], AF.Exp, scale=-0.5), reads=[rr], writes=[rr])
            for c in range(8):
                tt, tr = new_scr()
                sc = dv[:, l, p, 3 * which + 0, c:c + 1]
                sh = dv[:, l, p, 3 * which + 1, c:c + 1]
                fw.op("dve", lambda e, tt=tt, c=c, sc=sc, rt=rt: e.scalar_tensor_tensor(tt[:, 0:512], x[:, c, cs], sc, rt[:, 0:512], ALU.mult, ALU.mult),
                      reads=[x_r[c][tb], rr, r_dv], writes=[tr])
                fw.op("act", lambda e, tt=tt, c=c, sh=sh: e.activation(dst(c), tt[:, 0:512], AF.Identity, bias=sh),
                      reads=[tr, r_dv], writes=[dst_r(c)])

        def proj_fm(wt, wr, rhs_fn, rhs_regs, kc=8, n=512):
            pt, pr = new_ps()

            def mm(e):
                for k in range(kc):
                    ins = e.matmul(pt[:, 0:n], lhsT=wt[:, k, :], rhs=rhs_fn(k), start=(k == 0), stop=(k == kc - 1))
                return ins
            fw.op("pe", mm, reads=[wr] + list(rhs_regs), writes=[pr])
            return pt, pr

        def qk_post(l, pt, pr, gname, dst_ap, dst_regs, rope_cols=None, kout=None, dst_hi=None, dst_hi_regs=None):
            st, sr = new_scr()
            sq = st[:, 0:256].bitcast(BF16)
            fw.op("act", lambda e: e.activation(sq, pt[:], AF.Square), reads=[pr], writes=[sr])
            p2, p2r = new_ps()
            fw.op("pe", lambda e: e.matmul(p2[:], lhsT=onesblk_bf, rhs=sq, start=True, stop=True), reads=[sr, r_cst], writes=[p2r])
            rt, rr = new_scr()
            fw.op("act", lambda e: e.activation(rt[:, 0:512], p2[:], AF.Ln, bias=epsc[:, 0:1], scale=1.0 / 64), reads=[p2r, r_eps], writes=[rr])
            fw.op("act", lambda e: e.activation(rt[:, 0:512], rt[:, 0:512], AF.Exp, scale=-0.5), reads=[rr], writes=[rr])
            g = vcol(l, gname)
            if kout is not None:
                kst, kstr = new_scr()
                fw.op("dve", lambda e: e.scalar_tensor_tensor(kst[:, 0:512], pt[:], g, rt[:, 0:512], ALU.mult, ALU.mult),
                      reads=[pr, rr, r_vec], writes=[kstr])
                o = Reg()
                out_regs.append(o)
                fw.dma("sp", kout, kst[:, 0:512], reads=[kstr], writes=[o], stream="do")
            if rope_cols is None:
                if dst_hi is None:
                    fw.op("dve", lambda e: e.scalar_tensor_tensor(dst_ap, pt[:], g, rt[:, 0:512], ALU.mult, ALU.mult),
                          reads=[pr, rr, r_vec], writes=dst_regs)
                else:
                    fw.op("dve", lambda e: e.scalar_tensor_tensor(dst_ap[0:64], pt[0:64, :], g[0:64], rt[0:64, 0:512], ALU.mult, ALU.mult),
                          reads=[pr, rr, r_vec], writes=dst_regs)
                    fw.op("dve", lambda e: e.scalar_tensor_tensor(dst_hi[64:128], pt[64:128, :], g[64:128], rt[64:128, 0:512], ALU.mult, ALU.mult),
                          reads=[pr, rr, r_vec], writes=dst_hi_regs)
                return
            qt, qr = new_scr()
            qn = qt[:, 0:256].bitcast(BF16)
            fw.op("dve", lambda e: e.scalar_tensor_tensor(qn, pt[:], g, rt[:, 0:512], ALU.mult, ALU.mult),
                  reads=[pr, rr, r_vec], writes=[qr])
            p3, p3r = new_ps()
            fw.op("pe", lambda e: e.matmul(p3[:], lhsT=perm_bf, rhs=qn, start=True, stop=True), reads=[qr, r_cst], writes=[p3r])
            t1, t1r = new_scr()
            fw.op("pool", lambda e: e.tensor_tensor(t1[:, 0:512], qn, rope[:, 0, rope_cols], ALU.mult), reads=[qr, r_rope], writes=[t1r])
            t2, t2r = new_scr()
            fw.op("dve", lambda e: e.tensor_tensor(t2[:, 0:512], p3[:], rope[:, 1, rope_cols], ALU.mult), reads=[p3r, r_rope], writes=[t2r])
            if dst_hi is None:
                fw.op("pool", lambda e: e.tensor_tensor(dst_ap, t1[:, 0:512], t2[:, 0:512], ALU.add), reads=[t1r, t2r], writes=dst_regs)
            else:
                fw.op("dve", lambda e: e.tensor_tensor(dst_ap[0:64], t1[0:64, 0:512], t2[0:64, 0:512], ALU.add), reads=[t1r, t2r], writes=dst_regs)
                fw.op("dve", lambda e: e.tensor_tensor(dst_hi[64:128], t1[64:128, 0:512], t2[64:128, 0:512], ALU.add), reads=[t1r, t2r], writes=dst_hi_regs)

        def run_pass(p):
            T = TS if p == 0 else TP
            NB = T // 512
            sample = (p == 0)
            XIN, YOUT = (XS, YS) if sample else (XP, YP)
            if sample:
                seqs = [dict(t0=0, S=TS, past=PAST, kcol0=0, vch0=0, ucol0=0, nk=PAST + 2048)]
            else:
                seqs = [dict(t0=256 * i, S=256, past=0, kcol0=256 * i, vch0=2 * i, ucol0=272 * i, nk=256) for i in range(2)]
            for c in range(8):
                for tb in range(NB):
                    fw.dma("sp", x[:, c, tb * 512:(tb + 1) * 512], XIN[c, :, tb * 512:(tb + 1) * 512], writes=[x_r[c][tb]], stream="dx")

            def kreg(c, col):
                if sample:
                    return kt_r[c][0] if col < PAST else kt_r[c][1 + (col - PAST) // 1024]
                return kt_r[c][0]

            for l in range(L):
                if l == 1:
                    mod_flush()
                if sample:
                    for c in range(6):
                        fw.dma("pool", kt[:, c, 0:PAST], CK[l, c, :, :], writes=[kt_r[c][0]], stream="dc")
                    for j in range(2):
                        fw.dma("pool", vt[:, j, 0:576], CV[l, j, :, 0:576], writes=[vt_r[j]], stream="dc")
                        fw.dma("pool", vt[:, j, 640:704], CV[l, j, :, 576:640], writes=[vt_r[j]], stream="dc")
                fw.op("pool", lambda e: e.memset(vt[:, :, 576:640], 1.0), writes=vt_r)
                fw.op("pool", lambda e: e.memset(vt[:, :, 704:768], 1.0), writes=vt_r)
                fw.op("pool", lambda e: e.memset(ucb[:, :, :], 0.0), writes=[rr for row in uc_r for rr in row])

                for tb in range(NB):
                    rms_norm_block(l, p, tb, 0, lambda c: hT[:, c, :], lambda c: hT_r[c], tb * 512)
                    def k_post(c, pt, pr):
                        if sample:
                            kx_, kxr_ = new_scr()
                            kxb = kx_[:, 0:256].bitcast(BF16)
                            qk_post(l, pt, pr, "kna" if c < 4 else "knb", kxb, [kxr_],
                                    rope_cols=slice(tb * 512, tb * 512 + 512))
                            o = Reg()
                            xinK_regs.append(o)
                            fw.dma("sp", xinK[l].ap()[c * 128:(c + 1) * 128, tb * 512:(tb + 1) * 512], kxb, reads=[kxr_], writes=[o], stream="dxo")
                        else:
                            qk_post(l, pt, pr, "kna" if c < 4 else "knb", kt[:, c, 0:512], [kt_r[c][0]], kout=KO[l, c, :, :])
                    prev_ = None
                    for c in range(6):
                        wt, wr = wblk(l, BLK["K"] + c)
                        pt, pr = proj_fm(wt, wr, lambda k: hT[:, k, :], hT_r)
                        if prev_ is not None:
                            k_post(*prev_)
                        prev_ = (c, pt, pr)
                    k_post(*prev_)
                    for g in range(4):
                        wt, wr = wblk(l, BLK["UC"] + g)
                        pt, pr = proj_fm(wt, wr, lambda k: hT[:, k, :], hT_r)
                        if sample:
                            fw.op("act", lambda e, pt=pt, g=g, tb=tb: e.copy(ucb[:, g, 8 + tb * 512:8 + tb * 512 + 512], pt[:]),
                                  reads=[pr], writes=[uc_r[g][tb]])
                        else:
                            for i in range(2):
                                fw.op("act", lambda e, pt=pt, g=g, i=i: e.copy(ucb[:, g, 272 * i + 8:272 * i + 264], pt[:, 256 * i:256 * i + 256]),
                                      reads=[pr], writes=[uc_r[g][0]])
                    for j in range(5):
                        wt, wr = wblk(l, BLK["V"] + j)
                        for tt in range(4):
                            tok0 = tb * 512 + tt * 128
                            vch = tok0 // 128
                            pt, pr = new_ps()

                            def mmv(e, wt=wt, pt=pt, tt=tt):
                                for k in range(8):
                                    ins = e.matmul(pt[:, 0:128], lhsT=hT[:, k, tt * 128:(tt + 1) * 128], rhs=wt[:, k, :], start=(k == 0), stop=(k == 7))
                                return ins
                            fw.op("pe", mmv, reads=[wr] + hT_r, writes=[pr])
                            if j < 4:
                                dsts = [vt[:, vch, j * 128:(j + 1) * 128]]
                                srcs = [pt[:, 0:128]]
                            else:
                                dsts = [vt[:, vch, 512:576], vt[:, vch, 640:704]]
                                srcs = [pt[:, 0:64], pt[:, 64:128]]
                            if sample:
                                vx_, vxr_ = new_scr()
                                vxb = vx_[:, 0:64].bitcast(BF16)
                                fw.op("act", lambda e: e.copy(vxb, pt[:, 0:128]), reads=[pr], writes=[vxr_])
                                o = Reg()
                                xin_regs.append(o)
                                fw.dma("sp", xinV[l].ap()[tok0:tok0 + 128, j * 128:(j + 1) * 128], vxb, reads=[vxr_], writes=[o], stream="dxo")
                            else:
                                for d_, s_ in zip(dsts, srcs):
                                    fw.op("act", lambda e, d_=d_, s_=s_: e.copy(d_, s_), reads=[pr], writes=[vt_r[vch]])
                            if not sample:
                                vs_, vsr_ = new_scr()
                                fw.op("dve", lambda e, pt=pt, vs_=vs_: e.tensor_copy(vs_[:, 0:128], pt[:, 0:128]),
                                      reads=[pr], writes=[vsr_])
                                o = Reg()
                                out_regs.append(o)
                                fw.dma("sp", VO[l, tok0:tok0 + 128, j * 128:(j + 1) * 128], vs_[:, 0:128], reads=[vsr_], writes=[o], stream="do")

                if sample:
                    for g in range(4):
                        fw.op("dve", lambda e, g=g: e.tensor_copy(uh[:, g, 0:8], ucb[:, g, 8:16]), reads=uc_r[g], writes=[r_uh])
                        fw.op("dve", lambda e, g=g: e.tensor_copy(uh[:, g, 8:16], ucb[:, g, TS:TS + 8]), reads=uc_r[g], writes=[r_uh])
                    for g in range(4):
                        o = Reg()
                        xin_regs.append(o)
                        fw.dma("sp", xinV[l].ap()[g * 128:(g + 1) * 128, 640:672], uh[:, g, :], reads=[r_uh], writes=[o], stream="dxo")
                    r_xoutK, r_xout = Reg(), Reg()
                    fw.op("pool", lambda e: e.collective_compute("AllGather", ALU.bypass, replica_groups=PAIRS,
                                                                 ins=[xinK[l].ap()], outs=[xoutK[l].ap()]),
                          reads=list(xinK_regs), writes=[r_xoutK])
                    fw.op("pool", lambda e: e.collective_compute("AllGather", ALU.bypass, replica_groups=PAIRS,
                                                                 ins=[xinV[l].ap()], outs=[xoutV[l].ap()]),
                          reads=list(xin_regs), writes=[r_xout])
                    del xin_regs[:]
                    del xinK_regs[:]
                    for r_ in range(2):
                        for c in range(6):
                            fw.dma("sp", kt[:, c, PAST + r_ * 1024:PAST + (r_ + 1) * 1024], xoutK[l].ap()[r_ * 768 + c * 128:r_ * 768 + (c + 1) * 128, :],
                                   reads=[r_xoutK], writes=[kt_r[c][1 + r_]], stream="dxi")
                        base = r_ * 1024
                        for j in range(8):
                            ch = 2 + r_ * 8 + j
                            fw.dma("sp", vt[:, ch, 0:576], xoutV[l].ap()[base + j * 128:base + (j + 1) * 128, 0:576],
                                   reads=[r_xout], writes=[vt_r[ch]], stream="dxi")
                            fw.dma("sp", vt[:, ch, 640:704], xoutV[l].ap()[base + j * 128:base + (j + 1) * 128, 576:640],
                                   reads=[r_xout], writes=[vt_r[ch]], stream="dxi")
                    for g in range(4):
                        for r_ in range(2):
                            fw.dma("sp", halo_u[:, g, r_, :], xoutV[l].ap()[r_ * 1024 + g * 128:r_ * 1024 + (g + 1) * 128, 640:672],
                                   reads=[r_xout], writes=[r_halo_u], stream="dxi")
                    for g in range(4):
                        fw.op("pool", lambda e, g=g: e.tensor_scalar(ucb[:, g, 0:8], halo_u[:, g, 0, 8:16], msk[:, 0:1], None, ALU.mult),
                              reads=[r_halo_u, r_msk], writes=uc_r[g])
                        fw.op("pool", lambda e, g=g: e.tensor_scalar(ucb[:, g, 8 + TS:16 + TS], halo_u[:, g, 1, 0:8], msk[:, 1:2], None, ALU.mult),
                              reads=[r_halo_u, r_msk], writes=uc_r[g])

                for tb in range(NB):
                    c0 = tb * 512
                    rms_norm_block(l, p, tb, 0, lambda c: hT[:, c, :], lambda c: hT_r[c], c0)
                    fw.op("dve", lambda e: e.memset(qm[64:128, :, :], 0.0), writes=qm_r)
                    prev_ = None
                    for c in range(8):
                        wt, wr = wblk(l, BLK["Q"] + c)
                        pt, pr = proj_fm(wt, wr, lambda k: hT[:, k, :], hT_r)
                        if prev_ is not None:
                            qk_post(l, prev_[1], prev_[2], "qna" if prev_[0] < 4 else "qnb", qm[:, prev_[0], :], [qm_r[prev_[0]]],
                                    rope_cols=slice(c0, c0 + 512) if sample else None, dst_hi=qo[:, prev_[0], :], dst_hi_regs=[qo_r[prev_[0]]])
                        prev_ = (c, pt, pr)
                    qk_post(l, prev_[1], prev_[2], "qna" if prev_[0] < 4 else "qnb", qm[:, prev_[0], :], [qm_r[prev_[0]]],
                            rope_cols=slice(c0, c0 + 512) if sample else None, dst_hi=qo[:, prev_[0], :], dst_hi_regs=[qo_r[prev_[0]]])
                    if not sample and l == 0:
                        tap(0, hT, hT_r)
                        tap(1, qm, qm_r)
                    if sample:
                        units = [(seqs[0], 0, 512)]
                    else:
                        units = [(seqs[0], 0, 256), (seqs[1], 256, 256)]
                    LAG = 3
                    order = [("d", 0), ("g", 0), ("g", 1), ("d", 1), ("g", 2), ("g", 3), ("d", 2), ("g", 4), ("g", 5), ("d", 3), ("g", 6), ("g", 7)]
                    for (sq_, qc0, NQ) in units:
                        nkc = sq_["nk"] // 128
                        qs = slice(qc0, qc0 + NQ)
                        steps = []
                        for (kind, h) in order:
                            for kc in range(nkc):
                                for a_ in range(2 if kind == "d" else 1):
                                    steps.append((kind, h, kc, a_))
                        pend = []

                        def rec_qk(kind, h, kc, a_):
                            ksl = slice(sq_["kcol0"] + kc * 128, sq_["kcol0"] + (kc + 1) * 128)
                            if kind == "d":
                                hp = (h % 2) * 64
                                kch = a_ * 2 + h // 2
                                qch = kch
                            else:
                                hp = (h % 2) * 64
                                kch = 4 + h // 4
                                qch = 4 + h // 2
                            sc_t, sc_r = new_ps((0, 1, 2))
                            qsrc, qsrc_r = (qm, qm_r) if hp == 0 else (qo, qo_r)
                            fw.op("pe", lambda e: e.matmul(sc_t[:, 0:NQ], lhsT=kt[:, kch, ksl], rhs=qsrc[:, qch, qs], start=True, stop=True),
                                  reads=[kreg(kch, kc * 128), qsrc_r[qch]], writes=[sc_r])
                            pt_, ptr_ = new_scr()
                            pT = pt_[:, 0:256].bitcast(BF16)
                            fw.op("act", lambda e: e.activation(pT[:, 0:NQ], sc_t[:, 0:NQ], AF.Exp, scale=0.125), reads=[sc_r], writes=[ptr_])
                            return pT, ptr_

                        def rec_pv(kind, h, kc, a_, pT, ptr_):
                            vch = sq_["vch0"] + kc
                            first, last = (kc == 0), (kc == nkc - 1)
                            if kind == "d":
                                bset = h % 2
                                o_t, o_r = ps[4 + 2 * bset + a_], ps_r[4 + 2 * bset + a_]
                                fw.op("pe", lambda e: e.matmul(o_t[:, 0:NQ], lhsT=vt[:, vch, h * 128:(h + 1) * 128], rhs=pT[:, 0:NQ], start=first, stop=last),
                                      reads=[ptr_, vt_r[vch]], writes=[o_r])
                                pa, par = pacc[bset][a_], pacc_r[bset][a_]
                                eng_ = "dve"
                                if first:
                                    fw.op(eng_, lambda e: e.tensor_copy(pa[:, 0:NQ], pT[:, 0:NQ]), reads=[ptr_], writes=[par])
                                else:
                                    fw.op(eng_, lambda e: e.tensor_tensor(pa[:, 0:NQ], pa[:, 0:NQ], pT[:, 0:NQ], ALU.add), reads=[ptr_, par], writes=[par])
                                if last and a_ == 1:
                                    diff_epilogue(h)
                            else:
                                g = h // 4
                                o_t, o_r = ps[3], ps_r[3]
                                fw.op("pe", lambda e: e.matmul(o_t[:, 0:NQ], lhsT=vt[:, vch, 512 + g * 128:640 + g * 128], rhs=pT[:, 0:NQ], start=first, stop=last),
                                      reads=[ptr_, vt_r[vch]], writes=[o_r])
                                if last:
                                    jp = (h % 2) * 64
                                    qc = 4 + h // 2
                                    rc, rcr = new_scr()
                                    fw.op("act", lambda e: e.activation(rc[64:128, 0:NQ], o_t[64:128, 0:NQ], AF.Ln), reads=[o_r], writes=[rcr])
                                    fw.op("act", lambda e: e.activation(rc[64:128, 0:NQ], rc[64:128, 0:NQ], AF.Exp, scale=-1.0), reads=[rcr], writes=[rcr])
                                    fw.op("dve", lambda e: e.tensor_tensor(ao[jp:jp + 64, qc, qs], o_t[0:64, 0:NQ], rc[64:128, 0:NQ], ALU.mult),
                                          reads=[o_r, rcr], writes=[ao_r[qc]])

                        def diff_epilogue(h):
                            bset = h % 2
                            (o1, o1r), (o2, o2r) = [(ps[i], ps_r[i]) for i in (4 + 2 * bset, 5 + 2 * bset)]
                            r1, r1r = new_scr()
                            r2, r2r = new_scr()
                            for a_, (rx, rxr) in enumerate(((r1, r1r), (r2, r2r))):
                                d_t, d_r = new_ps((0, 1, 2))
                                fw.op("pe", lambda e: e.matmul(d_t[:, 0:NQ], lhsT=ones_f[:], rhs=pacc[bset][a_][:, 0:NQ], start=True, stop=True),
                                      reads=[pacc_r[bset][a_], r_onesf], writes=[d_r])
                                fw.op("act", lambda e: e.activation(rx[:, 0:NQ], d_t[:, 0:NQ], AF.Ln), reads=[d_r], writes=[rxr])
                                fw.op("act", lambda e: e.activation(rx[:, 0:NQ], rx[:, 0:NQ], AF.Exp, scale=-1.0), reads=[rxr], writes=[rxr])
                            fw.op("dve", lambda e: e.tensor_tensor(r1[:, 0:NQ], o1[:, 0:NQ], r1[:, 0:NQ], ALU.mult), reads=[o1r, r1r], writes=[r1r])
                            fw.op("dve", lambda e: e.tensor_tensor(r2[:, 0:NQ], o2[:, 0:NQ], r2[:, 0:NQ], ALU.mult), reads=[o2r, r2r], writes=[r2r])
                            fw.op("dve", lambda e: e.scalar_tensor_tensor(r1[:, 0:NQ], r2[:, 0:NQ], lamv[:, l, 0:1], r1[:, 0:NQ], ALU.mult, ALU.add),
                                  reads=[r2r, r_lam], writes=[r1r])
                            s_t, s_r = new_scr()
                            sqb = s_t[:, 0:256].bitcast(BF16)
                            fw.op("act", lambda e: e.activation(sqb[:, 0:NQ], r1[:, 0:NQ], AF.Square), reads=[r1r], writes=[s_r])
                            pss, pssr = new_ps((0, 1, 2))
                            fw.op("pe", lambda e: e.matmul(pss[:, 0:NQ], lhsT=ones_bf, rhs=sqb[:, 0:NQ], start=True, stop=True),
                                  reads=[s_r, r_cst], writes=[pssr])
                            fw.op("act", lambda e: e.activation(r2[:, 0:NQ], pss[:, 0:NQ], AF.Ln, bias=epsc[:, 0:1], scale=1.0 / 128),
                                  reads=[pssr, r_eps], writes=[r2r])
                            fw.op("act", lambda e: e.activation(r2[:, 0:NQ], r2[:, 0:NQ], AF.Exp, scale=-0.5), reads=[r2r], writes=[r2r])
                            fw.op("dve", lambda e: e.scalar_tensor_tensor(ao[:, h, qs], r1[:, 0:NQ], lamv[:, l, 1:2], r2[:, 0:NQ], ALU.mult, ALU.mult),
                                  reads=[r1r, r2r, r_lam], writes=[ao_r[h]])

                        for i in range(len(steps) + LAG):
                            if i < len(steps):
                                pend.append(rec_qk(*steps[i]))
                            if i >= LAG:
                                pT, ptr_ = pend.pop(0)
                                rec_pv(*steps[i - LAG], pT, ptr_)
                            if (not sample) and l == 0:
                                mod_step((0, 1, 2))
                    PO = 64 if sample else 0
                    for g in range(4):
                        w = 2 << g
                        if sample:
                            segs = [(seqs[0]["ucol0"] + c0, 512, 0, c0 == 0, c0 + 512 == TS)]
                        else:
                            segs = [(272 * i, 256, 256 * i, True, True) for i in range(2)]
                        dbf, dr_ = dbuf[:, :], r_dbuf
                        for (uc0, nt, oc0, is_s, is_e) in segs:
                            cur, cur_r = new_scr()
                            fw.op("pool", lambda e, cur=cur, uc0=uc0, nt=nt, g=g: e.tensor_copy(cur[:, 0:nt + 16], ucb[:, g, uc0:uc0 + nt + 16]),
                                  reads=uc_r[g], writes=[cur_r])
                            u_t, u_r = cur, cur_r
                            lo, hi = 0, nt + 16
                            step = 1
                            first = True
                            ww = 2
                            while ww <= w:
                                nxt, nxt_r = new_scr()
                                if first:
                                    fw.op("pool", lambda e, nxt=nxt, cur=cur, hi=hi: e.tensor_tensor(nxt[:, 1:hi], cur[:, 0:hi - 1], cur[:, 1:hi], ALU.add),
                                          reads=[cur_r], writes=[nxt_r])
                                    lo, hi = 1, hi
                                    first = False
                                else:
                                    sft = ww // 4
                                    fw.op("pool", lambda e, nxt=nxt, cur=cur, lo=lo, hi=hi, sft=sft: e.tensor_tensor(
                                        nxt[:, lo + sft:hi - sft], cur[:, lo:hi - 2 * sft], cur[:, lo + 2 * sft:hi], ALU.add),
                                        reads=[cur_r], writes=[nxt_r])
                                    lo, hi = lo + sft, hi - sft
                                cur, cur_r = nxt, nxt_r
                                ww *= 2
                            fw.op("dve", lambda e, cur=cur, u_t=u_t, nt=nt, oc0=oc0, w=w: e.scalar_tensor_tensor(
                                dbf[:, oc0:oc0 + nt], cur[:, 8:8 + nt], 1.0 / w, u_t[:, 8:8 + nt], ALU.mult, ALU.subtract),
                                reads=[cur_r, u_r], writes=[dr_])
                            hw_ = w // 2
                            if is_s:
                                fw.op("pool", lambda e, cur=cur, hw_=hw_, g=g: e.tensor_tensor(cur[:, 8:8 + hw_], cur[:, 8:8 + hw_], pcnt[:, PO + g * 16:PO + g * 16 + hw_], ALU.mult),
                                      reads=[cur_r, r_pcnt], writes=[cur_r])
                                fw.op("pool", lambda e, cur=cur, u_t=u_t, hw_=hw_, oc0=oc0: e.tensor_tensor(dbf[:, oc0:oc0 + hw_], cur[:, 8:8 + hw_], u_t[:, 8:8 + hw_], ALU.subtract),
                                      reads=[cur_r, u_r], writes=[dr_])
                            if is_e and hw_ > 1:
                                ne = hw_ - 1
                                a0 = 8 + nt - ne
                                fw.op("pool", lambda e, cur=cur, ne=ne, a0=a0, g=g: e.tensor_tensor(cur[:, a0:a0 + ne], cur[:, a0:a0 + ne], pcnt[:, PO + g * 16 + 8:PO + g * 16 + 8 + ne], ALU.mult),
                                      reads=[cur_r, r_pcnt], writes=[cur_r])
                                fw.op("pool", lambda e, cur=cur, u_t=u_t, ne=ne, a0=a0, oc0=oc0, nt=nt: e.tensor_tensor(
                                    dbf[:, oc0 + nt - ne:oc0 + nt], cur[:, a0:a0 + ne], u_t[:, a0:a0 + ne], ALU.subtract),
                                    reads=[cur_r, u_r], writes=[dr_])
                        wt, wr = wblk(l, BLK["POOLW"] + g, kc=1)
                        pt, pr = proj_fm(wt, wr, lambda k, dbf=dbf: dbf[:, 0:512], [dr_], kc=1)
                        fw.op("act", lambda e, pt=pt, g=g: e.activation(ocb[:, g, :], pt[:], AF.Copy, scale=vcol(l, "pscale", g)),
                              reads=[pr, r_vec], writes=[oc_r[g]])
                    if not sample and l == 0:
                        tap(2, ao, ao_r)
                        tap(3, ocb, oc_r, nchunk=4)
                    for m in range(8):
                        gts = []
                        for br in range(3):
                            wt, wr = wblk(l, BLK["GATE"] + br * 8 + m)
                            pt, pr = proj_fm(wt, wr, lambda k: hT[:, k, :], hT_r)
                            gt, gr = new_scr()
                            fw.op("act", lambda e, gt=gt, pt=pt, br=br, m=m: e.activation(gt[:, 0:512], pt[:], AF.Sigmoid, bias=vcol(l, "bgate", br * 8 + m)),
                                  reads=[pr, r_vec], writes=[gr])
                            srcT, srcR = [(ao, ao_r), (ao, ao_r), (ocb, oc_r)][br]
                            off = [0, 4, 0][br]
                            wt2, wr2 = wblk(l, BLK[("BRA", "BRB", "BRC")[br]] + m, kc=4)
                            pt2, pr2 = proj_fm(wt2, wr2, lambda k, srcT=srcT, off=off: srcT[:, off + k, :], srcR[off:off + 4], kc=4)
                            fw.op("dve", lambda e, gt=gt, pt2=pt2: e.tensor_tensor(gt[:, 0:512], gt[:, 0:512], pt2[:], ALU.mult), reads=[pr2, gr], writes=[gr])
                            gts.append((gt, gr))
                        (g0, g0r), (g1, g1r), (g2, g2r) = gts
                        fw.op("pool", lambda e, g0=g0, g1=g1: e.tensor_tensor(g0[:, 0:512], g0[:, 0:512], g1[:, 0:512], ALU.add), reads=[g1r, g0r], writes=[g0r])
                        fw.op("pool", lambda e, g0=g0, g2=g2, m=m: e.tensor_tensor(qm[:, m, :], g0[:, 0:512], g2[:, 0:512], ALU.add), reads=[g0r, g2r], writes=[qm_r[m]])
                    if not sample and l == 0:
                        tap(4, qm, qm_r)
                    for m in range(8):
                        wt, wr = wblk(l, BLK["OUT"] + m)
                        pt, pr = proj_fm(wt, wr, lambda k: qm[:, k, :], qm_r)
                        fw.op("dve", lambda e, pt=pt, m=m: e.scalar_tensor_tensor(x[:, m, c0:c0 + 512], pt[:], dv[:, l, p, 2, m:m + 1], x[:, m, c0:c0 + 512], ALU.mult, ALU.add),
                              reads=[pr, r_dv, x_r[m][tb]], writes=[x_r[m][tb]])

                if not sample and l == 0:
                    tap(5, x, [x_r[c][0] for c in range(8)], is_bf=False)
                alias_src = [rr for row in kt_r for rr in row] + [rr for row in uc_r for rr in row] + hT_r + qm_r + ao_r + oc_r
                for rr in [q for row in hid_r for q in row] + [q for row in h2_r for q in row] + [q for row in stg_r for q in row]:
                    for s_ in alias_src:
                        rr.inherit(s_)
                for tb in range(NB):
                    rms_norm_block(l, p, tb, 1, lambda c, tb=tb: h2T[:, c, tb * 512:(tb + 1) * 512], lambda c, tb=tb: h2_r[c][tb], tb * 512)
                fw.op("pool", lambda e: e.memset(stg[:, :, :], 0.0), writes=[q for row in stg_r for q in row] + stg_h)
                if sample:
                    fw.op("dve", lambda e: e.tensor_copy(h2s[:, 0:8], h2T[:, :, 0]), reads=[h2_r[c][0] for c in range(8)], writes=[r_h2s])
                    fw.op("dve", lambda e: e.tensor_copy(h2s[:, 8:16], h2T[:, :, TS - 1]), reads=[h2_r[c][NB - 1] for c in range(8)], writes=[r_h2s])
                    r_x2i, r_x2o = Reg(), Reg()
                    fw.dma("sp", xin2[l].ap()[:, :], h2s[:], reads=[r_h2s], writes=[r_x2i], stream="dxo")
                    fw.op("pool", lambda e: e.collective_compute("AllGather", ALU.bypass, replica_groups=PAIRS,
                                                                 ins=[xin2[l].ap()], outs=[xout2[l].ap()]),
                          reads=[r_x2i], writes=[r_x2o])
                    for r_ in range(2):
                        fw.dma("sp", h2g[:, r_, :], xout2[l].ap()[r_ * 128:(r_ + 1) * 128, :], reads=[r_x2o], writes=[r_h2g], stream="dxi")
                    fw.op("pool", lambda e: e.tensor_scalar(h2h[:, :, 0], h2g[:, 0, 8:16], msk[:, 0:1], None, ALU.mult), reads=[r_h2g, r_msk], writes=[r_h2h])
                    fw.op("pool", lambda e: e.tensor_scalar(h2h[:, :, 1], h2g[:, 1, 0:8], msk[:, 1:2], None, ALU.mult), reads=[r_h2g, r_msk], writes=[r_h2h])
                hbufs, hregs = [hid, hid2], [hid_r, hid2_r]

                def up_round(hh):
                    n_h = 8 if hh < 2 else 6
                    hb, hr = hbufs[hh % 2], hregs[hh % 2]
                    for jj in range(n_h):
                        j = hh * 8 + jj
                        accs = {}
                        for half in range(2):
                            ch = half * 22 + j
                            wt, wr = wblk(l, BLK["UP"] + ch)
                            if sample:
                                pth, prh = proj_fm(wt, wr, lambda k: h2h[:, k, :], [r_h2h], n=2)
                                fw.op("act", lambda e, pth=pth, half=half: e.copy(stg[:, half, 0:1], pth[:, 0:1]), reads=[prh], writes=[stg_h[half]])
                                fw.op("act", lambda e, pth=pth, half=half: e.copy(stg[:, half, TS + 1:TS + 2], pth[:, 1:2]), reads=[prh], writes=[stg_h[half]])
                            for tb in range(NB):
                                pt, pr = proj_fm(wt, wr, lambda k, tb=tb: h2T[:, k, tb * 512:(tb + 1) * 512], [h2_r[k][tb] for k in range(8)])
                                if sample:
                                    fw.op("act", lambda e, pt=pt, half=half, tb=tb: e.copy(stg[:, half, 1 + tb * 512:1 + tb * 512 + 512], pt[:]),
                                          reads=[pr], writes=[stg_r[half][tb]])
                                else:
                                    for i in range(2):
                                        fw.op("act", lambda e, pt=pt, half=half, i=i: e.copy(stg[:, half, 258 * i + 1:258 * i + 257], pt[:, 256 * i:256 * i + 256]),
                                              reads=[pr], writes=[stg_r[half][0]])
                        for tb in range(NB):
                            for half in range(2):
                                ch = half * 22 + j
                                acc, accr = new_scr()
                                eng = "dve"
                                w0, w1, w2 = (vcol(l, "convw", t_ * 44 + ch) for t_ in range(3))
                                bcol = vcol(l, "convb", ch)
                                if sample:
                                    pieces = [(1 + tb * 512, 512, 0)]
                                    rd = [stg_r[half][t_] for t_ in range(max(0, tb - 1), min(NB, tb + 2))] + [stg_h[half]]
                                else:
                                    pieces = [(258 * i + 1, 256, 256 * i) for i in range(2)]
                                    rd = [stg_r[half][0]]
                                for (s0, n_, o0) in pieces:
                                    fw.op("act", lambda e, acc=acc, s0=s0, n_=n_, o0=o0, half=half, w1=w1, bcol=bcol: e.activation(
                                        acc[:, o0:o0 + n_], stg[:, half, s0:s0 + n_], AF.Identity, bias=bcol, scale=w1), reads=rd + [r_vec], writes=[accr])
                                    fw.op(eng, lambda e, acc=acc, s0=s0, n_=n_, o0=o0, half=half, w0=w0: e.scalar_tensor_tensor(
                                        acc[:, o0:o0 + n_], stg[:, half, s0 - 1:s0 - 1 + n_], w0, acc[:, o0:o0 + n_], ALU.mult, ALU.add), reads=rd + [r_vec, accr], writes=[accr])
                                    fw.op(eng, lambda e, acc=acc, s0=s0, n_=n_, o0=o0, half=half, w2=w2: e.scalar_tensor_tensor(
                                        acc[:, o0:o0 + n_], stg[:, half, s0 + 1:s0 + 1 + n_], w2, acc[:, o0:o0 + n_], ALU.mult, ALU.add), reads=rd + [r_vec, accr], writes=[accr])
                                accs[half] = (acc, accr)
                            (aa, aar), (ag, agr) = accs[0], accs[1]
                            fw.op("act", lambda e, aa=aa: e.activation(aa[:, 0:512], aa[:, 0:512], AF.Silu), reads=[aar], writes=[aar])
                            fw.op("dve", lambda e, aa=aa, ag=ag, jj=jj, tb=tb: e.tensor_tensor(hb[:, jj, tb * 512:(tb + 1) * 512], aa[:, 0:512], ag[:, 0:512], ALU.mult),
                                  reads=[aar, agr], writes=[hr[jj][tb]])

                def down_round(hh):
                    n_h = 8 if hh < 2 else 6
                    hb, hr = hbufs[hh % 2], hregs[hh % 2]
                    for m in range(8):
                        wt, wr = wblk(l, BLK["DOWN"] + m * 3 + hh, kc=n_h)
                        for tb in range(NB):
                            pt, pr = new_ps()

                            def mmd(e, pt=pt, tb=tb, wt=wt, n_h=n_h):
                                for jj in range(n_h):
                                    ins = e.matmul(pt[:], lhsT=wt[:, jj, :], rhs=hb[:, jj, tb * 512:(tb + 1) * 512], start=(jj == 0), stop=(jj == n_h - 1))
                                return ins
                            fw.op("pe", mmd, reads=[wr] + [hr[jj][tb] for jj in range(n_h)], writes=[pr])
                            fw.op("dve", lambda e, pt=pt, m=m, tb=tb: e.scalar_tensor_tensor(
                                x[:, m, tb * 512:(tb + 1) * 512], pt[:], dv[:, l, p, 5, m:m + 1], x[:, m, tb * 512:(tb + 1) * 512], ALU.mult, ALU.add),
                                reads=[pr, r_dv, x_r[m][tb]], writes=[x_r[m][tb]])
                if not sample and l == 0:
                    tap(6, x, [x_r[c][0] for c in range(8)], is_bf=False)

                up_round(0)
                up_round(1)
                down_round(0)
                up_round(2)
                down_round(1)
                down_round(2)
                alias_src = [q for row in hid_r for q in row] + [q for row in h2_r for q in row] + [q for row in stg_r for q in row]
                for rr in [q for row in kt_r for q in row] + [q for row in uc_r for q in row] + hT_r + qm_r + ao_r + oc_r:
                    for s_ in alias_src:
                        rr.inherit(s_)
            for c in range(8):
                for tb in range(NB):
                    o = Reg()
                    out_regs.append(o)
                    fw.dma("sp", YOUT[c, :, tb * 512:(tb + 1) * 512], x[:, c, tb * 512:(tb + 1) * 512], reads=[x_r[c][tb]], writes=[o], stream="do")

        run_pass(1)
        run_pass(0)
        if wseq_out is not None:
            return None
        fw.final_wait("sp", out_regs)
        fw.emit(es)
    return nc


def _blk(wcols):
    K = wcols.shape[0]
    kc = K // 128
    out = np.zeros((128, 1024), np.float32)
    out[:, :kc * 128] = wcols.reshape(kc, 128, 128).transpose(1, 0, 2).reshape(128, kc * 128)
    return out


def _fm(v):
    return np.ascontiguousarray(v.reshape(-1, 128).T)


def _consts():
    cst = np.zeros((128, 384), np.float32)
    cst[:, 0:128] = 1.0
    cst[0:64, 128:192] = 1.0
    cst[64:128, 192:256] = 1.0
    for i in range(128):
        d = i % 64
        part = (d % 32) // 16
        partner = i + 16 if part == 0 else i - 16
        cst[partner, 256 + i] = 1.0
    t = np.arange(2048)
    row = (t // GRID_W).astype(np.float32)
    col = (t % GRID_W).astype(np.float32)
    inv = (10000.0 ** (-np.arange(16, dtype=np.float32) / 16)).astype(np.float32)
    rope = np.zeros((2, 128, 2048), np.float32)
    for i in range(128):
        d = i % 64
        axis, part, f = d // 32, (d % 32) // 16, d % 16
        ang = (row if axis == 0 else col) * inv[f]
        rope[0, i] = np.cos(ang)
        rope[1, i] = np.sin(ang) * (-1.0 if part == 0 else 1.0)
    pc = np.zeros((128, 64), np.float32)
    for g in range(4):
        w = 2 << g
        hw = w // 2
        for tt in range(hw):
            pc[:, g * 16 + tt] = 1.0 / (tt + hw)
        ne = hw - 1
        for i in range(ne):
            pc[:, g * 16 + 8 + i] = 1.0 / (ne - i + hw)
    return cst, rope, pc


def _pc_sample(rank):
    pc = np.zeros((128, 64), np.float32)
    for g in range(4):
        w = 2 << g
        hw = w // 2
        for tt in range(hw):
            pc[:, g * 16 + tt] = (1.0 / (tt + hw)) if rank == 0 else (1.0 / w)
        ne = hw - 1
        for i in range(ne):
            pc[:, g * 16 + 8 + i] = (1.0 / (ne - i + hw)) if rank == 1 else (1.0 / w)
    return pc


_NC_CACHE = {}


def kernel(x_prompt, x_sample, cache_diff_k, cache_diff_v, cache_gqa_k, cache_gqa_v, c, c_ctx,
           norm1_g, norm2_g, w_mod, b_mod, w_in, qn_a, kn_a, qn_b, kn_b,
           lam_q1, lam_k1, lam_q2, lam_k2, subln_g, w_pool, pool_scale,
           w_br_a, w_br_b, w_br_c, w_gate, b_gate, w_out, w_up, conv_w, conv_b, w_down):
    f = lambda a: np.asarray(a, dtype=np.float32)
    x_prompt, x_sample = f(x_prompt), f(x_sample)
    Wb = np.zeros((L, NBLK, 128, 1024), np.float32)
    vec = np.zeros((L, 128, NVEC), np.float32)
    for l in range(L):
        wi = f(w_in[l])
        for i in range(4):
            Wb[l, BLK["Q"] + i] = _blk(wi[:, i * 128:(i + 1) * 128])
            Wb[l, BLK["Q"] + 4 + i] = _blk(wi[:, 1536 + i * 128:1536 + (i + 1) * 128])
            Wb[l, BLK["K"] + i] = _blk(wi[:, 512 + i * 128:512 + (i + 1) * 128])
            Wb[l, BLK["V"] + i] = _blk(wi[:, 1024 + i * 128:1024 + (i + 1) * 128])
            Wb[l, BLK["UC"] + i] = _blk(wi[:, 2304 + i * 128:2304 + (i + 1) * 128])
        for g in range(2):
            kb = wi[:, 2048 + g * 64:2048 + (g + 1) * 64]
            Wb[l, BLK["K"] + 4 + g] = _blk(np.concatenate([kb, kb], axis=1))
        Wb[l, BLK["V"] + 4] = _blk(wi[:, 2176:2304])
        wg = f(w_gate[l])
        for j in range(24):
            Wb[l, BLK["GATE"] + j] = _blk(wg[:, j * 128:(j + 1) * 128])
        for nm, wsrc in (("BRA", w_br_a), ("BRB", w_br_b), ("BRC", w_br_c)):
            ws = f(wsrc[l])
            for m in range(8):
                Wb[l, BLK[nm] + m] = _blk(ws[:, m * 128:(m + 1) * 128])
        wo = f(w_out[l])
        for m in range(8):
            Wb[l, BLK["OUT"] + m] = _blk(wo[:, m * 128:(m + 1) * 128])
        wu = f(w_up[l])
        for j in range(44):
            Wb[l, BLK["UP"] + j] = _blk(wu[:, j * 128:(j + 1) * 128])
        wd = f(w_down[l])
        for m in range(8):
            for kg in range(3):
                rows = wd[kg * 1024:min((kg + 1) * 1024, DFF), m * 128:(m + 1) * 128]
                Wb[l, BLK["DOWN"] + m * 3 + kg] = _blk(rows)
        wp = f(w_pool[l])
        for g in range(4):
            Wb[l, BLK["POOLW"] + g] = _blk(wp[g])
        wm = f(w_mod[l])
        for j in range(48):
            Wb[l, BLK["MOD"] + j] = _blk(wm[:, j * 128:(j + 1) * 128])
        vec[l, :, VC["norm1"]:VC["norm1"] + 8] = _fm(f(norm1_g[l]))
        vec[l, :, VC["norm2"]:VC["norm2"] + 8] = _fm(f(norm2_g[l]))
        vec[l, :, VC["bmod"]:VC["bmod"] + 48] = _fm(f(b_mod[l]))
        vec[l, :, VC["bgate"]:VC["bgate"] + 24] = _fm(f(b_gate[l]))
        cw = f(conv_w[l])
        for t_ in range(3):
            vec[l, :, VC["convw"] + t_ * 44:VC["convw"] + (t_ + 1) * 44] = _fm(cw[t_])
        vec[l, :, VC["convb"]:VC["convb"] + 44] = _fm(f(conv_b[l]))
        vec[l, :, VC["pscale"]:VC["pscale"] + 4] = _fm(f(pool_scale[l]))
        for nm, src in (("qna", qn_a), ("kna", kn_a), ("qnb", qn_b), ("knb", kn_b)):
            vec[l, :, VC[nm]] = np.tile(f(src[l]), 2)
        vec[l, :, VC["subln"]] = f(subln_g[l])
        for i, src in enumerate((lam_q1, lam_k1, lam_q2, lam_k2)):
            vec[l, 0:64, VC["lam"] + i] = f(src[l])
    cst, rope_full, pc = _consts()
    cdk, cdv, cgk, cgv = f(cache_diff_k), f(cache_diff_v), f(cache_gqa_k), f(cache_gqa_v)
    in_maps = []
    for core in range(8):
        b = core // 2
        rank = core % 2
        xs = np.ascontiguousarray(x_sample[b, rank * TS:(rank + 1) * TS].T).reshape(8, 128, TS)
        rope = np.ascontiguousarray(rope_full[:, :, rank * TS:(rank + 1) * TS])
        mskv = np.zeros((128, 16), np.float32)
        mskv[:, 0] = float(rank)
        mskv[:, 1] = float(1 - rank)
        pcs = np.concatenate([pc, _pc_sample(rank)], axis=1)
        xp = np.ascontiguousarray(x_prompt[2 * core:2 * core + 2].reshape(TP, D).T).reshape(8, 128, TP)
        ct = np.concatenate([_fm(f(c)[b]), _fm(f(c_ctx))], axis=1)
        ck = np.zeros((L, 6, 128, PAST), np.float32)
        cv = np.zeros((L, 2, 128, 640), np.float32)
        for l in range(L):
            ka = cdk[b, l].reshape(PAST, 512).T
            ck[l, 0:4] = ka.reshape(4, 128, PAST)
            kb = cgk[b, l].reshape(PAST, 128).T
            for g in range(2):
                ck[l, 4 + g] = np.concatenate([kb[g * 64:(g + 1) * 64], kb[g * 64:(g + 1) * 64]], axis=0)
            va = cdv[b, l].reshape(PAST, 512)
            vb = cgv[b, l].reshape(PAST, 128)
            cv[l] = np.concatenate([va, vb], axis=1).reshape(2, 128, 640)
        in_maps.append(dict(xs=xs, xp=xp, w=Wb, vec=vec, ct=np.ascontiguousarray(ct), cst=cst, rope=rope, pcnt=pcs, msk=mskv, ck=ck, cv=cv))
    if "nc" not in _NC_CACHE:
        seq = []
        build_program(wseq=None, wseq_out=seq)
        _NC_CACHE["nc"] = build_program(wseq=seq)
    nc = _NC_CACHE["nc"]
    res = run_bass_kernel_spmd(nc, in_maps, core_ids=list(range(8)))
    R_ = res.results
    if DEBUG:
        DBG_OUT["dbg"] = [r["dbg"] for r in R_]
    y_p = np.zeros((16, 256, D), np.float32)
    y_s = np.zeros((4, 2048, D), np.float32)
    ndk = np.zeros((16, L, 256, 2, 4, 64), np.float32)
    ndv = np.zeros((16, L, 256, 4, 128), np.float32)
    ngk = np.zeros((16, L, 256, 2, 64), np.float32)
    ngv = np.zeros((16, L, 256, 2, 64), np.float32)
    for core in range(8):
        r = R_[core]
        yp = r["yp"].reshape(D, TP).T.reshape(2, 256, D)
        y_p[2 * core:2 * core + 2] = yp
        ys = r["ys"].reshape(D, TS).T
        b, hf = core // 2, core % 2
        y_s[b, hf * TS:(hf + 1) * TS] = ys
        ko, vo = r["ko"], r["vo"]
        for l in range(L):
            ka = ko[l, 0:4].reshape(512, TP).T.reshape(2, 256, 2, 4, 64)
            ndk[2 * core:2 * core + 2, l] = ka
            kb = np.stack([ko[l, 4, 0:64, :], ko[l, 5, 0:64, :]], axis=0)
            ngk[2 * core:2 * core + 2, l] = kb.transpose(2, 0, 1).reshape(2, 256, 2, 64)
            v = vo[l].reshape(2, 256, 640)
            ndv[2 * core:2 * core + 2, l] = v[:, :, 0:512].reshape(2, 256, 4, 128)
            ngv[2 * core:2 * core + 2, l] = v[:, :, 512:640].reshape(2, 256, 2, 64)
    return (y_p, y_s, ndk, ndv, ngk, ngv)
```
